# Optimizing a Trainium2 kernel written in Bass

```python
import math
import jax, jax.numpy as jnp
from jax import lax
import numpy as np

D_MODEL = 2048
BATCH = 8
SEQ = 4096
DEPTH = 4
DEC_BATCH = 8
DEC_SEQ = 32
PAST_LEN = 4096

CHUNK = 64
N_EVEN = (DEPTH + 1) // 2
N_ODD = DEPTH // 2
D_A = D_MODEL // 2
CONV_A = 31
D_B = D_MODEL // 2
G_B = 8
DH_B = D_B // G_B
GMLP_CHUNK = 128
D_C = D_MODEL // 2
CONV_C = 3
N_HEADS = 16
Q_RANK = 512
KV_RANK = 512
NOPE_DIM = 128
ROPE_DIM = 64
QK_DIM = NOPE_DIM + ROPE_DIM
V_DIM = 128
ROPE_THETA = 10000.0
D_FF = 4 * D_MODEL
Q_BLOCK = 128
EPS = 1e-6
EVEN_IN = 2 * D_A + 2 * D_B
EVEN_OUT = D_A + D_B
ODD_IN = 3 * D_C + Q_RANK + KV_RANK + ROPE_DIM
ODD_OUT = D_C + N_HEADS * V_DIM

kernel_name = "hybrid_streaming_encoder_step"


def rms_norm(x, g):
    xf = x.astype(jnp.float32)
    y = xf * lax.rsqrt(jnp.mean(xf * xf, axis=-1, keepdims=True) + EPS)
    return y.astype(x.dtype) * g


def layer_norm(x, g, b):
    xf = x.astype(jnp.float32)
    xc = xf - jnp.mean(xf, axis=-1, keepdims=True)
    y = xc * lax.rsqrt(jnp.mean(xc * xc, axis=-1, keepdims=True) + EPS)
    return y.astype(x.dtype) * g + b


def rope(x, pos):
    half = x.shape[-1] // 2
    inv = ROPE_THETA ** (-jnp.arange(half, dtype=jnp.float32) / half)
    ang = pos.astype(jnp.float32)[:, None] * inv[None, :]
    cos = jnp.cos(ang)[None, :, None, :].astype(x.dtype)
    sin = jnp.sin(ang)[None, :, None, :].astype(x.dtype)
    x1, x2 = x[..., :half], x[..., half:]
    return jnp.concatenate([x1 * cos - x2 * sin, x2 * cos + x1 * sin], axis=-1)


def depthwise_causal_conv(buf, w):
    return lax.conv_general_dilated(
        buf, w[:, None, :], window_strides=(1,), padding="VALID",
        dimension_numbers=("NWC", "WIO", "NWC"), feature_group_count=w.shape[1])


def spatial_gate(u, v, w_s, b_s):
    bn, s, _ = v.shape
    L = min(s, GMLP_CHUNK)
    n = s // L
    vg = v.reshape(bn, n, L, G_B, DH_B)
    w = w_s[:, :L, :L] * jnp.tril(jnp.ones((L, L), w_s.dtype))[None]
    bias = jnp.transpose(b_s[:, :L])[None, None, :, :, None]
    mixed = jnp.einsum("gij,bnjgd->bnigd", w, vg) + bias
    return u * mixed.reshape(bn, s, D_B)


def chunk_causal_attention(q, k, v, q_pos, k_pos):
    bn, sq, h, dk = q.shape
    qb = min(Q_BLOCK, sq)
    nb = sq // qb
    scale = 1.0 / math.sqrt(dk)
    q_blocks = jnp.transpose(q.reshape(bn, nb, qb, h, dk), (1, 0, 2, 3, 4))
    p_blocks = q_pos.reshape(nb, qb)
    k_chunk = k_pos // CHUNK
    neg = jnp.finfo(jnp.float32).min

    def one_block(args):
        qi, pi = args
        s = jnp.einsum("bqhd,bkhd->bhqk", qi, k, preferred_element_type=jnp.float32) * scale
        mask = k_chunk[None, :] <= (pi // CHUNK)[:, None]
        p = jax.nn.softmax(jnp.where(mask[None, None], s, neg), axis=-1)
        return jnp.einsum("bhqk,bkhd->bqhd", p.astype(v.dtype), v)

    out = lax.map(one_block, (q_blocks, p_blocks))
    return jnp.transpose(out, (1, 0, 2, 3, 4)).reshape(bn, sq, h, v.shape[-1])


def even_mixer(h, conv_prev, w_in, conv_w, conv_b, ln_a_g, ln_a_b, ln_v_g, ln_v_b, w_s, b_s, w_out):
    z = h @ w_in
    a_val, a_gate, b_u, b_v = jnp.split(z, [D_A, 2 * D_A, 2 * D_A + D_B], axis=-1)
    a = a_val * jax.nn.sigmoid(a_gate)
    buf = jnp.concatenate([conv_prev, a], axis=1)
    conv = depthwise_causal_conv(buf, conv_w) + conv_b
    a_out = jax.nn.silu(layer_norm(conv, ln_a_g, ln_a_b))
    new_conv = buf[:, -(CONV_A - 1):]
    u = jax.nn.gelu(b_u, approximate=False)
    v = layer_norm(jax.nn.gelu(b_v, approximate=False), ln_v_g, ln_v_b)
    b_out = spatial_gate(u, v, w_s, b_s)
    out = jnp.concatenate([a_out, b_out], axis=-1) @ w_out
    return out, new_conv, v


def odd_mixer(h, conv_prev, lat_prev, kr_prev, w_in, conv_w, q_norm_g, w_uq, kv_norm_g, w_ukv, w_out):
    bn, s, _ = h.shape
    past = lat_prev.shape[1]
    z = h @ w_in
    g_b, g_c, x_c, z_q, z_kv, z_kr = jnp.split(
        z, [D_C, 2 * D_C, 3 * D_C, 3 * D_C + Q_RANK, 3 * D_C + Q_RANK + KV_RANK], axis=-1)
    buf = jnp.concatenate([conv_prev, g_c * x_c], axis=1)
    conv = buf[:, 0:s] * conv_w[0]
    for j in range(1, CONV_C):
        conv = conv + buf[:, j:j + s] * conv_w[j]
    c_out = g_b * conv
    new_conv = buf[:, -(CONV_C - 1):]

    q_pos = past + jnp.arange(s, dtype=jnp.int32)
    k_pos = jnp.arange(past + s, dtype=jnp.int32)
    q = (rms_norm(z_q, q_norm_g) @ w_uq).reshape(bn, s, N_HEADS, QK_DIM)
    q = jnp.concatenate([q[..., :NOPE_DIM], rope(q[..., NOPE_DIM:], q_pos)], axis=-1)
    c_kv = rms_norm(z_kv, kv_norm_g)
    k_r = rope(z_kr[:, :, None, :], q_pos)[:, :, 0, :]
    c_all = jnp.concatenate([lat_prev, c_kv], axis=1)
    kr_all = jnp.concatenate([kr_prev, k_r], axis=1)
    t = past + s
    kv = (c_all @ w_ukv).reshape(bn, t, N_HEADS, NOPE_DIM + V_DIM)
    k = jnp.concatenate(
        [kv[..., :NOPE_DIM], jnp.broadcast_to(kr_all[:, :, None, :], (bn, t, N_HEADS, ROPE_DIM))], axis=-1)
    v = kv[..., NOPE_DIM:]
    attn = chunk_causal_attention(q, k, v, q_pos, k_pos).reshape(bn, s, N_HEADS * V_DIM)
    out = jnp.concatenate([c_out, attn], axis=-1) @ w_out
    return out, new_conv, c_kv, k_r


def sq_relu_mlp(h, w_up, w_down):
    return jnp.square(jax.nn.relu(h @ w_up)) @ w_down


def setup_inputs(seed: int = 0) -> dict:
    key = jax.random.key(seed)
    ks = iter(jax.random.split(key, 40))

    def nrm(shape, scale):
        return jax.random.normal(next(ks), shape, jnp.float32) * scale

    def gain(shape):
        return 1.0 + nrm(shape, 0.01)

    return {
        "x_prompt": nrm((BATCH, SEQ, D_MODEL), 1.0),
        "x_sample": nrm((DEC_BATCH, DEC_SEQ, D_MODEL), 1.0),
        "state_conv_a": nrm((N_EVEN, DEC_BATCH, CONV_A - 1, D_A), 0.5),
        "state_conv_c": nrm((N_ODD, DEC_BATCH, CONV_C - 1, D_C), 1.0),
        "cache_kv_latent": nrm((N_ODD, DEC_BATCH, PAST_LEN, KV_RANK), 1.0),
        "cache_k_rope": nrm((N_ODD, DEC_BATCH, PAST_LEN, ROPE_DIM), 1.0),
        "norm_mix": gain((DEPTH, D_MODEL)),
        "norm_ffn": gain((DEPTH, D_MODEL)),
        "norm_final": gain((D_MODEL,)),
        "w_in_even": nrm((N_EVEN, D_MODEL, EVEN_IN), D_MODEL ** -0.5),
        "conv_a_w": nrm((N_EVEN, CONV_A, D_A), CONV_A ** -0.5),
        "conv_a_b": nrm((N_EVEN, D_A), 0.02),
        "ln_a_g": gain((N_EVEN, D_A)),
        "ln_a_b": nrm((N_EVEN, D_A), 0.02),
        "ln_v_g": gain((N_EVEN, D_B)),
        "ln_v_b": nrm((N_EVEN, D_B), 0.02),
        "w_spatial": nrm((N_EVEN, G_B, GMLP_CHUNK, GMLP_CHUNK), GMLP_CHUNK ** -0.5),
        "b_spatial": gain((N_EVEN, G_B, GMLP_CHUNK)),
        "w_out_even": nrm((N_EVEN, EVEN_OUT, D_MODEL), EVEN_OUT ** -0.5),
        "w_in_odd": nrm((N_ODD, D_MODEL, ODD_IN), D_MODEL ** -0.5),
        "conv_c_w": nrm((N_ODD, CONV_C, D_C), CONV_C ** -0.5),
        "q_norm_g": gain((N_ODD, Q_RANK)),
        "w_uq": nrm((N_ODD, Q_RANK, N_HEADS * QK_DIM), Q_RANK ** -0.5),
        "kv_norm_g": gain((N_ODD, KV_RANK)),
        "w_ukv": nrm((N_ODD, KV_RANK, N_HEADS * (NOPE_DIM + V_DIM)), KV_RANK ** -0.5),
        "w_out_odd": nrm((N_ODD, ODD_OUT, D_MODEL), ODD_OUT ** -0.5),
        "w_ffn_up": nrm((DEPTH, D_MODEL, D_FF), D_MODEL ** -0.5),
        "w_ffn_down": nrm((DEPTH, D_FF, D_MODEL), D_FF ** -0.5),
    }


def reference(x_prompt, x_sample, state_conv_a, state_conv_c, cache_kv_latent, cache_k_rope,
              norm_mix, norm_ffn, norm_final,
              w_in_even, conv_a_w, conv_a_b, ln_a_g, ln_a_b, ln_v_g, ln_v_b, w_spatial, b_spatial, w_out_even,
              w_in_odd, conv_c_w, q_norm_g, w_uq, kv_norm_g, w_ukv, w_out_odd,
              w_ffn_up, w_ffn_down):
    hp, hs = x_prompt, x_sample
    bp = hp.shape[0]
    ca_p, ca_s, gv_s, cc_p, cc_s, lat_p, kr_p, lat_s, kr_s = [], [], [], [], [], [], [], [], []
    for l in range(DEPTH):
        p = l // 2
        np_ = rms_norm(hp, norm_mix[l])
        ns_ = rms_norm(hs, norm_mix[l])
        if l % 2 == 0:
            ew = (w_in_even[p], conv_a_w[p], conv_a_b[p], ln_a_g[p], ln_a_b[p],
                  ln_v_g[p], ln_v_b[p], w_spatial[p], b_spatial[p], w_out_even[p])
            zeros_a = jnp.zeros((bp, CONV_A - 1, D_A), hp.dtype)
            op, buf_p, _ = even_mixer(np_, zeros_a, *ew)
            os_, buf_s, v_s = even_mixer(ns_, state_conv_a[p], *ew)
            ca_p.append(buf_p)
            ca_s.append(buf_s)
            gv_s.append(v_s)
        else:
            ow = (w_in_odd[p], conv_c_w[p], q_norm_g[p], w_uq[p], kv_norm_g[p], w_ukv[p], w_out_odd[p])
            zeros_c = jnp.zeros((bp, CONV_C - 1, D_C), hp.dtype)
            no_lat = jnp.zeros((bp, 0, KV_RANK), hp.dtype)
            no_kr = jnp.zeros((bp, 0, ROPE_DIM), hp.dtype)
            op, bufc_p, c_p, r_p = odd_mixer(np_, zeros_c, no_lat, no_kr, *ow)
            os_, bufc_s, c_s, r_s = odd_mixer(ns_, state_conv_c[p], cache_kv_latent[p], cache_k_rope[p], *ow)
            cc_p.append(bufc_p)
            cc_s.append(bufc_s)
            lat_p.append(c_p)
            kr_p.append(r_p)
            lat_s.append(c_s)
            kr_s.append(r_s)
        hp = hp + op
        hs = hs + os_
        hp = hp + sq_relu_mlp(rms_norm(hp, norm_ffn[l]), w_ffn_up[l], w_ffn_down[l])
        hs = hs + sq_relu_mlp(rms_norm(hs, norm_ffn[l]), w_ffn_up[l], w_ffn_down[l])
    y_prompt = rms_norm(hp, norm_final)
    y_sample = rms_norm(hs, norm_final)
    conv_a_prompt = jnp.stack(ca_p)
    conv_a_sample = jnp.stack(ca_s)
    gmlp_v_sample = jnp.stack(gv_s)
    conv_c_prompt = jnp.stack(cc_p)
    conv_c_sample = jnp.stack(cc_s)
    kv_latent_prompt = jnp.stack(lat_p)
    k_rope_prompt = jnp.stack(kr_p)
    kv_latent_sample = jnp.stack(lat_s)
    k_rope_sample = jnp.stack(kr_s)
    return (y_prompt, y_sample, conv_a_prompt, conv_a_sample, gmlp_v_sample, conv_c_prompt, conv_c_sample,
            kv_latent_prompt, k_rope_prompt, kv_latent_sample, k_rope_sample)
```

```python
import math
from contextlib import ExitStack

import numpy as np
import concourse.bass as bass
import concourse.mybir as mybir
from concourse.bass_utils import run_bass_kernel_spmd

F32 = mybir.dt.float32
BF16 = mybir.dt.bfloat16
I32 = mybir.dt.int32
AF = mybir.ActivationFunctionType
ALU = mybir.AluOpType

D = 2048
DA = 1024
DFF = 8192
NH = 16
QR = 512
KVR = 512
ROPE = 64
EPS = 1e-6
CONVA = 31
TT = 256
NSUB = TT // 128
QSCALE = 1.0 / math.sqrt(192.0)
TWO_PI = 2.0 * math.pi


class Cfg:
    def __init__(self, SEQ=4096, PAST=4096, DEPTH=4, DEC=32, NCORES=8):
        self.SEQ, self.PAST, self.DEPTH, self.DEC, self.NCORES = SEQ, PAST, DEPTH, DEC, NCORES
        self.NE = (DEPTH + 1) // 2
        self.NO = DEPTH // 2


class _Op:
    __slots__ = ("eng", "fn", "deps", "key", "ndma", "sig", "ordv", "idx")


class Prog:
    ENGS = ("pe", "act", "dve", "pool", "sp")

    def __init__(self):
        self.ops = []
        self.res = {}
        self.key_count = {}
        self.key_last = {}

    def add(self, eng, fn, r=(), w=(), key=None, ndma=1):
        op = _Op()
        op.eng, op.fn, op.key, op.ndma = eng, fn, key, ndma
        op.sig = key is not None
        op.idx = len(self.ops)
        deps = set()
        isdma = key is not None
        for name in r:
            st = self.res.get(name)
            if st is None:
                st = self.res[name] = [None, {}, []]
            if st[0] is not None:
                deps.add((st[0], 0))
        for name in w:
            st = self.res.get(name)
            if st is None:
                st = self.res[name] = [None, {}, []]
            if st[0] is not None:
                deps.add((st[0], 1))
            for ri in st[1].values():
                deps.add((ri, 2))
            for ri in st[2]:
                deps.add((ri, 2))
        for name in r:
            st = self.res[name]
            if isdma:
                st[2].append(op.idx)
            else:
                st[1][eng] = op.idx
        for name in w:
            st = self.res[name]
            st[0] = op.idx
            st[1] = {}
            st[2] = []
        if isdma:
            prev = self.key_last.get(key)
            if prev is not None:
                deps.add((prev, 1))
            self.key_last[key] = op.idx
            c = self.key_count.get(key, 0) + ndma
            self.key_count[key] = c
            op.ordv = 16 * c
        op.deps = [d for d in deps if d[0] != op.idx]
        self.ops.append(op)
        return op

    def emit(self, nc, es):
        ops = self.ops
        need = []
        for op in ops:
            lst = []
            for (di, kind) in op.deps:
                p = ops[di]
                if p.key is None and op.key is None and p.eng == op.eng:
                    if op.eng == "pe" or kind == 2:
                        continue
                lst.append(di)
                if p.key is None:
                    p.sig = True
            need.append(lst)
        cnt = {e: 0 for e in self.ENGS}
        for op in ops:
            if op.key is None and op.sig:
                cnt[op.eng] += 1
                op.ordv = cnt[op.eng]
        esem = {e: es.enter_context(nc.semaphore("tl_" + e)) for e in self.ENGS}
        ksem = {}
        for i, k in enumerate(self.key_count):
            ksem[k] = es.enter_context(nc.semaphore("dk%d" % i))
        per = {e: [] for e in self.ENGS}
        for op, lst in zip(ops, need):
            per[op.eng].append((op, lst))
        blk = es.enter_context(nc.Block())

        def run(ename, eobj):
            waited = {}
            for op, lst in per[ename]:
                req = {}
                for di in lst:
                    p = ops[di]
                    s = ksem[p.key] if p.key is not None else esem[p.eng]
                    sid = id(s)
                    if sid not in req or req[sid][1] < p.ordv:
                        req[sid] = (s, p.ordv)
                for sid, (s, v) in req.items():
                    if waited.get(sid, 0) < v:
                        eobj.wait_ge(s, v)
                        waited[sid] = v
                ins = op.fn(eobj)
                if op.key is not None:
                    if not isinstance(ins, (list, tuple)):
                        ins = [ins]
                    assert len(ins) == op.ndma, (len(ins), op.ndma)
                    for i_ in ins:
                        i_.then_inc(ksem[op.key], 16)
                elif op.sig:
                    ins.then_inc(esem[ename], 1)
            if ename == "sp":
                for k, c in self.key_count.items():
                    eobj.wait_ge(ksem[k], 16 * c)

        @blk.tensor
        def _(e):
            run("pe", e)

        @blk.scalar
        def _(e):
            run("act", e)

        @blk.vector
        def _(e):
            run("dve", e)

        @blk.gpsimd
        def _(e):
            run("pool", e)

        @blk.sync
        def _(e):
            run("sp", e)


class SeqCtx:
    pass


def build(cfg):
    nc = bass.Bass("TRN2", target_bir_lowering=False)
    NE, NO, DEPTH = cfg.NE, cfg.NO, cfg.DEPTH
    SEQ, PAST, DEC = cfg.SEQ, cfg.PAST, cfg.DEC
    NT = SEQ // TT
    NPT = PAST // TT
    TS = PAST + TT

    def din(name, shape, dt=F32):
        return nc.dram_tensor(name, list(shape), dt, kind="ExternalInput").ap()

    def dout(name, shape):
        return nc.dram_tensor(name, list(shape), F32, kind="ExternalOutput").ap()

    def dint(name, shape, dt):
        return nc.dram_tensor(name, list(shape), dt, kind="Internal").ap()

    xp = din("xp", [SEQ, D])
    xs = din("xs", [DEC, D])
    sca = din("sca", [NE, 30, DA])
    scc = din("scc", [max(NO, 1), 2, DA])
    ckv_in = din("ckv", [max(NO, 1), PAST, KVR])
    ckr_in = din("ckr", [max(NO, 1), PAST, ROPE])
    nrm = din("nrm", [2 * DEPTH + 1, D])
    w_in_even = din("w_in_even", [NE, D, 4096])
    w_out_even = din("w_out_even", [NE, D, D])
    w_in_odd = din("w_in_odd", [max(NO, 1), D, 4160])
    w_uq = din("w_uq", [max(NO, 1), QR, 3072])
    w_ukv = din("w_ukv", [max(NO, 1), KVR, 4096])
    w_out_odd = din("w_out_odd", [max(NO, 1), 3072, D])
    w_up = din("w_ffn_up", [DEPTH, D, DFF])
    w_down = din("w_ffn_down", [DEPTH, DFF, D])
    pcol_in = din("pcol", [128, 320 * NE + 24 * max(NO, 1)])
    lnv_in = din("lnv", [NE, 2, DA])
    qkg_in = din("qkg", [max(NO, 1), 2, 512])
    bsp_in = din("bsp", [NE, 1024])
    wsp_in = din("wsp", [NE, 8, 128, 128])
    invf_in = din("c_invf", [64, 1])
    mask_in = din("c_mask", [128, NSUB * TT])

    yp = dout("yp", [SEQ, D])
    ys = dout("ys", [DEC, D])
    cap = dout("cap", [NE, 30, DA])
    cas = dout("cas", [NE, 30, DA])
    gvs = dout("gvs", [NE, DEC, DA])
    ccp = dout("ccp", [max(NO, 1), 2, DA])
    ccs = dout("ccs", [max(NO, 1), 2, DA])
    latp = dout("latp", [max(NO, 1), SEQ, KVR])
    krp = dout("krp", [max(NO, 1), SEQ, ROPE])
    lats = dout("lats", [max(NO, 1), DEC, KVR])
    krs = dout("krs", [max(NO, 1), DEC, ROPE])

    KTd = {"p": dint("KT_p", [max(NO, 1), NH, 128, SEQ], BF16), "s": dint("KT_s", [max(NO, 1), NH, 128, TS], BF16)}
    Vd = {"p": dint("V_p", [max(NO, 1), NH, SEQ, 128], BF16), "s": dint("V_s", [max(NO, 1), NH, TS, 128], BF16)}
    KRd = {"p": dint("KR_p", [max(NO, 1), 64, SEQ], BF16), "s": dint("KR_s", [max(NO, 1), 64, TS], BF16)}

    P = Prog()
    es = ExitStack()

    def sb(name, shape, dt):
        return es.enter_context(nc.sbuf_tensor("s_" + name, list(shape), dt))

    X = sb("X", [128, NSUB, D], F32)
    gbc = sb("gbc", [128, D], F32)
    xn_tm = [sb("xn_tm%d" % i, [128, D], BF16) for i in range(2)]
    xnT = sb("xnT", [128, 16, TT], BF16)
    NWR = 3
    wr = [sb("wr%d" % i, [128, 16, 512], BF16) for i in range(NWR)]
    big = sb("big", [128, 24, TT], BF16)
    rscr = [sb("rscr%d" % i, [128, TT], BF16) for i in range(2)]
    aT = sb("aT", [128, 8, 30 + TT], F32)
    sg = sb("sg", [128, 4, TT], F32)
    acc = [sb("acc%d" % i, [128, TT], F32) for i in range(2)]
    convo = sb("convo", [128, 8, TT], BF16)
    sqb = [sb("sqb%d" % i, [128, TT], BF16) for i in range(2)]
    uT = sb("uT", [128, 8, TT], BF16)
    v_tm = sb("v_tm", [128, NSUB, DA], BF16)
    gv = [sb("gv%d" % i, [128, DA], F32) for i in range(2)]
    f1 = sb("f1", [128, TT], F32)
    f2 = sb("f2", [128, TT], F32)
    f3 = sb("f3", [128, TT], F32)
    stage = sb("stage", [128, DA], F32)
    ss = sb("ss", [128, 4], F32)
    rstd = sb("rstd", [128, 4], F32)
    bst = sb("bst", [128, 12], F32)
    mv = sb("mv", [128, 2], F32)
    lrs = sb("lrs", [128, 1], F32)
    pcol = sb("pcol", [128, 320 * NE + 24 * max(NO, 1)], F32)
    ones_f = sb("ones_f", [1, 128], F32)
    WsT = sb("WsT", [128, NE, 8, 128], BF16)
    haloA = sb("haloA", [128, NE, 8, 30], F32)
    haloC = sb("haloC", [128, max(NO, 1), 8, 2], F32)
    idf = sb("idf", [128, 128], F32)
    idb = sb("idb", [128, 128], BF16)
    ones_b = sb("ones_b", [128, 128], BF16)
    mask = sb("mask", [128, NSUB, TT], BF16)
    invf = sb("invf", [64, 1], F32)
    sgn = sb("sgn", [64, 1], F32)
    iota_f = sb("iota_f", [64, TT], F32)
    cosT = sb("cosT", [64, TT], F32)
    sinS = sb("sinS", [64, TT], F32)
    cosQ = sb("cosQ", [64, TT], F32)
    sinQ = sb("sinQ", [64, TT], F32)
    ri = sb("ri", [64, TT], I32)
    zqnT = sb("zqnT", [128, 4, TT], BF16)
    ckvT = sb("ckvT", [128, 4, TT], BF16)
    ztm = [sb("ztm%d" % i, [128, 512], F32) for i in range(2)]
    ztb = [sb("ztb%d" % i, [128, 512], BF16) for i in range(2)]
    krT_all = sb("krT_all", [64, TS], BF16)
    krf = sb("krf", [64, TT], F32)
    krb = sb("krb", [64, TT], BF16)
    qn = [sb("qn%d" % i, [128, TT], BF16) for i in range(2)]
    qrb = [sb("qrb%d" % i, [64, TT], BF16) for i in range(2)]
    NKV = 6
    kvK = [sb("kvK%d" % i, [128, TT], BF16) for i in range(NKV)]
    kvV = [sb("kvV%d" % i, [128, NSUB, 128], BF16) for i in range(NKV)]
    NPT_ = 3
    pt = [sb("pt%d" % i, [128, TT], BF16) for i in range(NPT_)]
    NST = 3
    kst = [sb("kst%d" % i, [128, TT], BF16) for i in range(NST)]
    vst = [sb("vst%d" % i, [128, 256], BF16) for i in range(NST)]

    ps = [es.enter_context(nc.psum_tensor("ps%d" % i, [128, 512], F32)) for i in range(8)]
    psb = [p_[:].bitcast(BF16) for p_ in ps]

    cnt = {"fm": 0, "tr": 0, "wr": 0, "rs": 0, "acc": 0, "sq": 0, "gv": 0, "xn": 0, "zt": 0, "q": 0,
           "kv": 0, "pt": 0, "st": 0, "ob": 0}

    def rr(name, n):
        v = cnt[name] % n
        cnt[name] += 1
        return v

    def fm_bank():
        return 4 + rr("fm", 2)

    def tr_bank():
        return 6 + rr("tr", 2)

    def PS(b):
        return ("ps", b)

    def dma(q, out, in_, r, w, key):
        P.add(q, lambda e, o=out, i=in_: e.dma_start(out=o, in_=i), r=r, w=w, key=key)

    def wpiece(src, nk, ncol, dst_col=0, slot=None, extra=None):
        if slot is None:
            slot = rr("wr", NWR)
        srcs = [(src, dst_col, ncol)] + (extra or [])

        def fn(e, slot=slot, srcs=srcs, nk=nk):
            out = []
            for (s_, c0, ncl) in srcs:
                out.append(e.dma_start(out=wr[slot][:, 0:nk, c0:c0 + ncl],
                                       in_=s_.rearrange("(k p) c -> p k c", p=128)))
            return out
        P.add("pool", fn, r=(), w=[("wr", slot)], key=("wr", slot), ndma=len(srcs))
        return slot

    def mm(out, lhsT, rhs, start, stop, r, w):
        P.add("pe", lambda e, o=out, l=lhsT, rh=rhs, s=start, t=stop: e.matmul(o, lhsT=l, rhs=rh, start=s, stop=t),
              r=r, w=w)

    def tp(out, in_, ident, r, w):
        P.add("pe", lambda e, o=out, i=in_, d=ident: e.transpose(out=o, in_=i, identity=d), r=r, w=w)

    def act(out, in_, func, r, w, **kw):
        P.add("act", lambda e, o=out, i=in_, f=func, kw=kw: e.activation(out=o, in_=i, func=f, **kw), r=r, w=w)

    def tt(out, in0, in1, op, r, w, eng="dve"):
        P.add(eng, lambda e, o=out, a=in0, b=in1, p_=op: e.tensor_tensor(out=o, in0=a, in1=b, op=p_), r=r, w=w)

    def ts(out, in0, s1, s2, op0, op1, r, w, eng="dve"):
        if s2 is None:
            P.add(eng, lambda e, o=out, a=in0, s1=s1, p0=op0: e.tensor_scalar(out=o, in0=a, scalar1=s1, scalar2=None, op0=p0),
                  r=r, w=w)
        else:
            P.add(eng, lambda e, o=out, a=in0, s1=s1, s2=s2, p0=op0, p1=op1:
                  e.tensor_scalar(out=o, in0=a, scalar1=s1, scalar2=s2, op0=p0, op1=p1), r=r, w=w)

    def stt(out, in0, sc, in1, op0, op1, r, w, eng="dve"):
        P.add(eng, lambda e, o=out, a=in0, s=sc, b=in1, p0=op0, p1=op1:
              e.scalar_tensor_tensor(out=o, in0=a, scalar=s, in1=b, op0=p0, op1=p1), r=r, w=w)

    def cp(out, in_, r, w, eng="dve"):
        P.add(eng, lambda e, o=out, i=in_: e.tensor_copy(out=o, in_=i), r=r, w=w)

    def ms(ap, val, w, eng="dve"):
        P.add(eng, lambda e, a=ap, v=val: e.memset(a, v), r=(), w=w)

    def recip(out, in_, r, w):
        P.add("dve", lambda e, o=out, i=in_: e.reciprocal(out=o, in_=i), r=r, w=w)

    def rstd_chain(dst, src, n_inv, pp, ncols, r, w):
        ts(dst[0:pp, 0:ncols], src, n_inv, EPS, ALU.mult, ALU.add, r=r, w=w)
        act(dst[0:pp, 0:ncols], dst[0:pp, 0:ncols], AF.Sqrt, r=w, w=w)
        recip(dst[0:pp, 0:ncols], dst[0:pp, 0:ncols], r=w, w=w)

    dma("sp", pcol[:], pcol_in[:, :], r=(), w=["pcol"], key="ld_pcol")
    dma("sp", invf[:], invf_in[:, :], r=(), w=["invf"], key="ld_misc")
    dma("sp", f1[:], mask_in[:, 0:TT], r=(), w=["f1"], key="ld_misc")
    ms(idf[:], 0.0, w=["idf"], eng="pool")
    P.add("pool", lambda e: e.affine_select(out=idf[:], in_=idf[:], pattern=[[-1, 128]], compare_op=ALU.not_equal,
                                            fill=1.0, base=0, channel_multiplier=1), r=["idf"], w=["idf"])
    P.add("pool", lambda e: e.iota(ri[:], pattern=[[1, TT]], base=0, channel_multiplier=0), r=(), w=["ri"])
    cp(idb[:], idf[:], r=["idf"], w=["idb"])
    cp(iota_f[:], ri[:], r=["ri"], w=["iota_f"])
    ms(ones_b[:], 1.0, w=["ones_b"])
    ms(ones_f[:], 1.0, w=["ones_f"])
    ms(sgn[0:32, :], -1.0, w=["sgn"])
    ms(sgn[32:64, :], 1.0, w=["sgn"])
    ms(haloA[:], 0.0, w=["haloA"])
    ms(haloC[:], 0.0, w=["haloC"])
    for kb in range(NSUB):
        if kb > 0:
            dma("sp", f1[:], mask_in[:, kb * TT:(kb + 1) * TT], r=(), w=["f1"], key="ld_misc")
        cp(mask[:, kb, :], f1[:], r=["f1"], w=["mask"])
    for p_ in range(NE):
        for g in range(8):
            dma("sp", f2[:, 0:128], wsp_in[p_, g], r=(), w=["f2"], key="ld_misc")
            P.add("pool", lambda e: e.affine_select(out=f2[:, 0:128], in_=f2[:, 0:128], pattern=[[-1, 128]],
                                                    compare_op=ALU.is_ge, fill=0.0, base=0, channel_multiplier=1),
                  r=["f2"], w=["f2"])
            b = tr_bank()
            tp(ps[b][:, 0:128], f2[:, 0:128], idf[:], r=["f2", "idf"], w=[PS(b)])
            act(WsT[:, p_, g, :], ps[b][:, 0:128], AF.Copy, r=[PS(b)], w=["WsT"])

    def pc_even(p_):
        base = 320 * p_
        return dict(cw=base, cb=base + 248, lg=base + 256, lb=base + 264)

    def pc_odd(p_):
        return 320 * NE + 24 * p_

    def rmsnorm_to_xnT(c, gidx):
        pp, nsub, N = c.pp, c.nsub, c.N
        dma("sp", gbc[:], nrm[gidx].partition_broadcast(128), r=(), w=["gbc"], key="ld_gbc")
        ms(ss[:], 0.0, w=["ss"])
        for s in range(nsub):
            i = rr("xn", 2)
            act(xn_tm[i][0:pp, :], X[0:pp, s, :], AF.Square, r=[("X", s), "ss"], w=[("xn_tm", i), "ss"],
                accum_out=ss[0:pp, s:s + 1])
        rstd_chain(rstd, ss[0:pp, 0:nsub], 1.0 / D, pp, nsub, r=["ss"], w=["rstd"])
        for s in range(nsub):
            i = rr("xn", 2)
            stt(xn_tm[i][0:pp, :], X[0:pp, s, :], rstd[0:pp, s:s + 1], gbc[0:pp, :], ALU.mult, ALU.mult,
                r=[("X", s), "rstd", "gbc"], w=[("xn_tm", i)])
            for half in range(2):
                b = tr_bank()
                for k in range(8):
                    kk = half * 8 + k
                    tp(psb[b][:, k * pp:(k + 1) * pp], xn_tm[i][0:pp, kk * 128:(kk + 1) * 128], idb[0:pp, 0:pp],
                       r=[("xn_tm", i), "idb"], w=[PS(b)])
                act(xnT[:, half * 8:half * 8 + 8, s * 128:s * 128 + pp],
                    psb[b][:, 0:8 * pp].rearrange("p (k t) -> p k t", k=8), AF.Copy,
                    r=[PS(b)], w=[("xnT", s)])

    def xnT_res(c):
        return [("xnT", s) for s in range(c.nsub)]

    def fm_group(c, slot, nk, col, M, rhs_t, rhs_res, b, prow=0):
        N = c.N
        for k in range(nk):
            mm(ps[b][prow:prow + M, 0:N], wr[slot][:, k, col:col + M], rhs_t[:, k, 0:N], k == 0, k == nk - 1,
               r=[("wr", slot)] + rhs_res, w=[PS(b)])

    def residual_add_from(c, s, nb, b):
        pp = c.pp
        tt(X[0:pp, s, nb * 512:(nb + 1) * 512], X[0:pp, s, nb * 512:(nb + 1) * 512], ps[b][0:pp, :], ALU.add,
           r=[("X", s), PS(b)], w=[("X", s)])

    def out_proj(c, W, nkc):
        pp, nsub = c.pp, c.nsub
        bigres = [("big", k) for k in range(nkc)]
        halves = [(0, nkc)] if nkc <= 16 else [(0, nkc // 2), (nkc // 2, nkc)]
        for nb in range(4):
            slots = []
            for (k0, k1) in halves:
                slots.append(wpiece(W[k0 * 128:k1 * 128, nb * 512:(nb + 1) * 512], k1 - k0, 512))
            for s in range(nsub):
                for hi, (k0, k1) in enumerate(halves):
                    for k in range(k0, k1):
                        mm(ps[s][0:pp, :], big[:, k, s * 128:s * 128 + pp], wr[slots[hi]][:, k - k0, :],
                           k == 0, k == nkc - 1, r=[("wr", slots[hi]), ("big", k)], w=[PS(s)])
                residual_add_from(c, s, nb, s)

    def ffn(c, l):
        pp, nsub, N = c.pp, c.nsub, c.N
        for q in range(4):
            for j in range(4):
                slot = wpiece(w_up[l, :, q * 2048 + j * 512: q * 2048 + (j + 1) * 512], 16, 512)
                for m in range(4):
                    hc = j * 4 + m
                    b = fm_bank()
                    fm_group(c, slot, 16, m * 128, 128, xnT, xnT_res(c), b)
                    i = rr("rs", 2)
                    act(rscr[i][:, 0:N], ps[b][:, 0:N], AF.Relu, r=[PS(b)], w=[("rscr", i)])
                    act(big[:, hc, 0:N], rscr[i][:, 0:N], AF.Square, r=[("rscr", i)], w=[("big", hc)])
            for nb in range(4):
                slot = wpiece(w_down[l, q * 2048:(q + 1) * 2048, nb * 512:(nb + 1) * 512], 16, 512)
                for s in range(nsub):
                    for k in range(16):
                        mm(ps[s][0:pp, :], big[:, k, s * 128:s * 128 + pp], wr[slot][:, k, :], k == 0, k == 15,
                           r=[("wr", slot), ("big", k)], w=[PS(s)])
                    residual_add_from(c, s, nb, s)

    def save_state_rows(c, src3, ncols_state, col0, out_ap):
        n = ncols_state
        for half in range(2):
            b = tr_bank()
            for k in range(4):
                ch = half * 4 + k
                tp(ps[b][0:n, k * 128:(k + 1) * 128], src3[:, ch, col0:col0 + n], idf[:], r=["aT", "idf"], w=[PS(b)])
            act(stage[0:n, half * 512:(half + 1) * 512], ps[b][0:n, :], AF.Copy, r=[PS(b)], w=["stage"])
        dma("sp", out_ap, stage[0:n, :], r=["stage"], w=(), key="st_stage")

    def load_state_rows(c, in_ap, n, dst3):
        dma("sp", stage[0:n, :], in_ap, r=(), w=["stage"], key="ld_stage")
        b = tr_bank()
        for ch in range(8):
            tp(ps[b][:, ch * n:(ch + 1) * n], stage[0:n, ch * 128:(ch + 1) * 128], idf[0:n, 0:n],
               r=["stage", "idf"], w=[PS(b)])
        act(dst3[:, :, 0:n], ps[b][:, 0:8 * n].rearrange("p (k t) -> p k t", k=8), AF.Copy, r=[PS(b)], w=["aT"])

    def even_mixer(c, p_, last):
        pp, nsub, N = c.pp, c.nsub, c.N
        W = w_in_even[p_]
        pc = pc_even(p_)
        H = 30
        if c.name == "s":
            load_state_rows(c, sca[p_], H, aT)
        else:
            cp(aT[:, :, 0:H], haloA[:, p_, :, :], r=["haloA"], w=["aT"])
        for grp in range(2):
            slot = wpiece(W[:, 1024 + grp * 512:1024 + (grp + 1) * 512], 16, 512)
            for m in range(4):
                b = fm_bank()
                fm_group(c, slot, 16, m * 128, 128, xnT, xnT_res(c), b)
                act(sg[:, m, 0:N], ps[b][:, 0:N], AF.Sigmoid, r=[PS(b)], w=[("sg", m)])
            slot = wpiece(W[:, grp * 512:(grp + 1) * 512], 16, 512)
            for m in range(4):
                ch = grp * 4 + m
                b = fm_bank()
                fm_group(c, slot, 16, m * 128, 128, xnT, xnT_res(c), b)
                tt(aT[:, ch, H:H + N], ps[b][:, 0:N], sg[:, m, 0:N], ALU.mult, r=[PS(b), ("sg", m)], w=["aT"])
        for grp in range(2):
            slot = wpiece(W[:, 2048 + grp * 512:2048 + (grp + 1) * 512], 16, 512)
            for m in range(4):
                ch = grp * 4 + m
                b = fm_bank()
                fm_group(c, slot, 16, m * 128, 128, xnT, xnT_res(c), b)
                act(uT[:, ch, 0:N], ps[b][:, 0:N], AF.Gelu, r=[PS(b)], w=[("uT", ch)])
        vslots = [wpiece(W[:, 3072 + j * 512:3072 + (j + 1) * 512], 16, 512) for j in range(2)]
        dma("sp", gbc[:], lnv_in[p_].rearrange("a b -> (a b)").partition_broadcast(128), r=(), w=["gbc"], key="ld_gbc")
        for s in range(nsub):
            gi = rr("gv", 2)
            for j in range(2):
                b = j + 2 * (s % 2)
                for k in range(16):
                    mm(ps[b][0:pp, :], xnT[:, k, s * 128:s * 128 + pp], wr[vslots[j]][:, k, :], k == 0, k == 15,
                       r=[("wr", vslots[j]), ("xnT", s)], w=[PS(b)])
                act(gv[gi][0:pp, j * 512:(j + 1) * 512], ps[b][0:pp, :], AF.Gelu, r=[PS(b)], w=[("gv", gi)])
            for j in range(2):
                P.add("dve", lambda e, gi=gi, j=j, pp=pp: e.bn_stats(out=bst[0:pp, j * 6:(j + 1) * 6], in_=gv[gi][0:pp, j * 512:(j + 1) * 512]),
                      r=[("gv", gi)], w=["bst"])
            P.add("dve", lambda e, pp=pp: e.bn_aggr(out=mv[0:pp, :], in_=bst[0:pp, :]), r=["bst"], w=["mv"])
            rstd_chain(lrs, mv[0:pp, 1:2], 1.0, pp, 1, r=["mv"], w=["lrs"])
            stt(gv[gi][0:pp, :], gv[gi][0:pp, :], mv[0:pp, 0:1], gbc[0:pp, 0:DA], ALU.subtract, ALU.mult,
                r=[("gv", gi), "mv", "gbc"], w=[("gv", gi)])
            stt(gv[gi][0:pp, :], gv[gi][0:pp, :], lrs[0:pp, 0:1], gbc[0:pp, DA:2 * DA], ALU.mult, ALU.add,
                r=[("gv", gi), "lrs", "gbc"], w=[("gv", gi)])
            act(v_tm[0:pp, s, :], gv[gi][0:pp, :], AF.Copy, r=[("gv", gi)], w=[("v_tm", s)])
            if c.name == "s":
                dma("sp", gvs[p_], gv[gi][0:pp, :], r=[("gv", gi)], w=(), key=("st_gv", gi))
        SUMB, SQB = 0, 1
        for ch in range(8):
            ai = rr("acc", 2)
            a_ = acc[ai]
            ts(a_[:, 0:N], aT[:, ch, 0:N], pcol[:, pc["cw"] + ch * 31:pc["cw"] + ch * 31 + 1],
               pcol[:, pc["cb"] + ch:pc["cb"] + ch + 1], ALU.mult, ALU.add, r=["aT", "pcol"], w=[("acc", ai)])
            for j in range(1, CONVA):
                stt(a_[:, 0:N], aT[:, ch, j:j + N], pcol[:, pc["cw"] + ch * 31 + j:pc["cw"] + ch * 31 + j + 1], a_[:, 0:N],
                    ALU.mult, ALU.add, r=["aT", "pcol", ("acc", ai)], w=[("acc", ai)])
            act(convo[:, ch, 0:N], a_[:, 0:N], AF.Copy, r=[("acc", ai)], w=[("convo", ch)])
            si = rr("sq", 2)
            act(sqb[si][:, 0:N], a_[:, 0:N], AF.Square, r=[("acc", ai)], w=[("sqb", si)])
            mm(ps[SUMB][:, 0:N], ones_b[:], convo[:, ch, 0:N], ch == 0, ch == 7, r=["ones_b", ("convo", ch)], w=[PS(SUMB)])
            mm(ps[SQB][:, 0:N], ones_b[:], sqb[si][:, 0:N], ch == 0, ch == 7, r=["ones_b", ("sqb", si)], w=[PS(SQB)])
        if c.name == "s":
            save_state_rows(c, aT, H, N, cas[p_])
        else:
            if last:
                save_state_rows(c, aT, H, N, cap[p_])
            cp(haloA[:, p_, :, :], aT[:, :, N:N + H], r=["aT"], w=["haloA"])
        ts(f1[:, 0:N], ps[SUMB][:, 0:N], 1.0 / DA, None, ALU.mult, None, r=[PS(SUMB)], w=["f1"])
        tt(f3[:, 0:N], f1[:, 0:N], f1[:, 0:N], ALU.mult, r=["f1"], w=["f3"])
        stt(f2[:, 0:N], ps[SQB][:, 0:N], 1.0 / DA, f3[:, 0:N], ALU.mult, ALU.subtract, r=[PS(SQB), "f3"], w=["f2"])
        ts(f2[:, 0:N], f2[:, 0:N], 0.0, EPS, ALU.max, ALU.add, r=["f2"], w=["f2"])
        act(f2[:, 0:N], f2[:, 0:N], AF.Sqrt, r=["f2"], w=["f2"])
        recip(f2[:, 0:N], f2[:, 0:N], r=["f2"], w=["f2"])
        for ch in range(8):
            ai = rr("acc", 2)
            a_ = acc[ai]
            tt(a_[:, 0:N], convo[:, ch, 0:N], f1[:, 0:N], ALU.subtract, r=[("convo", ch), "f1"], w=[("acc", ai)])
            tt(a_[:, 0:N], a_[:, 0:N], f2[:, 0:N], ALU.mult, r=[("acc", ai), "f2"], w=[("acc", ai)])
            act(big[:, ch, 0:N], a_[:, 0:N], AF.Silu, r=[("acc", ai), "pcol"], w=[("big", ch)],
                scale=pcol[:, pc["lg"] + ch:pc["lg"] + ch + 1], bias=pcol[:, pc["lb"] + ch:pc["lb"] + ch + 1])
        L = pp
        dma("sp", stage[0:1, :], bsp_in[p_:p_ + 1, :], r=(), w=["stage"], key="ld_stage")
        for g in range(8):
            b = fm_bank()
            for s in range(nsub):
                mm(ps[b][:, s * 128:s * 128 + L], v_tm[0:L, s, g * 128:(g + 1) * 128], WsT[0:L, p_, g, 0:L], True, False,
                   r=[("v_tm", s), "WsT"], w=[PS(b)])
                mm(ps[b][:, s * 128:s * 128 + L], ones_f[0:1, :], stage[0:1, g * 128:g * 128 + L], False, True,
                   r=["ones_f", "stage"], w=[PS(b)])
            tt(big[:, 8 + g, 0:N], ps[b][:, 0:N], uT[:, g, 0:N], ALU.mult, r=[PS(b), ("uT", g)], w=[("big", 8 + g)])
        out_proj(c, w_out_even[p_], 16)

    def rope_tables(c):
        N = c.N
        pos0 = float(c.pos0)
        ts(krf[:, 0:N], iota_f[:, 0:N], pos0, None, ALU.add, None, r=["iota_f"], w=["krf"])
        ts(krf[:, 0:N], krf[:, 0:N], invf[:, 0:1], None, ALU.mult, None, r=["krf", "invf"], w=["krf"])
        for (dst, phase) in ((cosT, 0.25), (sinS, 0.0)):
            nm = "cosT" if dst is cosT else "sinS"
            ts(f1[0:64, 0:N], krf[:, 0:N], 1.0 / TWO_PI, phase, ALU.mult, ALU.add, r=["krf"], w=["f1"])
            cp(ri[:, 0:N], f1[0:64, 0:N], r=["f1"], w=["ri"])
            cp(f2[0:64, 0:N], ri[:, 0:N], r=["ri"], w=["f2"])
            tt(f1[0:64, 0:N], f1[0:64, 0:N], f2[0:64, 0:N], ALU.subtract, r=["f1", "f2"], w=["f1"])
            stt(f2[0:64, 0:N], f1[0:64, 0:N], 0.5, f1[0:64, 0:N], ALU.is_gt, ALU.subtract, r=["f1"], w=["f2"])
            stt(f1[0:64, 0:N], f1[0:64, 0:N], -0.5, f2[0:64, 0:N], ALU.is_lt, ALU.subtract, r=["f1", "f2"], w=["f1"])
            act(dst[:, 0:N], f1[0:64, 0:N], AF.Sin, r=["f1"], w=[nm], scale=TWO_PI)
        ts(sinS[:, 0:N], sinS[:, 0:N], sgn[:, 0:1], None, ALU.mult, None, r=["sinS", "sgn"], w=["sinS"])
        ts(cosQ[:, 0:N], cosT[:, 0:N], QSCALE, None, ALU.mult, None, r=["cosT"], w=["cosQ"])
        ts(sinQ[:, 0:N], sinS[:, 0:N], QSCALE, None, ALU.mult, None, r=["sinS"], w=["sinQ"])

    def kv_up(c_name, p_, srcT, src_res, pos0, N, pp, nsub):
        Wk = w_ukv[p_]
        for pi in range(8):
            slot = wpiece(Wk[:, pi * 512:(pi + 1) * 512], 4, 512)
            for hh in range(2):
                h = 2 * pi + hh
                b = fm_bank()
                for k in range(4):
                    mm(ps[b][:, 0:N], wr[slot][:, k, hh * 256:hh * 256 + 128], srcT[:, k, 0:N], k == 0, k == 3,
                       r=[("wr", slot)] + src_res, w=[PS(b)])
                si = rr("st", NST)
                act(kst[si][:, 0:N], ps[b][:, 0:N], AF.Copy, r=[PS(b)], w=[("kst", si)])
                dma("sp", KTd[c_name][p_, h, :, pos0:pos0 + N], kst[si][:, 0:N], r=[("kst", si)],
                    w=[("KT", c_name, p_, h, pos0 // TT)], key=("st_k", si))
            for s in range(nsub):
                b = s
                for k in range(4):
                    mm(ps[b][0:pp, 0:256].rearrange("p (h d) -> p h d", h=2), srcT[:, k, s * 128:s * 128 + pp],
                       wr[slot][:, k, :].rearrange("p (h t d) -> p h t d", h=2, t=2)[:, :, 1, :],
                       k == 0, k == 3, r=[("wr", slot)] + src_res, w=[PS(b)])
                si = rr("st", NST)
                act(vst[si][0:pp, :], ps[b][0:pp, 0:256], AF.Copy, r=[PS(b)], w=[("vst", si)])
                for hh in range(2):
                    h = 2 * pi + hh
                    dma("sp", Vd[c_name][p_, h, pos0 + s * 128:pos0 + s * 128 + pp, :], vst[si][0:pp, hh * 128:(hh + 1) * 128],
                        r=[("vst", si)], w=[("V", c_name, p_, h, pos0 // TT, s)], key=("st_v", si, hh))

    def odd_mixer(c, p_, last):
        pp, nsub, N = c.pp, c.nsub, c.N
        W = w_in_odd[p_]
        pco = pc_odd(p_)
        H = 2
        rope_tables(c)
        if c.name == "s":
            load_state_rows(c, scc[p_], H, aT)
        else:
            cp(aT[:, :, 0:H], haloC[:, p_, :, :], r=["haloC"], w=["aT"])
        for grp in range(2):
            slot = wpiece(W[:, 1024 + grp * 512:1024 + (grp + 1) * 512], 16, 512)
            for m in range(4):
                b = fm_bank()
                fm_group(c, slot, 16, m * 128, 128, xnT, xnT_res(c), b)
                act(sg[:, m, 0:N], ps[b][:, 0:N], AF.Copy, r=[PS(b)], w=[("sg", m)])
            slot = wpiece(W[:, 2048 + grp * 512:2048 + (grp + 1) * 512], 16, 512)
            for m in range(4):
                ch = grp * 4 + m
                b = fm_bank()
                fm_group(c, slot, 16, m * 128, 128, xnT, xnT_res(c), b)
                tt(aT[:, ch, H:H + N], ps[b][:, 0:N], sg[:, m, 0:N], ALU.mult, r=[PS(b), ("sg", m)], w=["aT"])
        for ch in range(8):
            ai = rr("acc", 2)
            a_ = acc[ai]
            ts(a_[:, 0:N], aT[:, ch, 0:N], pcol[:, pco + ch * 3:pco + ch * 3 + 1], None, ALU.mult, None,
               r=["aT", "pcol"], w=[("acc", ai)])
            stt(a_[:, 0:N], aT[:, ch, 1:1 + N], pcol[:, pco + ch * 3 + 1:pco + ch * 3 + 2], a_[:, 0:N], ALU.mult, ALU.add,
                r=["aT", "pcol", ("acc", ai)], w=[("acc", ai)])
            stt(convo[:, ch, 0:N], aT[:, ch, 2:2 + N], pcol[:, pco + ch * 3 + 2:pco + ch * 3 + 3], a_[:, 0:N], ALU.mult, ALU.add,
                r=["aT", "pcol", ("acc", ai)], w=[("convo", ch)])
        if c.name == "s":
            save_state_rows(c, aT, H, N, ccs[p_])
        else:
            if last:
                save_state_rows(c, aT, H, N, ccp[p_])
            cp(haloC[:, p_, :, :], aT[:, :, N:N + H], r=["aT"], w=["haloC"])
        for grp in range(2):
            slot = wpiece(W[:, grp * 512:(grp + 1) * 512], 16, 512)
            for m in range(4):
                ch = grp * 4 + m
                b = fm_bank()
                fm_group(c, slot, 16, m * 128, 128, xnT, xnT_res(c), b)
                tt(big[:, ch, 0:N], ps[b][:, 0:N], convo[:, ch, 0:N], ALU.mult, r=[PS(b), ("convo", ch)], w=[("big", ch)])
        dma("sp", gbc[:, 0:1024], qkg_in[p_].rearrange("a b -> (a b)").partition_broadcast(128), r=(), w=["gbc"], key="ld_gbc")
        for which in range(2):
            slot = wpiece(W[:, 3072 + which * 512:3072 + (which + 1) * 512], 16, 512)
            dstT = zqnT if which == 0 else ckvT
            dname = "zqnT" if which == 0 else "ckvT"
            ms(ss[:], 0.0, w=["ss"])
            for s in range(nsub):
                b = s
                for k in range(16):
                    mm(ps[b][0:pp, :], xnT[:, k, s * 128:s * 128 + pp], wr[slot][:, k, :], k == 0, k == 15,
                       r=[("wr", slot), ("xnT", s)], w=[PS(b)])
                zi = rr("zt", 2)
                act(ztm[zi][0:pp, :], ps[b][0:pp, :], AF.Square, r=[PS(b), "ss"], w=[("ztm", zi), "ss"],
                    accum_out=ss[0:pp, s:s + 1])
            rstd_chain(rstd, ss[0:pp, 0:nsub], 1.0 / 512, pp, nsub, r=["ss"], w=["rstd"])
            for s in range(nsub):
                b = s
                zi = rr("zt", 2)
                stt(ztm[zi][0:pp, :], ps[b][0:pp, :], rstd[0:pp, s:s + 1], gbc[0:pp, which * 512:(which + 1) * 512], ALU.mult, ALU.mult,
                    r=[PS(b), "rstd", "gbc"], w=[("ztm", zi)])
                if which == 1:
                    out_ap = (lats[p_] if c.name == "s" else latp[p_, c.pos0 + s * 128:c.pos0 + s * 128 + pp, :])
                    dma("sp", out_ap, ztm[zi][0:pp, :], r=[("ztm", zi)], w=(), key=("st_lat", zi))
                act(ztb[zi][0:pp, :], ztm[zi][0:pp, :], AF.Copy, r=[("ztm", zi)], w=[("ztb", zi)])
                tb = tr_bank()
                for k in range(4):
                    tp(psb[tb][:, k * pp:(k + 1) * pp], ztb[zi][0:pp, k * 128:(k + 1) * 128], idb[0:pp, 0:pp],
                       r=[("ztb", zi), "idb"], w=[PS(tb)])
                act(dstT[:, :, s * 128:s * 128 + pp], psb[tb][:, 0:4 * pp].rearrange("p (k t) -> p k t", k=4), AF.Copy,
                    r=[PS(tb)], w=[dname])
        slot = wpiece(W[:, 4096:4160], 16, 64, extra=[(W[:, 4128:4160], 64, 32), (W[:, 4096:4128], 96, 32)])
        bA = fm_bank()
        fm_group(c, slot, 16, 0, 64, xnT, xnT_res(c), bA)
        bB = fm_bank()
        fm_group(c, slot, 16, 64, 64, xnT, xnT_res(c), bB)
        tt(f1[0:64, 0:N], ps[bA][0:64, 0:N], cosT[:, 0:N], ALU.mult, r=[PS(bA), "cosT"], w=["f1"])
        tt(f2[0:64, 0:N], ps[bB][0:64, 0:N], sinS[:, 0:N], ALU.mult, r=[PS(bB), "sinS"], w=["f2"])
        tt(krf[:, 0:N], f1[0:64, 0:N], f2[0:64, 0:N], ALU.add, r=["f1", "f2"], w=["krf"])
        act(krb[:, 0:N], krf[:, 0:N], AF.Copy, r=["krf"], w=["krb"])
        dma("sp", KRd[c.name][p_, :, c.pos0:c.pos0 + N], krb[:, 0:N], r=["krb"], w=[("KR", c.name, p_, c.pos0 // TT)], key="st_kr")
        tb = tr_bank()
        for s in range(nsub):
            tp(ps[tb][0:pp, s * 64:(s + 1) * 64], krf[:, s * 128:s * 128 + pp], idf[0:64, 0:64], r=["krf", "idf"], w=[PS(tb)])
        act(stage[0:pp, 0:nsub * 64], ps[tb][0:pp, 0:nsub * 64], AF.Copy, r=[PS(tb)], w=["stage"])
        if c.name == "s":
            dma("sp", krs[p_], stage[0:pp, 0:64], r=["stage"], w=(), key="st_stage")
        else:
            dma("sp", krp[p_, c.pos0:c.pos0 + N, :].rearrange("(s p) r -> p s r", p=128),
                stage[0:pp, 0:nsub * 64].rearrange("p (s r) -> p s r", s=nsub), r=["stage"], w=(), key="st_stage")
        kv_up(c.name, p_, ckvT, ["ckvT"], c.pos0, N, pp, nsub)
        nkeys = c.pos0 + N
        dma("sp", krT_all[:, 0:nkeys], KRd[c.name][p_, :, 0:nkeys], r=[("KR", c.name, p_, kt_) for kt_ in range(c.pos0 // TT + 1)], w=["krT_all"], key="ld_krT")
        Wq = w_uq[p_]
        nkt_full = c.pos0 // TT
        for h in range(NH):
            if h % 2 == 0:
                ex = []
                for hh in range(2):
                    o = hh * 256
                    src0 = (h + hh) * 192
                    if hh == 1:
                        ex.append((Wq[:, src0:src0 + 192], o, 192))
                    ex.append((Wq[:, src0 + 160:src0 + 192], o + 192, 32))
                    ex.append((Wq[:, src0 + 128:src0 + 160], o + 224, 32))
                qslot = wpiece(Wq[:, h * 192:h * 192 + 192], 4, 192, extra=ex)
            o = (h % 2) * 256
            qi = rr("q", 2)
            b = fm_bank()
            fm_group(c, qslot, 4, o, 128, zqnT, ["zqnT"], b)
            act(qn[qi][:, 0:N], ps[b][:, 0:N], AF.Identity, r=[PS(b)], w=[("qn", qi)], scale=QSCALE)
            bA = fm_bank()
            fm_group(c, qslot, 4, o + 128, 64, zqnT, ["zqnT"], bA)
            bB = fm_bank()
            fm_group(c, qslot, 4, o + 192, 64, zqnT, ["zqnT"], bB)
            tt(f1[0:64, 0:N], ps[bA][0:64, 0:N], cosQ[:, 0:N], ALU.mult, r=[PS(bA), "cosQ"], w=["f1"])
            tt(f2[0:64, 0:N], ps[bB][0:64, 0:N], sinQ[:, 0:N], ALU.mult, r=[PS(bB), "sinQ"], w=["f2"])
            tt(qrb[qi][:, 0:N], f1[0:64, 0:N], f2[0:64, 0:N], ALU.add, r=["f1", "f2"], w=[("qrb", qi)])
            blocks = []
            for kt in range(nkt_full):
                for kb in range(NSUB):
                    blocks.append((kt, kb, 128, False))
            if c.name == "p":
                for kb in range(NSUB):
                    blocks.append((nkt_full, kb, 128, True))
            else:
                blocks.append((nkt_full, 0, N, False))
            ob = rr("ob", 2)
            OB, RB = ob, 2 + ob
            slots = {}

            def load_kt(kt, h=h):
                ki = rr("kv", NKV)
                kn = TT if (c.name == "p" or kt < nkt_full) else N
                dma("sp", kvK[ki][:, 0:kn], KTd[c.name][p_, h, :, kt * TT:kt * TT + kn], r=[("KT", c.name, p_, h, kt)],
                    w=[("kvK", ki)], key=("ld_k", ki))
                if kn == TT:
                    dma("sp", kvV[ki][:, :, :], Vd[c.name][p_, h, kt * TT:(kt + 1) * TT, :].rearrange("(b p) d -> p b d", p=128),
                        r=[("V", c.name, p_, h, kt, s_) for s_ in range(NSUB)], w=[("kvV", ki)], key=("ld_v", ki))
                else:
                    dma("sp", kvV[ki][0:kn, 0, :], Vd[c.name][p_, h, kt * TT:kt * TT + kn, :],
                        r=[("V", c.name, p_, h, kt, 0)], w=[("kvV", ki)], key=("ld_v", ki))
                slots[kt] = ki

            sbanks = {}

            def score(ix):
                kt, kb, kp, diag = blocks[ix]
                if kt not in slots:
                    load_kt(kt)
                ki = slots[kt]
                b = fm_bank()
                sbanks[ix] = b
                mm(ps[b][0:kp, 0:N], kvK[ki][:, kb * 128:kb * 128 + kp], qn[qi][:, 0:N], True, False,
                   r=[("kvK", ki), ("qn", qi)], w=[PS(b)])
                mm(ps[b][0:kp, 0:N], krT_all[:, kt * TT + kb * 128:kt * TT + kb * 128 + kp], qrb[qi][:, 0:N], False, True,
                   r=["krT_all", ("qrb", qi)], w=[PS(b)])

            score(0)
            nblk = len(blocks)
            for ix in range(nblk):
                kt, kb, kp, diag = blocks[ix]
                if ix + 1 < nblk:
                    score(ix + 1)
                b = sbanks[ix]
                pi = rr("pt", NPT_)
                act(pt[pi][0:kp, 0:N], ps[b][0:kp, 0:N], AF.Exp, r=[PS(b)], w=[("pt", pi)])
                if diag:
                    tt(pt[pi][0:kp, 0:N], pt[pi][0:kp, 0:N], mask[0:kp, kb, 0:N], ALU.mult, r=[("pt", pi), "mask"], w=[("pt", pi)])
                ki = slots[kt]
                mm(ps[OB][:, 0:N], kvV[ki][0:kp, kb, :], pt[pi][0:kp, 0:N], ix == 0, ix == nblk - 1,
                   r=[("kvV", ki), ("pt", pi)], w=[PS(OB)])
                mm(ps[RB][:, 0:N], ones_b[0:kp, :], pt[pi][0:kp, 0:N], ix == 0, ix == nblk - 1,
                   r=["ones_b", ("pt", pi)], w=[PS(RB)])
            recip(f3[:, 0:N], ps[RB][:, 0:N], r=[PS(RB)], w=["f3"])
            tt(big[:, 8 + h, 0:N], ps[OB][:, 0:N], f3[:, 0:N], ALU.mult, r=[PS(OB), "f3"], w=[("big", 8 + h)])
        out_proj(c, w_out_odd[p_], 24)

    def sample_cache_prep(p_):
        cst = X[:, 0, 0:NSUB * 512].rearrange("p (s r) -> p s r", s=NSUB)
        ckrst = X[:, 1, 0:NSUB * 64].rearrange("p (s r) -> p s r", s=NSUB)
        for kt in range(NPT):
            dma("sp", cst, ckv_in[p_, kt * TT:(kt + 1) * TT, :].rearrange("(s p) r -> p s r", p=128),
                r=(), w=[("X", 0)], key=("ld_x", 0))
            for s in range(NSUB):
                act(ztb[0][:, :], cst[:, s, :], AF.Copy, r=[("X", 0)], w=[("ztb", 0)])
                tb = tr_bank()
                for k in range(4):
                    tp(psb[tb][:, k * 128:(k + 1) * 128], ztb[0][:, k * 128:(k + 1) * 128], idb[:], r=[("ztb", 0), "idb"], w=[PS(tb)])
                act(ckvT[:, :, s * 128:(s + 1) * 128], psb[tb][:, 0:512].rearrange("p (k t) -> p k t", k=4), AF.Copy,
                    r=[PS(tb)], w=["ckvT"])
            kv_up("s", p_, ckvT, ["ckvT"], kt * TT, TT, 128, NSUB)
            dma("sp", ckrst, ckr_in[p_, kt * TT:(kt + 1) * TT, :].rearrange("(s p) r -> p s r", p=128),
                r=(), w=[("X", 1)], key=("ld_x", 1))
            tb = tr_bank()
            for s in range(NSUB):
                tp(ps[tb][0:64, s * 128:(s + 1) * 128], ckrst[:, s, :], idf[:], r=[("X", 1), "idf"], w=[PS(tb)])
            act(krb[:, :], ps[tb][0:64, 0:TT], AF.Copy, r=[PS(tb)], w=["krb"])
            dma("sp", KRd["s"][p_, :, kt * TT:(kt + 1) * TT], krb[:, :], r=["krb"], w=[("KR", "s", p_, kt)], key="st_kr")

    def process_tile(c, last):
        pp, nsub, N = c.pp, c.nsub, c.N
        for s in range(nsub):
            dma("sp", X[0:pp, s, :], c.x_in[c.row0 + s * 128:c.row0 + s * 128 + pp, :], r=(), w=[("X", s)], key=("ld_x", s))
        for l in range(DEPTH):
            rmsnorm_to_xnT(c, l)
            if l % 2 == 0:
                even_mixer(c, l // 2, last)
            else:
                odd_mixer(c, l // 2, last)
            rmsnorm_to_xnT_ffn(c, DEPTH + l)
            ffn(c, l)
        dma("sp", gbc[:], nrm[2 * DEPTH].partition_broadcast(128), r=(), w=["gbc"], key="ld_gbc")
        ms(ss[:], 0.0, w=["ss"])
        for s in range(nsub):
            i = rr("xn", 2)
            act(xn_tm[i][0:pp, :], X[0:pp, s, :], AF.Square, r=[("X", s), "ss"], w=[("xn_tm", i), "ss"],
                accum_out=ss[0:pp, s:s + 1])
        rstd_chain(rstd, ss[0:pp, 0:nsub], 1.0 / D, pp, nsub, r=["ss"], w=["rstd"])
        for s in range(nsub):
            stt(X[0:pp, s, :], X[0:pp, s, :], rstd[0:pp, s:s + 1], gbc[0:pp, :], ALU.mult, ALU.mult,
                r=[("X", s), "rstd", "gbc"], w=[("X", s)])
            dma("sp", c.y_out[c.row0 + s * 128:c.row0 + s * 128 + pp, :], X[0:pp, s, :], r=[("X", s)], w=(), key=("st_y", s))

    rmsnorm_to_xnT_ffn = rmsnorm_to_xnT

    for p_ in range(NO):
        sample_cache_prep(p_)
    for t in range(NT):
        c = SeqCtx()
        c.name, c.N, c.pp, c.nsub = "p", TT, 128, NSUB
        c.pos0, c.row0, c.x_in, c.y_out = t * TT, t * TT, xp, yp
        process_tile(c, last=(t == NT - 1))
    c = SeqCtx()
    c.name, c.N, c.pp, c.nsub = "s", DEC, DEC, 1
    c.pos0, c.row0, c.x_in, c.y_out = PAST, 0, xs, ys
    process_tile(c, last=True)

    P.emit(nc, es)
    es.close()
    return nc


def _consts():
    half = 32
    inv = (10000.0 ** (-np.arange(half, dtype=np.float32) / half)).astype(np.float32)
    invf = np.concatenate([inv, inv]).reshape(64, 1).astype(np.float32)
    k = np.arange(128)[:, None, None]
    r = np.arange(NSUB)[None, :, None]
    q = np.arange(TT)[None, None, :]
    m = (((r * 128 + k) // 64) <= (q // 64)).astype(np.float32).reshape(128, NSUB * TT)
    return invf, np.ascontiguousarray(m)


def run(cfg, inputs, trace=False):
    f = lambda a: np.ascontiguousarray(np.asarray(a, dtype=np.float32))
    NE, NO = cfg.NE, cfg.NO
    NOm = max(NO, 1)
    invf, maskc = _consts()
    cols = []
    for p_ in range(NE):
        cw = f(inputs["conv_a_w"])[p_]
        cols.append(cw.T.reshape(8, 128, 31).transpose(1, 0, 2).reshape(128, 248))
        for nm in ("conv_a_b", "ln_a_g", "ln_a_b"):
            cols.append(f(inputs[nm])[p_].reshape(8, 128).T)
        cols.append(np.zeros((128, 320 - 248 - 24), np.float32))
    for p_ in range(NOm):
        if NO:
            cw = f(inputs["conv_c_w"])[p_]
            cols.append(cw.T.reshape(8, 128, 3).transpose(1, 0, 2).reshape(128, 24))
        else:
            cols.append(np.zeros((128, 24), np.float32))
    pcol = np.ascontiguousarray(np.concatenate(cols, axis=1))
    nrm = np.ascontiguousarray(np.concatenate([f(inputs["norm_mix"]), f(inputs["norm_ffn"]),
                                               f(inputs["norm_final"])[None]], axis=0))
    lnv = np.ascontiguousarray(np.stack([f(inputs["ln_v_g"]), f(inputs["ln_v_b"])], axis=1))
    qkg = np.ascontiguousarray(np.stack([f(inputs["q_norm_g"]), f(inputs["kv_norm_g"])], axis=1))
    bsp = np.ascontiguousarray(f(inputs["b_spatial"]).reshape(NE, 1024))
    shared = dict(nrm=nrm, w_in_even=f(inputs["w_in_even"]), w_out_even=f(inputs["w_out_even"]),
                  w_in_odd=f(inputs["w_in_odd"]), w_uq=f(inputs["w_uq"]), w_ukv=f(inputs["w_ukv"]),
                  w_out_odd=f(inputs["w_out_odd"]), w_ffn_up=f(inputs["w_ffn_up"]), w_ffn_down=f(inputs["w_ffn_down"]),
                  pcol=pcol, lnv=lnv, qkg=qkg, bsp=bsp, wsp=f(inputs["w_spatial"]), c_invf=invf, c_mask=maskc)
    xpr, xsa = f(inputs["x_prompt"]), f(inputs["x_sample"])
    sca, scc = f(inputs["state_conv_a"]), f(inputs["state_conv_c"])
    ckv, ckr = f(inputs["cache_kv_latent"]), f(inputs["cache_k_rope"])
    in_maps = []
    for c in range(cfg.NCORES):
        m = dict(shared)
        m["xp"] = xpr[c]
        m["xs"] = xsa[c]
        m["sca"] = np.ascontiguousarray(sca[:, c])
        m["scc"] = np.ascontiguousarray(scc[:, c])
        m["ckv"] = np.ascontiguousarray(ckv[:, c])
        m["ckr"] = np.ascontiguousarray(ckr[:, c])
        in_maps.append(m)
    nc = build(cfg)
    res = run_bass_kernel_spmd(nc, in_maps, core_ids=list(range(cfg.NCORES)), **({"trace": True} if trace else {}))
    R = res.results
    st0 = lambda k: np.stack([R[c][k] for c in range(cfg.NCORES)], axis=0)
    st1 = lambda k: np.stack([R[c][k] for c in range(cfg.NCORES)], axis=1)
    outs = (st0("yp"), st0("ys"), st1("cap"), st1("cas"), st1("gvs"), st1("ccp"), st1("ccs"),
            st1("latp"), st1("krp"), st1("lats"), st1("krs"))
    return tuple(np.ascontiguousarray(o, dtype=np.float32) for o in outs), res


def kernel(**inputs):
    cfg = Cfg()
    outs, _ = run(cfg, inputs)
    return outs
```

```python
import math
from contextlib import ExitStack

import numpy as np
import concourse.bass as bass
import concourse.mybir as mybir
from concourse.bass_utils import run_bass_kernel_spmd

F32 = mybir.dt.float32
BF16 = mybir.dt.bfloat16
I32 = mybir.dt.int32
AF = mybir.ActivationFunctionType
ALU = mybir.AluOpType

D = 2048
DA = 1024
DFF = 8192
NH = 16
QR = 512
KVR = 512
ROPE = 64
EPS = 1e-6
CONVA = 31
TT = 256
NSUB = TT // 128
QSCALE = 1.0 / math.sqrt(192.0)
TWO_PI = 2.0 * math.pi


class Cfg:
    def __init__(self, SEQ=4096, PAST=4096, DEPTH=4, DEC=32, NCORES=8):
        self.SEQ, self.PAST, self.DEPTH, self.DEC, self.NCORES = SEQ, PAST, DEPTH, DEC, NCORES
        self.NE = (DEPTH + 1) // 2
        self.NO = DEPTH // 2


class _Op:
    __slots__ = ("eng", "fn", "deps", "key", "ndma", "sig", "ordv", "idx")


class Prog:
    ENGS = ("pe", "act", "dve", "pool", "sp")

    def __init__(self):
        self.ops = []
        self.res = {}
        self.key_count = {}
        self.key_last = {}

    def add(self, eng, fn, r=(), w=(), key=None, ndma=1):
        op = _Op()
        op.eng, op.fn, op.key, op.ndma = eng, fn, key, ndma
        op.sig = key is not None
        op.idx = len(self.ops)
        deps = set()
        isdma = key is not None
        for name in r:
            st = self.res.get(name)
            if st is None:
                st = self.res[name] = [None, {}, []]
            if st[0] is not None:
                deps.add((st[0], 0))
        for name in w:
            st = self.res.get(name)
            if st is None:
                st = self.res[name] = [None, {}, []]
            if st[0] is not None:
                deps.add((st[0], 1))
            for ri in st[1].values():
                deps.add((ri, 2))
            for ri in st[2]:
                deps.add((ri, 2))
        for name in r:
            st = self.res[name]
            if isdma:
                st[2].append(op.idx)
            else:
                st[1][eng] = op.idx
        for name in w:
            st = self.res[name]
            st[0] = op.idx
            st[1] = {}
            st[2] = []
        if isdma:
            prev = self.key_last.get(key)
            if prev is not None:
                deps.add((prev, 1))
            self.key_last[key] = op.idx
            c = self.key_count.get(key, 0) + ndma
            self.key_count[key] = c
            op.ordv = 16 * c
        op.deps = [d for d in deps if d[0] != op.idx]
        self.ops.append(op)
        return op

    def emit(self, nc, es):
        ops = self.ops
        need = []
        for op in ops:
            lst = []
            for (di, kind) in op.deps:
                p = ops[di]
                if p.key is None and op.key is None and p.eng == op.eng:
                    if op.eng == "pe" or kind == 2:
                        continue
                lst.append(di)
                if p.key is None:
                    p.sig = True
            need.append(lst)
        cnt = {e: 0 for e in self.ENGS}
        for op in ops:
            if op.key is None and op.sig:
                cnt[op.eng] += 1
                op.ordv = cnt[op.eng]
        esem = {e: es.enter_context(nc.semaphore("tl_" + e)) for e in self.ENGS}
        ksem = {}
        for i, k in enumerate(self.key_count):
            ksem[k] = es.enter_context(nc.semaphore("dk%d" % i))
        per = {e: [] for e in self.ENGS}
        for op, lst in zip(ops, need):
            per[op.eng].append((op, lst))
        blk = es.enter_context(nc.Block())

        def run(ename, eobj):
            waited = {}
            for op, lst in per[ename]:
                req = {}
                for di in lst:
                    p = ops[di]
                    s = ksem[p.key] if p.key is not None else esem[p.eng]
                    sid = id(s)
                    if sid not in req or req[sid][1] < p.ordv:
                        req[sid] = (s, p.ordv)
                for sid, (s, v) in req.items():
                    if waited.get(sid, 0) < v:
                        eobj.wait_ge(s, v)
                        waited[sid] = v
                ins = op.fn(eobj)
                if op.key is not None:
                    if not isinstance(ins, (list, tuple)):
                        ins = [ins]
                    assert len(ins) == op.ndma, (len(ins), op.ndma)
                    for i_ in ins:
                        i_.then_inc(ksem[op.key], 16)
                elif op.sig:
                    ins.then_inc(esem[ename], 1)
            if ename == "sp":
                for k, c in self.key_count.items():
                    eobj.wait_ge(ksem[k], 16 * c)

        @blk.tensor
        def _(e):
            run("pe", e)

        @blk.scalar
        def _(e):
            run("act", e)

        @blk.vector
        def _(e):
            run("dve", e)

        @blk.gpsimd
        def _(e):
            run("pool", e)

        @blk.sync
        def _(e):
            run("sp", e)


class SeqCtx:
    pass


def build(cfg):
    nc = bass.Bass("TRN2", target_bir_lowering=False)
    NE, NO, DEPTH = cfg.NE, cfg.NO, cfg.DEPTH
    SEQ, PAST, DEC = cfg.SEQ, cfg.PAST, cfg.DEC
    NT = SEQ // TT
    NPT = PAST // TT
    TS = PAST + TT

    def din(name, shape, dt=F32):
        return nc.dram_tensor(name, list(shape), dt, kind="ExternalInput").ap()

    def dout(name, shape):
        return nc.dram_tensor(name, list(shape), F32, kind="ExternalOutput").ap()

    def dint(name, shape, dt):
        return nc.dram_tensor(name, list(shape), dt, kind="Internal").ap()

    xp = din("xp", [SEQ, D])
    xs = din("xs", [DEC, D])
    sca = din("sca", [NE, 30, DA])
    scc = din("scc", [max(NO, 1), 2, DA])
    ckv_in = din("ckv", [max(NO, 1), PAST, KVR])
    ckr_in = din("ckr", [max(NO, 1), PAST, ROPE])
    nrm = din("nrm", [2 * DEPTH + 1, D])
    w_in_even = din("w_in_even", [NE, D, 4096])
    w_out_even = din("w_out_even", [NE, D, D])
    w_in_odd = din("w_in_odd", [max(NO, 1), D, 4160])
    w_uq = din("w_uq", [max(NO, 1), QR, 3072])
    w_ukv = din("w_ukv", [max(NO, 1), KVR, 4096])
    w_out_odd = din("w_out_odd", [max(NO, 1), 3072, D])
    w_up = din("w_ffn_up", [DEPTH, D, DFF])
    w_down = din("w_ffn_down", [DEPTH, DFF, D])
    pcol_in = din("pcol", [128, 320 * NE + 24 * max(NO, 1)])
    lnv_in = din("lnv", [NE, 2, DA])
    qkg_in = din("qkg", [max(NO, 1), 2, 512])
    bsp_in = din("bsp", [NE, 1024])
    wsp_in = din("wsp", [NE, 8, 128, 128])
    invf_in = din("c_invf", [64, 1])
    mask_in = din("c_mask", [128, NSUB * TT])

    yp = dout("yp", [SEQ, D])
    ys = dout("ys", [DEC, D])
    cap = dout("cap", [NE, 30, DA])
    cas = dout("cas", [NE, 30, DA])
    gvs = dout("gvs", [NE, DEC, DA])
    ccp = dout("ccp", [max(NO, 1), 2, DA])
    ccs = dout("ccs", [max(NO, 1), 2, DA])
    latp = dout("latp", [max(NO, 1), SEQ, KVR])
    krp = dout("krp", [max(NO, 1), SEQ, ROPE])
    lats = dout("lats", [max(NO, 1), DEC, KVR])
    krs = dout("krs", [max(NO, 1), DEC, ROPE])

    KTd = {"p": dint("KT_p", [max(NO, 1), NH, 128, SEQ], BF16), "s": dint("KT_s", [max(NO, 1), NH, 128, TS], BF16)}
    Vd = {"p": dint("V_p", [max(NO, 1), NH, SEQ, 128], BF16), "s": dint("V_s", [max(NO, 1), NH, TS, 128], BF16)}
    KRd = {"p": dint("KR_p", [max(NO, 1), 64, SEQ], BF16), "s": dint("KR_s", [max(NO, 1), 64, TS], BF16)}

    P = Prog()
    es = ExitStack()

    def sb(name, shape, dt):
        return es.enter_context(nc.sbuf_tensor("s_" + name, list(shape), dt))

    X = sb("X", [128, NSUB, D], F32)
    gbc = sb("gbc", [128, D], F32)
    xn_tm = [sb("xn_tm%d" % i, [128, D], BF16) for i in range(2)]
    xnT = sb("xnT", [128, 16, TT], BF16)
    NWR = 3
    wr = [sb("wr%d" % i, [128, 16, 512], BF16) for i in range(NWR)]
    big = sb("big", [128, 24, TT], BF16)
    rscr = [sb("rscr%d" % i, [128, TT], BF16) for i in range(2)]
    aT = sb("aT", [128, 8, 30 + TT], F32)
    sg = sb("sg", [128, 4, TT], F32)
    acc = [sb("acc%d" % i, [128, TT], F32) for i in range(2)]
    convo = sb("convo", [128, 8, TT], BF16)
    sqb = [sb("sqb%d" % i, [128, TT], BF16) for i in range(2)]
    uT = sb("uT", [128, 8, TT], BF16)
    v_tm = sb("v_tm", [128, NSUB, DA], BF16)
    gv = [sb("gv%d" % i, [128, DA], F32) for i in range(2)]
    f1 = sb("f1", [128, TT], F32)
    f2 = sb("f2", [128, TT], F32)
    f3 = sb("f3", [128, TT], F32)
    stage = sb("stage", [128, DA], F32)
    ss = sb("ss", [128, 4], F32)
    rstd = sb("rstd", [128, 4], F32)
    bst = sb("bst", [128, 12], F32)
    mv = sb("mv", [128, 2], F32)
    lrs = sb("lrs", [128, 1], F32)
    pcol = sb("pcol", [128, 320 * NE + 24 * max(NO, 1)], F32)
    ones_f = sb("ones_f", [1, 128], F32)
    WsT = sb("WsT", [128, NE, 8, 128], BF16)
    haloA = sb("haloA", [128, NE, 8, 30], F32)
    haloC = sb("haloC", [128, max(NO, 1), 8, 2], F32)
    idf = sb("idf", [128, 128], F32)
    idb = sb("idb", [128, 128], BF16)
    ones_b = sb("ones_b", [128, 128], BF16)
    mask = sb("mask", [128, NSUB, TT], BF16)
    invf = sb("invf", [64, 1], F32)
    sgn = sb("sgn", [64, 1], F32)
    iota_f = sb("iota_f", [64, TT], F32)
    cosT = sb("cosT", [64, TT], F32)
    sinS = sb("sinS", [64, TT], F32)
    cosQ = sb("cosQ", [64, TT], F32)
    sinQ = sb("sinQ", [64, TT], F32)
    ri = sb("ri", [64, TT], I32)
    zqnT = sb("zqnT", [128, 4, TT], BF16)
    ckvT = sb("ckvT", [128, 4, TT], BF16)
    ztm = [sb("ztm%d" % i, [128, 512], F32) for i in range(2)]
    ztb = [sb("ztb%d" % i, [128, 512], BF16) for i in range(2)]
    krT_all = sb("krT_all", [64, TS], BF16)
    krf = sb("krf", [64, TT], F32)
    krb = sb("krb", [64, TT], BF16)
    qn = [sb("qn%d" % i, [128, TT], BF16) for i in range(2)]
    qrb = [sb("qrb%d" % i, [64, TT], BF16) for i in range(2)]
    NKV = 6
    kvK = [sb("kvK%d" % i, [128, TT], BF16) for i in range(NKV)]
    kvV = [sb("kvV%d" % i, [128, NSUB, 128], BF16) for i in range(NKV)]
    NPT_ = 3
    pt = [sb("pt%d" % i, [128, TT], BF16) for i in range(NPT_)]
    NST = 3
    kst = [sb("kst%d" % i, [128, TT], BF16) for i in range(NST)]
    vst = [sb("vst%d" % i, [128, 256], BF16) for i in range(NST)]

    ps = [es.enter_context(nc.psum_tensor("ps%d" % i, [128, 512], F32)) for i in range(8)]
    psb = [p_[:].bitcast(BF16) for p_ in ps]

    cnt = {"fm": 0, "tr": 0, "wr": 0, "rs": 0, "acc": 0, "sq": 0, "gv": 0, "xn": 0, "zt": 0, "q": 0,
           "kv": 0, "pt": 0, "st": 0, "ob": 0}

    def rr(name, n):
        v = cnt[name] % n
        cnt[name] += 1
        return v

    def fm_bank():
        return 4 + rr("fm", 2)

    def tr_bank():
        return 6 + rr("tr", 2)

    def PS(b):
        return ("ps", b)

    def dma(q, out, in_, r, w, key):
        P.add(q, lambda e, o=out, i=in_: e.dma_start(out=o, in_=i), r=r, w=w, key=key)

    LOOKAHEAD = 6
    pst = {"order": [], "index": {}, "ptr": 0, "dry": True, "wsc": None}

    def _emit_conv(i):
        (srcs, nk) = pst["order"][i]
        view = pst["wsc"][i // 64][i % 64].rearrange("p (k c) -> p k c", c=512)

        def fn(e, srcs=srcs, nk=nk, view=view):
            out = []
            for (s_, c0, ncl) in srcs:
                out.append(e.dma_start(out=view[:, 0:nk, c0:c0 + ncl], in_=s_.rearrange("(k p) c -> p k c", p=128)))
            return out
        P.add("pool", fn, r=(), w=[("wsc", i)], key=("cv", i % (LOOKAHEAD + 2)), ndma=len(srcs))

    def wpiece(src, nk, ncol, dst_col=0, slot=None, extra=None, pid=None):
        if slot is None:
            slot = rr("wr", NWR)
        srcs = [(src, dst_col, ncol)] + (extra or [])
        assert pid is not None
        if pst["dry"]:
            if pid not in pst["index"]:
                pst["index"][pid] = len(pst["order"])
                pst["order"].append((srcs, nk))
            return slot
        i = pst["index"][pid]
        tgt = min(i + LOOKAHEAD, len(pst["order"]) - 1)
        while pst["ptr"] <= tgt:
            _emit_conv(pst["ptr"])
            pst["ptr"] += 1
        view = pst["wsc"][i // 64][i % 64].rearrange("p (k c) -> p k c", c=512)
        P.add("pool", lambda e, slot=slot, nk=nk, view=view: e.dma_start(out=wr[slot][:, 0:nk, :], in_=view[:, 0:nk, :]),
              r=[("wsc", i)], w=[("wr", slot)], key=("wr", slot))
        return slot

    def mm(out, lhsT, rhs, start, stop, r, w):
        P.add("pe", lambda e, o=out, l=lhsT, rh=rhs, s=start, t=stop: e.matmul(o, lhsT=l, rhs=rh, start=s, stop=t),
              r=r, w=w)

    def tp(out, in_, ident, r, w):
        P.add("pe", lambda e, o=out, i=in_, d=ident: e.transpose(out=o, in_=i, identity=d), r=r, w=w)

    def act(out, in_, func, r, w, **kw):
        P.add("act", lambda e, o=out, i=in_, f=func, kw=kw: e.activation(out=o, in_=i, func=f, **kw), r=r, w=w)

    def tt(out, in0, in1, op, r, w, eng="dve"):
        P.add(eng, lambda e, o=out, a=in0, b=in1, p_=op: e.tensor_tensor(out=o, in0=a, in1=b, op=p_), r=r, w=w)

    def ts(out, in0, s1, s2, op0, op1, r, w, eng="dve"):
        if s2 is None:
            P.add(eng, lambda e, o=out, a=in0, s1=s1, p0=op0: e.tensor_scalar(out=o, in0=a, scalar1=s1, scalar2=None, op0=p0),
                  r=r, w=w)
        else:
            P.add(eng, lambda e, o=out, a=in0, s1=s1, s2=s2, p0=op0, p1=op1:
                  e.tensor_scalar(out=o, in0=a, scalar1=s1, scalar2=s2, op0=p0, op1=p1), r=r, w=w)

    def stt(out, in0, sc, in1, op0, op1, r, w, eng="dve"):
        P.add(eng, lambda e, o=out, a=in0, s=sc, b=in1, p0=op0, p1=op1:
              e.scalar_tensor_tensor(out=o, in0=a, scalar=s, in1=b, op0=p0, op1=p1), r=r, w=w)

    def cp(out, in_, r, w, eng="dve"):
        P.add(eng, lambda e, o=out, i=in_: e.tensor_copy(out=o, in_=i), r=r, w=w)

    def ms(ap, val, w, eng="dve"):
        P.add(eng, lambda e, a=ap, v=val: e.memset(a, v), r=(), w=w)

    def recip(out, in_, r, w):
        P.add("dve", lambda e, o=out, i=in_: e.reciprocal(out=o, in_=i), r=r, w=w)

    def rstd_chain(dst, src, n_inv, pp, ncols, r, w):
        ts(dst[0:pp, 0:ncols], src, n_inv, EPS, ALU.mult, ALU.add, r=r, w=w)
        act(dst[0:pp, 0:ncols], dst[0:pp, 0:ncols], AF.Sqrt, r=w, w=w)
        recip(dst[0:pp, 0:ncols], dst[0:pp, 0:ncols], r=w, w=w)

    def setup():
        dma("sp", pcol[:], pcol_in[:, :], r=(), w=["pcol"], key="ld_pcol")
        dma("sp", invf[:], invf_in[:, :], r=(), w=["invf"], key="ld_misc")
        dma("sp", f1[:], mask_in[:, 0:TT], r=(), w=["f1"], key="ld_misc")
        ms(idf[:], 0.0, w=["idf"], eng="pool")
        P.add("pool", lambda e: e.affine_select(out=idf[:], in_=idf[:], pattern=[[-1, 128]], compare_op=ALU.not_equal,
                                                fill=1.0, base=0, channel_multiplier=1), r=["idf"], w=["idf"])
        P.add("pool", lambda e: e.iota(ri[:], pattern=[[1, TT]], base=0, channel_multiplier=0), r=(), w=["ri"])
        cp(idb[:], idf[:], r=["idf"], w=["idb"])
        cp(iota_f[:], ri[:], r=["ri"], w=["iota_f"])
        ms(ones_b[:], 1.0, w=["ones_b"])
        ms(ones_f[:], 1.0, w=["ones_f"])
        ms(sgn[0:32, :], -1.0, w=["sgn"])
        ms(sgn[32:64, :], 1.0, w=["sgn"])
        ms(haloA[:], 0.0, w=["haloA"])
        ms(haloC[:], 0.0, w=["haloC"])
        for kb in range(NSUB):
            if kb > 0:
                dma("sp", f1[:], mask_in[:, kb * TT:(kb + 1) * TT], r=(), w=["f1"], key="ld_misc")
            cp(mask[:, kb, :], f1[:], r=["f1"], w=["mask"])
        for p_ in range(NE):
            for g in range(8):
                dma("sp", f2[:, 0:128], wsp_in[p_, g], r=(), w=["f2"], key="ld_misc")
                P.add("pool", lambda e: e.affine_select(out=f2[:, 0:128], in_=f2[:, 0:128], pattern=[[-1, 128]],
                                                        compare_op=ALU.is_ge, fill=0.0, base=0, channel_multiplier=1),
                      r=["f2"], w=["f2"])
                b = tr_bank()
                tp(ps[b][:, 0:128], f2[:, 0:128], idf[:], r=["f2", "idf"], w=[PS(b)])
                act(WsT[:, p_, g, :], ps[b][:, 0:128], AF.Copy, r=[PS(b)], w=["WsT"])

    def pc_even(p_):
        base = 320 * p_
        return dict(cw=base, cb=base + 248, lg=base + 256, lb=base + 264)

    def pc_odd(p_):
        return 320 * NE + 24 * p_

    def rmsnorm_to_xnT(c, gidx):
        pp, nsub, N = c.pp, c.nsub, c.N
        dma("sp", gbc[:], nrm[gidx].partition_broadcast(128), r=(), w=["gbc"], key="ld_gbc")
        ms(ss[:], 0.0, w=["ss"])
        for s in range(nsub):
            i = rr("xn", 2)
            act(xn_tm[i][0:pp, :], X[0:pp, s, :], AF.Square, r=[("X", s), "ss"], w=[("xn_tm", i), "ss"],
                accum_out=ss[0:pp, s:s + 1])
        rstd_chain(rstd, ss[0:pp, 0:nsub], 1.0 / D, pp, nsub, r=["ss"], w=["rstd"])
        for s in range(nsub):
            i = rr("xn", 2)
            stt(xn_tm[i][0:pp, :], X[0:pp, s, :], rstd[0:pp, s:s + 1], gbc[0:pp, :], ALU.mult, ALU.mult,
                r=[("X", s), "rstd", "gbc"], w=[("xn_tm", i)])
            for half in range(2):
                b = tr_bank()
                for k in range(8):
                    kk = half * 8 + k
                    tp(psb[b][:, k * pp:(k + 1) * pp], xn_tm[i][0:pp, kk * 128:(kk + 1) * 128], idb[0:pp, 0:pp],
                       r=[("xn_tm", i), "idb"], w=[PS(b)])
                act(xnT[:, half * 8:half * 8 + 8, s * 128:s * 128 + pp],
                    psb[b][:, 0:8 * pp].rearrange("p (k t) -> p k t", k=8), AF.Copy,
                    r=[PS(b)], w=[("xnT", s)])

    def xnT_res(c):
        return [("xnT", s) for s in range(c.nsub)]

    def fm_group(c, slot, nk, col, M, rhs_t, rhs_res, b, prow=0):
        N = c.N
        for k in range(nk):
            mm(ps[b][prow:prow + M, 0:N], wr[slot][:, k, col:col + M], rhs_t[:, k, 0:N], k == 0, k == nk - 1,
               r=[("wr", slot)] + rhs_res, w=[PS(b)])

    def residual_add_from(c, s, nb, b):
        pp = c.pp
        tt(X[0:pp, s, nb * 512:(nb + 1) * 512], X[0:pp, s, nb * 512:(nb + 1) * 512], ps[b][0:pp, :], ALU.add,
           r=[("X", s), PS(b)], w=[("X", s)])

    def out_proj(c, W, nkc, wname):
        pp, nsub = c.pp, c.nsub
        bigres = [("big", k) for k in range(nkc)]
        halves = [(0, nkc)] if nkc <= 16 else [(0, nkc // 2), (nkc // 2, nkc)]
        for nb in range(4):
            slots = []
            for (k0, k1) in halves:
                slots.append(wpiece(W[k0 * 128:k1 * 128, nb * 512:(nb + 1) * 512], k1 - k0, 512, pid=(wname, nb, k0)))
            for s in range(nsub):
                for hi, (k0, k1) in enumerate(halves):
                    for k in range(k0, k1):
                        mm(ps[s][0:pp, :], big[:, k, s * 128:s * 128 + pp], wr[slots[hi]][:, k - k0, :],
                           k == 0, k == nkc - 1, r=[("wr", slots[hi]), ("big", k)], w=[PS(s)])
                residual_add_from(c, s, nb, s)

    def ffn(c, l):
        pp, nsub, N = c.pp, c.nsub, c.N
        for q in range(4):
            for j in range(4):
                slot = wpiece(w_up[l, :, q * 2048 + j * 512: q * 2048 + (j + 1) * 512], 16, 512, pid=("up", l, q, j))
                for m in range(4):
                    hc = j * 4 + m
                    b = fm_bank()
                    fm_group(c, slot, 16, m * 128, 128, xnT, xnT_res(c), b)
                    i = rr("rs", 2)
                    act(rscr[i][:, 0:N], ps[b][:, 0:N], AF.Relu, r=[PS(b)], w=[("rscr", i)])
                    act(big[:, hc, 0:N], rscr[i][:, 0:N], AF.Square, r=[("rscr", i)], w=[("big", hc)])
            for nb in range(4):
                slot = wpiece(w_down[l, q * 2048:(q + 1) * 2048, nb * 512:(nb + 1) * 512], 16, 512, pid=("dn", l, q, nb))
                for s in range(nsub):
                    for k in range(16):
                        mm(ps[s][0:pp, :], big[:, k, s * 128:s * 128 + pp], wr[slot][:, k, :], k == 0, k == 15,
                           r=[("wr", slot), ("big", k)], w=[PS(s)])
                    residual_add_from(c, s, nb, s)

    def save_state_rows(c, src3, ncols_state, col0, out_ap):
        n = ncols_state
        for half in range(2):
            b = tr_bank()
            for k in range(4):
                ch = half * 4 + k
                tp(ps[b][0:n, k * 128:(k + 1) * 128], src3[:, ch, col0:col0 + n], idf[:], r=["aT", "idf"], w=[PS(b)])
            act(stage[0:n, half * 512:(half + 1) * 512], ps[b][0:n, :], AF.Copy, r=[PS(b)], w=["stage"])
        dma("sp", out_ap, stage[0:n, :], r=["stage"], w=(), key="st_stage")

    def load_state_rows(c, in_ap, n, dst3):
        dma("sp", stage[0:n, :], in_ap, r=(), w=["stage"], key="ld_stage")
        b = tr_bank()
        for ch in range(8):
            tp(ps[b][:, ch * n:(ch + 1) * n], stage[0:n, ch * 128:(ch + 1) * 128], idf[0:n, 0:n],
               r=["stage", "idf"], w=[PS(b)])
        act(dst3[:, :, 0:n], ps[b][:, 0:8 * n].rearrange("p (k t) -> p k t", k=8), AF.Copy, r=[PS(b)], w=["aT"])

    def even_mixer(c, p_, last):
        pp, nsub, N = c.pp, c.nsub, c.N
        W = w_in_even[p_]
        pc = pc_even(p_)
        H = 30
        if c.name == "s":
            load_state_rows(c, sca[p_], H, aT)
        else:
            cp(aT[:, :, 0:H], haloA[:, p_, :, :], r=["haloA"], w=["aT"])
        for grp in range(2):
            slot = wpiece(W[:, 1024 + grp * 512:1024 + (grp + 1) * 512], 16, 512, pid=("wie", p_, 2 + grp))
            for m in range(4):
                b = fm_bank()
                fm_group(c, slot, 16, m * 128, 128, xnT, xnT_res(c), b)
                act(sg[:, m, 0:N], ps[b][:, 0:N], AF.Sigmoid, r=[PS(b)], w=[("sg", m)])
            slot = wpiece(W[:, grp * 512:(grp + 1) * 512], 16, 512, pid=("wie", p_, grp))
            for m in range(4):
                ch = grp * 4 + m
                b = fm_bank()
                fm_group(c, slot, 16, m * 128, 128, xnT, xnT_res(c), b)
                tt(aT[:, ch, H:H + N], ps[b][:, 0:N], sg[:, m, 0:N], ALU.mult, r=[PS(b), ("sg", m)], w=["aT"])
        for grp in range(2):
            slot = wpiece(W[:, 2048 + grp * 512:2048 + (grp + 1) * 512], 16, 512, pid=("wie", p_, 4 + grp))
            for m in range(4):
                ch = grp * 4 + m
                b = fm_bank()
                fm_group(c, slot, 16, m * 128, 128, xnT, xnT_res(c), b)
                act(uT[:, ch, 0:N], ps[b][:, 0:N], AF.Gelu, r=[PS(b)], w=[("uT", ch)])
        vslots = [wpiece(W[:, 3072 + j * 512:3072 + (j + 1) * 512], 16, 512, pid=("wie", p_, 6 + j)) for j in range(2)]
        dma("sp", gbc[:], lnv_in[p_].rearrange("a b -> (a b)").partition_broadcast(128), r=(), w=["gbc"], key="ld_gbc")
        for s in range(nsub):
            gi = rr("gv", 2)
            for j in range(2):
                b = j + 2 * (s % 2)
                for k in range(16):
                    mm(ps[b][0:pp, :], xnT[:, k, s * 128:s * 128 + pp], wr[vslots[j]][:, k, :], k == 0, k == 15,
                       r=[("wr", vslots[j]), ("xnT", s)], w=[PS(b)])
                act(gv[gi][0:pp, j * 512:(j + 1) * 512], ps[b][0:pp, :], AF.Gelu, r=[PS(b)], w=[("gv", gi)])
            for j in range(2):
                P.add("dve", lambda e, gi=gi, j=j, pp=pp: e.bn_stats(out=bst[0:pp, j * 6:(j + 1) * 6], in_=gv[gi][0:pp, j * 512:(j + 1) * 512]),
                      r=[("gv", gi)], w=["bst"])
            P.add("dve", lambda e, pp=pp: e.bn_aggr(out=mv[0:pp, :], in_=bst[0:pp, :]), r=["bst"], w=["mv"])
            rstd_chain(lrs, mv[0:pp, 1:2], 1.0, pp, 1, r=["mv"], w=["lrs"])
            stt(gv[gi][0:pp, :], gv[gi][0:pp, :], mv[0:pp, 0:1], gbc[0:pp, 0:DA], ALU.subtract, ALU.mult,
                r=[("gv", gi), "mv", "gbc"], w=[("gv", gi)])
            stt(gv[gi][0:pp, :], gv[gi][0:pp, :], lrs[0:pp, 0:1], gbc[0:pp, DA:2 * DA], ALU.mult, ALU.add,
                r=[("gv", gi), "lrs", "gbc"], w=[("gv", gi)])
            act(v_tm[0:pp, s, :], gv[gi][0:pp, :], AF.Copy, r=[("gv", gi)], w=[("v_tm", s)])
            if c.name == "s":
                dma("sp", gvs[p_], gv[gi][0:pp, :], r=[("gv", gi)], w=(), key=("st_gv", gi))
        SUMB, SQB = 0, 1
        for ch in range(8):
            ai = rr("acc", 2)
            a_ = acc[ai]
            ts(a_[:, 0:N], aT[:, ch, 0:N], pcol[:, pc["cw"] + ch * 31:pc["cw"] + ch * 31 + 1],
               pcol[:, pc["cb"] + ch:pc["cb"] + ch + 1], ALU.mult, ALU.add, r=["aT", "pcol"], w=[("acc", ai)])
            for j in range(1, CONVA):
                stt(a_[:, 0:N], aT[:, ch, j:j + N], pcol[:, pc["cw"] + ch * 31 + j:pc["cw"] + ch * 31 + j + 1], a_[:, 0:N],
                    ALU.mult, ALU.add, r=["aT", "pcol", ("acc", ai)], w=[("acc", ai)])
            act(convo[:, ch, 0:N], a_[:, 0:N], AF.Copy, r=[("acc", ai)], w=[("convo", ch)])
            si = rr("sq", 2)
            act(sqb[si][:, 0:N], a_[:, 0:N], AF.Square, r=[("acc", ai)], w=[("sqb", si)])
            mm(ps[SUMB][:, 0:N], ones_b[:], convo[:, ch, 0:N], ch == 0, ch == 7, r=["ones_b", ("convo", ch)], w=[PS(SUMB)])
            mm(ps[SQB][:, 0:N], ones_b[:], sqb[si][:, 0:N], ch == 0, ch == 7, r=["ones_b", ("sqb", si)], w=[PS(SQB)])
        if c.name == "s":
            save_state_rows(c, aT, H, N, cas[p_])
        else:
            if last:
                save_state_rows(c, aT, H, N, cap[p_])
            cp(haloA[:, p_, :, :], aT[:, :, N:N + H], r=["aT"], w=["haloA"])
        ts(f1[:, 0:N], ps[SUMB][:, 0:N], 1.0 / DA, None, ALU.mult, None, r=[PS(SUMB)], w=["f1"])
        tt(f3[:, 0:N], f1[:, 0:N], f1[:, 0:N], ALU.mult, r=["f1"], w=["f3"])
        stt(f2[:, 0:N], ps[SQB][:, 0:N], 1.0 / DA, f3[:, 0:N], ALU.mult, ALU.subtract, r=[PS(SQB), "f3"], w=["f2"])
        ts(f2[:, 0:N], f2[:, 0:N], 0.0, EPS, ALU.max, ALU.add, r=["f2"], w=["f2"])
        act(f2[:, 0:N], f2[:, 0:N], AF.Sqrt, r=["f2"], w=["f2"])
        recip(f2[:, 0:N], f2[:, 0:N], r=["f2"], w=["f2"])
        for ch in range(8):
            ai = rr("acc", 2)
            a_ = acc[ai]
            tt(a_[:, 0:N], convo[:, ch, 0:N], f1[:, 0:N], ALU.subtract, r=[("convo", ch), "f1"], w=[("acc", ai)])
            tt(a_[:, 0:N], a_[:, 0:N], f2[:, 0:N], ALU.mult, r=[("acc", ai), "f2"], w=[("acc", ai)])
            act(big[:, ch, 0:N], a_[:, 0:N], AF.Silu, r=[("acc", ai), "pcol"], w=[("big", ch)],
                scale=pcol[:, pc["lg"] + ch:pc["lg"] + ch + 1], bias=pcol[:, pc["lb"] + ch:pc["lb"] + ch + 1])
        L = pp
        dma("sp", stage[0:1, :], bsp_in[p_:p_ + 1, :], r=(), w=["stage"], key="ld_stage")
        for g in range(8):
            b = fm_bank()
            for s in range(nsub):
                mm(ps[b][:, s * 128:s * 128 + L], v_tm[0:L, s, g * 128:(g + 1) * 128], WsT[0:L, p_, g, 0:L], True, False,
                   r=[("v_tm", s), "WsT"], w=[PS(b)])
                mm(ps[b][:, s * 128:s * 128 + L], ones_f[0:1, :], stage[0:1, g * 128:g * 128 + L], False, True,
                   r=["ones_f", "stage"], w=[PS(b)])
            tt(big[:, 8 + g, 0:N], ps[b][:, 0:N], uT[:, g, 0:N], ALU.mult, r=[PS(b), ("uT", g)], w=[("big", 8 + g)])
        out_proj(c, w_out_even[p_], 16, ("woe", p_))

    def rope_tables(c):
        N = c.N
        pos0 = float(c.pos0)
        ts(krf[:, 0:N], iota_f[:, 0:N], pos0, None, ALU.add, None, r=["iota_f"], w=["krf"])
        ts(krf[:, 0:N], krf[:, 0:N], invf[:, 0:1], None, ALU.mult, None, r=["krf", "invf"], w=["krf"])
        for (dst, phase) in ((cosT, 0.25), (sinS, 0.0)):
            nm = "cosT" if dst is cosT else "sinS"
            ts(f1[0:64, 0:N], krf[:, 0:N], 1.0 / TWO_PI, phase, ALU.mult, ALU.add, r=["krf"], w=["f1"])
            cp(ri[:, 0:N], f1[0:64, 0:N], r=["f1"], w=["ri"])
            cp(f2[0:64, 0:N], ri[:, 0:N], r=["ri"], w=["f2"])
            tt(f1[0:64, 0:N], f1[0:64, 0:N], f2[0:64, 0:N], ALU.subtract, r=["f1", "f2"], w=["f1"])
            stt(f2[0:64, 0:N], f1[0:64, 0:N], 0.5, f1[0:64, 0:N], ALU.is_gt, ALU.subtract, r=["f1"], w=["f2"])
            stt(f1[0:64, 0:N], f1[0:64, 0:N], -0.5, f2[0:64, 0:N], ALU.is_lt, ALU.subtract, r=["f1", "f2"], w=["f1"])
            act(dst[:, 0:N], f1[0:64, 0:N], AF.Sin, r=["f1"], w=[nm], scale=TWO_PI)
        ts(sinS[:, 0:N], sinS[:, 0:N], sgn[:, 0:1], None, ALU.mult, None, r=["sinS", "sgn"], w=["sinS"])
        ts(cosQ[:, 0:N], cosT[:, 0:N], QSCALE, None, ALU.mult, None, r=["cosT"], w=["cosQ"])
        ts(sinQ[:, 0:N], sinS[:, 0:N], QSCALE, None, ALU.mult, None, r=["sinS"], w=["sinQ"])

    def kv_up(c_name, p_, srcT, src_res, pos0, N, pp, nsub):
        Wk = w_ukv[p_]
        for pi in range(8):
            slot = wpiece(Wk[:, pi * 512:(pi + 1) * 512], 4, 512, pid=("ukv", p_, pi))
            for hh in range(2):
                h = 2 * pi + hh
                b = fm_bank()
                for k in range(4):
                    mm(ps[b][:, 0:N], wr[slot][:, k, hh * 256:hh * 256 + 128], srcT[:, k, 0:N], k == 0, k == 3,
                       r=[("wr", slot)] + src_res, w=[PS(b)])
                si = rr("st", NST)
                act(kst[si][:, 0:N], ps[b][:, 0:N], AF.Copy, r=[PS(b)], w=[("kst", si)])
                dma("sp", KTd[c_name][p_, h, :, pos0:pos0 + N], kst[si][:, 0:N], r=[("kst", si)],
                    w=[("KT", c_name, p_, h, pos0 // TT)], key=("st_k", si))
            for s in range(nsub):
                b = s
                for k in range(4):
                    mm(ps[b][0:pp, 0:256].rearrange("p (h d) -> p h d", h=2), srcT[:, k, s * 128:s * 128 + pp],
                       wr[slot][:, k, :].rearrange("p (h t d) -> p h t d", h=2, t=2)[:, :, 1, :],
                       k == 0, k == 3, r=[("wr", slot)] + src_res, w=[PS(b)])
                si = rr("st", NST)
                act(vst[si][0:pp, :], ps[b][0:pp, 0:256], AF.Copy, r=[PS(b)], w=[("vst", si)])
                for hh in range(2):
                    h = 2 * pi + hh
                    dma("sp", Vd[c_name][p_, h, pos0 + s * 128:pos0 + s * 128 + pp, :], vst[si][0:pp, hh * 128:(hh + 1) * 128],
                        r=[("vst", si)], w=[("V", c_name, p_, h, pos0 // TT, s)], key=("st_v", si, hh))

    def odd_mixer(c, p_, last):
        pp, nsub, N = c.pp, c.nsub, c.N
        W = w_in_odd[p_]
        pco = pc_odd(p_)
        H = 2
        rope_tables(c)
        if c.name == "s":
            load_state_rows(c, scc[p_], H, aT)
        else:
            cp(aT[:, :, 0:H], haloC[:, p_, :, :], r=["haloC"], w=["aT"])
        for grp in range(2):
            slot = wpiece(W[:, 1024 + grp * 512:1024 + (grp + 1) * 512], 16, 512, pid=("wio", p_, 2 + grp))
            for m in range(4):
                b = fm_bank()
                fm_group(c, slot, 16, m * 128, 128, xnT, xnT_res(c), b)
                act(sg[:, m, 0:N], ps[b][:, 0:N], AF.Copy, r=[PS(b)], w=[("sg", m)])
            slot = wpiece(W[:, 2048 + grp * 512:2048 + (grp + 1) * 512], 16, 512, pid=("wio", p_, 4 + grp))
            for m in range(4):
                ch = grp * 4 + m
                b = fm_bank()
                fm_group(c, slot, 16, m * 128, 128, xnT, xnT_res(c), b)
                tt(aT[:, ch, H:H + N], ps[b][:, 0:N], sg[:, m, 0:N], ALU.mult, r=[PS(b), ("sg", m)], w=["aT"])
        for ch in range(8):
            ai = rr("acc", 2)
            a_ = acc[ai]
            ts(a_[:, 0:N], aT[:, ch, 0:N], pcol[:, pco + ch * 3:pco + ch * 3 + 1], None, ALU.mult, None,
               r=["aT", "pcol"], w=[("acc", ai)])
            stt(a_[:, 0:N], aT[:, ch, 1:1 + N], pcol[:, pco + ch * 3 + 1:pco + ch * 3 + 2], a_[:, 0:N], ALU.mult, ALU.add,
                r=["aT", "pcol", ("acc", ai)], w=[("acc", ai)])
            stt(convo[:, ch, 0:N], aT[:, ch, 2:2 + N], pcol[:, pco + ch * 3 + 2:pco + ch * 3 + 3], a_[:, 0:N], ALU.mult, ALU.add,
                r=["aT", "pcol", ("acc", ai)], w=[("convo", ch)])
        if c.name == "s":
            save_state_rows(c, aT, H, N, ccs[p_])
        else:
            if last:
                save_state_rows(c, aT, H, N, ccp[p_])
            cp(haloC[:, p_, :, :], aT[:, :, N:N + H], r=["aT"], w=["haloC"])
        for grp in range(2):
            slot = wpiece(W[:, grp * 512:(grp + 1) * 512], 16, 512, pid=("wio", p_, grp))
            for m in range(4):
                ch = grp * 4 + m
                b = fm_bank()
                fm_group(c, slot, 16, m * 128, 128, xnT, xnT_res(c), b)
                tt(big[:, ch, 0:N], ps[b][:, 0:N], convo[:, ch, 0:N], ALU.mult, r=[PS(b), ("convo", ch)], w=[("big", ch)])
        dma("sp", gbc[:, 0:1024], qkg_in[p_].rearrange("a b -> (a b)").partition_broadcast(128), r=(), w=["gbc"], key="ld_gbc")
        for which in range(2):
            slot = wpiece(W[:, 3072 + which * 512:3072 + (which + 1) * 512], 16, 512, pid=("wio", p_, 6 + which))
            dstT = zqnT if which == 0 else ckvT
            dname = "zqnT" if which == 0 else "ckvT"
            ms(ss[:], 0.0, w=["ss"])
            for s in range(nsub):
                b = s
                for k in range(16):
                    mm(ps[b][0:pp, :], xnT[:, k, s * 128:s * 128 + pp], wr[slot][:, k, :], k == 0, k == 15,
                       r=[("wr", slot), ("xnT", s)], w=[PS(b)])
                zi = rr("zt", 2)
                act(ztm[zi][0:pp, :], ps[b][0:pp, :], AF.Square, r=[PS(b), "ss"], w=[("ztm", zi), "ss"],
                    accum_out=ss[0:pp, s:s + 1])
            rstd_chain(rstd, ss[0:pp, 0:nsub], 1.0 / 512, pp, nsub, r=["ss"], w=["rstd"])
            for s in range(nsub):
                b = s
                zi = rr("zt", 2)
                stt(ztm[zi][0:pp, :], ps[b][0:pp, :], rstd[0:pp, s:s + 1], gbc[0:pp, which * 512:(which + 1) * 512], ALU.mult, ALU.mult,
                    r=[PS(b), "rstd", "gbc"], w=[("ztm", zi)])
                if which == 1:
                    out_ap = (lats[p_] if c.name == "s" else latp[p_, c.pos0 + s * 128:c.pos0 + s * 128 + pp, :])
                    dma("sp", out_ap, ztm[zi][0:pp, :], r=[("ztm", zi)], w=(), key=("st_lat", zi))
                act(ztb[zi][0:pp, :], ztm[zi][0:pp, :], AF.Copy, r=[("ztm", zi)], w=[("ztb", zi)])
                tb = tr_bank()
                for k in range(4):
                    tp(psb[tb][:, k * pp:(k + 1) * pp], ztb[zi][0:pp, k * 128:(k + 1) * 128], idb[0:pp, 0:pp],
                       r=[("ztb", zi), "idb"], w=[PS(tb)])
                act(dstT[:, :, s * 128:s * 128 + pp], psb[tb][:, 0:4 * pp].rearrange("p (k t) -> p k t", k=4), AF.Copy,
                    r=[PS(tb)], w=[dname])
        slot = wpiece(W[:, 4096:4160], 16, 64, extra=[(W[:, 4128:4160], 64, 32), (W[:, 4096:4128], 96, 32)], pid=("wio", p_, 8))
        bA = fm_bank()
        fm_group(c, slot, 16, 0, 64, xnT, xnT_res(c), bA)
        bB = fm_bank()
        fm_group(c, slot, 16, 64, 64, xnT, xnT_res(c), bB)
        tt(f1[0:64, 0:N], ps[bA][0:64, 0:N], cosT[:, 0:N], ALU.mult, r=[PS(bA), "cosT"], w=["f1"])
        tt(f2[0:64, 0:N], ps[bB][0:64, 0:N], sinS[:, 0:N], ALU.mult, r=[PS(bB), "sinS"], w=["f2"])
        tt(krf[:, 0:N], f1[0:64, 0:N], f2[0:64, 0:N], ALU.add, r=["f1", "f2"], w=["krf"])
        act(krb[:, 0:N], krf[:, 0:N], AF.Copy, r=["krf"], w=["krb"])
        dma("sp", KRd[c.name][p_, :, c.pos0:c.pos0 + N], krb[:, 0:N], r=["krb"], w=[("KR", c.name, p_, c.pos0 // TT)], key="st_kr")
        tb = tr_bank()
        for s in range(nsub):
            tp(ps[tb][0:pp, s * 64:(s + 1) * 64], krf[:, s * 128:s * 128 + pp], idf[0:64, 0:64], r=["krf", "idf"], w=[PS(tb)])
        act(stage[0:pp, 0:nsub * 64], ps[tb][0:pp, 0:nsub * 64], AF.Copy, r=[PS(tb)], w=["stage"])
        if c.name == "s":
            dma("sp", krs[p_], stage[0:pp, 0:64], r=["stage"], w=(), key="st_stage")
        else:
            dma("sp", krp[p_, c.pos0:c.pos0 + N, :].rearrange("(s p) r -> p s r", p=128),
                stage[0:pp, 0:nsub * 64].rearrange("p (s r) -> p s r", s=nsub), r=["stage"], w=(), key="st_stage")
        kv_up(c.name, p_, ckvT, ["ckvT"], c.pos0, N, pp, nsub)
        nkeys = c.pos0 + N
        dma("sp", krT_all[:, 0:nkeys], KRd[c.name][p_, :, 0:nkeys], r=[("KR", c.name, p_, kt_) for kt_ in range(c.pos0 // TT + 1)], w=["krT_all"], key="ld_krT")
        Wq = w_uq[p_]
        nkt_full = c.pos0 // TT
        for h in range(NH):
            if h % 2 == 0:
                ex = []
                for hh in range(2):
                    o = hh * 256
                    src0 = (h + hh) * 192
                    if hh == 1:
                        ex.append((Wq[:, src0:src0 + 192], o, 192))
                    ex.append((Wq[:, src0 + 160:src0 + 192], o + 192, 32))
                    ex.append((Wq[:, src0 + 128:src0 + 160], o + 224, 32))
                qslot = wpiece(Wq[:, h * 192:h * 192 + 192], 4, 192, extra=ex, pid=("uq", p_, h))
            o = (h % 2) * 256
            qi = rr("q", 2)
            b = fm_bank()
            fm_group(c, qslot, 4, o, 128, zqnT, ["zqnT"], b)
            act(qn[qi][:, 0:N], ps[b][:, 0:N], AF.Identity, r=[PS(b)], w=[("qn", qi)], scale=QSCALE)
            bA = fm_bank()
            fm_group(c, qslot, 4, o + 128, 64, zqnT, ["zqnT"], bA)
            bB = fm_bank()
            fm_group(c, qslot, 4, o + 192, 64, zqnT, ["zqnT"], bB)
            tt(f1[0:64, 0:N], ps[bA][0:64, 0:N], cosQ[:, 0:N], ALU.mult, r=[PS(bA), "cosQ"], w=["f1"])
            tt(f2[0:64, 0:N], ps[bB][0:64, 0:N], sinQ[:, 0:N], ALU.mult, r=[PS(bB), "sinQ"], w=["f2"])
            tt(qrb[qi][:, 0:N], f1[0:64, 0:N], f2[0:64, 0:N], ALU.add, r=["f1", "f2"], w=[("qrb", qi)])
            blocks = []
            for kt in range(nkt_full):
                for kb in range(NSUB):
                    blocks.append((kt, kb, 128, False))
            if c.name == "p":
                for kb in range(NSUB):
                    blocks.append((nkt_full, kb, 128, True))
            else:
                blocks.append((nkt_full, 0, N, False))
            ob = rr("ob", 2)
            OB, RB = ob, 2 + ob
            slots = {}

            def load_kt(kt, h=h):
                ki = rr("kv", NKV)
                kn = TT if (c.name == "p" or kt < nkt_full) else N
                dma("sp", kvK[ki][:, 0:kn], KTd[c.name][p_, h, :, kt * TT:kt * TT + kn], r=[("KT", c.name, p_, h, kt)],
                    w=[("kvK", ki)], key=("ld_k", ki))
                if kn == TT:
                    dma("sp", kvV[ki][:, :, :], Vd[c.name][p_, h, kt * TT:(kt + 1) * TT, :].rearrange("(b p) d -> p b d", p=128),
                        r=[("V", c.name, p_, h, kt, s_) for s_ in range(NSUB)], w=[("kvV", ki)], key=("ld_v", ki))
                else:
                    dma("sp", kvV[ki][0:kn, 0, :], Vd[c.name][p_, h, kt * TT:kt * TT + kn, :],
                        r=[("V", c.name, p_, h, kt, 0)], w=[("kvV", ki)], key=("ld_v", ki))
                slots[kt] = ki

            sbanks = {}

            def score(ix):
                kt, kb, kp, diag = blocks[ix]
                if kt not in slots:
                    load_kt(kt)
                ki = slots[kt]
                b = fm_bank()
                sbanks[ix] = b
                mm(ps[b][0:kp, 0:N], kvK[ki][:, kb * 128:kb * 128 + kp], qn[qi][:, 0:N], True, False,
                   r=[("kvK", ki), ("qn", qi)], w=[PS(b)])
                mm(ps[b][0:kp, 0:N], krT_all[:, kt * TT + kb * 128:kt * TT + kb * 128 + kp], qrb[qi][:, 0:N], False, True,
                   r=["krT_all", ("qrb", qi)], w=[PS(b)])

            score(0)
            nblk = len(blocks)
            for ix in range(nblk):
                kt, kb, kp, diag = blocks[ix]
                if ix + 1 < nblk:
                    score(ix + 1)
                b = sbanks[ix]
                pi = rr("pt", NPT_)
                act(pt[pi][0:kp, 0:N], ps[b][0:kp, 0:N], AF.Exp, r=[PS(b)], w=[("pt", pi)])
                if diag:
                    tt(pt[pi][0:kp, 0:N], pt[pi][0:kp, 0:N], mask[0:kp, kb, 0:N], ALU.mult, r=[("pt", pi), "mask"], w=[("pt", pi)])
                ki = slots[kt]
                mm(ps[OB][:, 0:N], kvV[ki][0:kp, kb, :], pt[pi][0:kp, 0:N], ix == 0, ix == nblk - 1,
                   r=[("kvV", ki), ("pt", pi)], w=[PS(OB)])
                mm(ps[RB][:, 0:N], ones_b[0:kp, :], pt[pi][0:kp, 0:N], ix == 0, ix == nblk - 1,
                   r=["ones_b", ("pt", pi)], w=[PS(RB)])
            recip(f3[:, 0:N], ps[RB][:, 0:N], r=[PS(RB)], w=["f3"])
            tt(big[:, 8 + h, 0:N], ps[OB][:, 0:N], f3[:, 0:N], ALU.mult, r=[PS(OB), "f3"], w=[("big", 8 + h)])
        out_proj(c, w_out_odd[p_], 24, ("woo", p_))

    def sample_cache_prep(p_):
        cst = X[:, 0, 0:NSUB * 512].rearrange("p (s r) -> p s r", s=NSUB)
        ckrst = X[:, 1, 0:NSUB * 64].rearrange("p (s r) -> p s r", s=NSUB)
        for kt in range(NPT):
            dma("sp", cst, ckv_in[p_, kt * TT:(kt + 1) * TT, :].rearrange("(s p) r -> p s r", p=128),
                r=(), w=[("X", 0)], key=("ld_x", 0))
            for s in range(NSUB):
                act(ztb[0][:, :], cst[:, s, :], AF.Copy, r=[("X", 0)], w=[("ztb", 0)])
                tb = tr_bank()
                for k in range(4):
                    tp(psb[tb][:, k * 128:(k + 1) * 128], ztb[0][:, k * 128:(k + 1) * 128], idb[:], r=[("ztb", 0), "idb"], w=[PS(tb)])
                act(ckvT[:, :, s * 128:(s + 1) * 128], psb[tb][:, 0:512].rearrange("p (k t) -> p k t", k=4), AF.Copy,
                    r=[PS(tb)], w=["ckvT"])
            kv_up("s", p_, ckvT, ["ckvT"], kt * TT, TT, 128, NSUB)
            dma("sp", ckrst, ckr_in[p_, kt * TT:(kt + 1) * TT, :].rearrange("(s p) r -> p s r", p=128),
                r=(), w=[("X", 1)], key=("ld_x", 1))
            tb = tr_bank()
            for s in range(NSUB):
                tp(ps[tb][0:64, s * 128:(s + 1) * 128], ckrst[:, s, :], idf[:], r=[("X", 1), "idf"], w=[PS(tb)])
            act(krb[:, :], ps[tb][0:64, 0:TT], AF.Copy, r=[PS(tb)], w=["krb"])
            dma("sp", KRd["s"][p_, :, kt * TT:(kt + 1) * TT], krb[:, :], r=["krb"], w=[("KR", "s", p_, kt)], key="st_kr")

    def process_tile(c, last):
        pp, nsub, N = c.pp, c.nsub, c.N
        for s in range(nsub):
            dma("sp", X[0:pp, s, :], c.x_in[c.row0 + s * 128:c.row0 + s * 128 + pp, :], r=(), w=[("X", s)], key=("ld_x", s))
        for l in range(DEPTH):
            rmsnorm_to_xnT(c, l)
            if l % 2 == 0:
                even_mixer(c, l // 2, last)
            else:
                odd_mixer(c, l // 2, last)
            rmsnorm_to_xnT_ffn(c, DEPTH + l)
            ffn(c, l)
        dma("sp", gbc[:], nrm[2 * DEPTH].partition_broadcast(128), r=(), w=["gbc"], key="ld_gbc")
        ms(ss[:], 0.0, w=["ss"])
        for s in range(nsub):
            i = rr("xn", 2)
            act(xn_tm[i][0:pp, :], X[0:pp, s, :], AF.Square, r=[("X", s), "ss"], w=[("xn_tm", i), "ss"],
                accum_out=ss[0:pp, s:s + 1])
        rstd_chain(rstd, ss[0:pp, 0:nsub], 1.0 / D, pp, nsub, r=["ss"], w=["rstd"])
        for s in range(nsub):
            stt(X[0:pp, s, :], X[0:pp, s, :], rstd[0:pp, s:s + 1], gbc[0:pp, :], ALU.mult, ALU.mult,
                r=[("X", s), "rstd", "gbc"], w=[("X", s)])
            dma("sp", c.y_out[c.row0 + s * 128:c.row0 + s * 128 + pp, :], X[0:pp, s, :], r=[("X", s)], w=(), key=("st_y", s))

    rmsnorm_to_xnT_ffn = rmsnorm_to_xnT

    def schedule():
        setup()
        for p_ in range(NO):
            sample_cache_prep(p_)
        for t in range(NT):
            c = SeqCtx()
            c.name, c.N, c.pp, c.nsub = "p", TT, 128, NSUB
            c.pos0, c.row0, c.x_in, c.y_out = t * TT, t * TT, xp, yp
            process_tile(c, last=(t == NT - 1))
        c = SeqCtx()
        c.name, c.N, c.pp, c.nsub = "s", DEC, DEC, 1
        c.pos0, c.row0, c.x_in, c.y_out = PAST, 0, xs, ys
        process_tile(c, last=True)


    cnt0 = dict(cnt)
    real_P = P
    P = Prog()
    pst["dry"] = True
    schedule()
    npc = len(pst["order"])
    pst["wsc"] = [dint("wsc%d" % g_, [min(64, npc - 64 * g_), 128, 16 * 512], BF16) for g_ in range((npc + 63) // 64)]
    pst["dry"] = False
    cnt.update(cnt0)
    P = real_P
    schedule()

    P.emit(nc, es)
    es.close()
    return nc


def _consts():
    half = 32
    inv = (10000.0 ** (-np.arange(half, dtype=np.float32) / half)).astype(np.float32)
    invf = np.concatenate([inv, inv]).reshape(64, 1).astype(np.float32)
    k = np.arange(128)[:, None, None]
    r = np.arange(NSUB)[None, :, None]
    q = np.arange(TT)[None, None, :]
    m = (((r * 128 + k) // 64) <= (q // 64)).astype(np.float32).reshape(128, NSUB * TT)
    return invf, np.ascontiguousarray(m)


def run(cfg, inputs, trace=False):
    f = lambda a: np.ascontiguousarray(np.asarray(a, dtype=np.float32))
    NE, NO = cfg.NE, cfg.NO
    NOm = max(NO, 1)
    invf, maskc = _consts()
    cols = []
    for p_ in range(NE):
        cw = f(inputs["conv_a_w"])[p_]
        cols.append(cw.T.reshape(8, 128, 31).transpose(1, 0, 2).reshape(128, 248))
        for nm in ("conv_a_b", "ln_a_g", "ln_a_b"):
            cols.append(f(inputs[nm])[p_].reshape(8, 128).T)
        cols.append(np.zeros((128, 320 - 248 - 24), np.float32))
    for p_ in range(NOm):
        if NO:
            cw = f(inputs["conv_c_w"])[p_]
            cols.append(cw.T.reshape(8, 128, 3).transpose(1, 0, 2).reshape(128, 24))
        else:
            cols.append(np.zeros((128, 24), np.float32))
    pcol = np.ascontiguousarray(np.concatenate(cols, axis=1))
    nrm = np.ascontiguousarray(np.concatenate([f(inputs["norm_mix"]), f(inputs["norm_ffn"]),
                                               f(inputs["norm_final"])[None]], axis=0))
    lnv = np.ascontiguousarray(np.stack([f(inputs["ln_v_g"]), f(inputs["ln_v_b"])], axis=1))
    qkg = np.ascontiguousarray(np.stack([f(inputs["q_norm_g"]), f(inputs["kv_norm_g"])], axis=1))
    bsp = np.ascontiguousarray(f(inputs["b_spatial"]).reshape(NE, 1024))
    shared = dict(nrm=nrm, w_in_even=f(inputs["w_in_even"]), w_out_even=f(inputs["w_out_even"]),
                  w_in_odd=f(inputs["w_in_odd"]), w_uq=f(inputs["w_uq"]), w_ukv=f(inputs["w_ukv"]),
                  w_out_odd=f(inputs["w_out_odd"]), w_ffn_up=f(inputs["w_ffn_up"]), w_ffn_down=f(inputs["w_ffn_down"]),
                  pcol=pcol, lnv=lnv, qkg=qkg, bsp=bsp, wsp=f(inputs["w_spatial"]), c_invf=invf, c_mask=maskc)
    xpr, xsa = f(inputs["x_prompt"]), f(inputs["x_sample"])
    sca, scc = f(inputs["state_conv_a"]), f(inputs["state_conv_c"])
    ckv, ckr = f(inputs["cache_kv_latent"]), f(inputs["cache_k_rope"])
    in_maps = []
    for c in range(cfg.NCORES):
        m = dict(shared)
        m["xp"] = xpr[c]
        m["xs"] = xsa[c]
        m["sca"] = np.ascontiguousarray(sca[:, c])
        m["scc"] = np.ascontiguousarray(scc[:, c])
        m["ckv"] = np.ascontiguousarray(ckv[:, c])
        m["ckr"] = np.ascontiguousarray(ckr[:, c])
        in_maps.append(m)
    nc = build(cfg)
    res = run_bass_kernel_spmd(nc, in_maps, core_ids=list(range(cfg.NCORES)), **({"trace": True} if trace else {}))
    R = res.results
    st0 = lambda k: np.stack([R[c][k] for c in range(cfg.NCORES)], axis=0)
    st1 = lambda k: np.stack([R[c][k] for c in range(cfg.NCORES)], axis=1)
    outs = (st0("yp"), st0("ys"), st1("cap"), st1("cas"), st1("gvs"), st1("ccp"), st1("ccs"),
            st1("latp"), st1("krp"), st1("lats"), st1("krs"))
    return tuple(np.ascontiguousarray(o, dtype=np.float32) for o in outs), res


def kernel(**inputs):
    cfg = Cfg()
    outs, _ = run(cfg, inputs)
    return outs
```

```python
import math
from contextlib import ExitStack

import numpy as np
import concourse.bass as bass
import concourse.mybir as mybir
from concourse.bass_utils import run_bass_kernel_spmd

F32 = mybir.dt.float32
BF16 = mybir.dt.bfloat16
I32 = mybir.dt.int32
AF = mybir.ActivationFunctionType
ALU = mybir.AluOpType

D = 2048
DA = 1024
DFF = 8192
NH = 16
QR = 512
KVR = 512
ROPE = 64
EPS = 1e-6
CONVA = 31
TT = 256
NSUB = TT // 128
QSCALE = 1.0 / math.sqrt(192.0)
TWO_PI = 2.0 * math.pi


class Cfg:
    def __init__(self, SEQ=4096, PAST=4096, DEPTH=4, DEC=32, NCORES=8):
        self.SEQ, self.PAST, self.DEPTH, self.DEC, self.NCORES = SEQ, PAST, DEPTH, DEC, NCORES
        self.NE = (DEPTH + 1) // 2
        self.NO = DEPTH // 2


class _Op:
    __slots__ = ("eng", "fn", "deps", "key", "ndma", "sig", "ordv", "idx")


class Prog:
    ENGS = ("pe", "act", "dve", "pool", "sp")

    def __init__(self):
        self.ops = []
        self.res = {}
        self.key_count = {}
        self.key_last = {}

    def add(self, eng, fn, r=(), w=(), key=None, ndma=1):
        op = _Op()
        op.eng, op.fn, op.key, op.ndma = eng, fn, key, ndma
        op.sig = key is not None
        op.idx = len(self.ops)
        deps = set()
        isdma = key is not None
        for name in r:
            st = self.res.get(name)
            if st is None:
                st = self.res[name] = [None, {}, []]
            if st[0] is not None:
                deps.add((st[0], 0))
        for name in w:
            st = self.res.get(name)
            if st is None:
                st = self.res[name] = [None, {}, []]
            if st[0] is not None:
                deps.add((st[0], 1))
            for ri in st[1].values():
                deps.add((ri, 2))
            for ri in st[2]:
                deps.add((ri, 2))
        for name in r:
            st = self.res[name]
            if isdma:
                st[2].append(op.idx)
            else:
                st[1][eng] = op.idx
        for name in w:
            st = self.res[name]
            st[0] = op.idx
            st[1] = {}
            st[2] = []
        if isdma:
            prev = self.key_last.get(key)
            if prev is not None:
                deps.add((prev, 1))
            self.key_last[key] = op.idx
            c = self.key_count.get(key, 0) + ndma
            self.key_count[key] = c
            op.ordv = 16 * c
        op.deps = [d for d in deps if d[0] != op.idx]
        self.ops.append(op)
        return op

    def emit(self, nc, es):
        ops = self.ops
        need = []
        for op in ops:
            lst = []
            for (di, kind) in op.deps:
                p = ops[di]
                if p.key is None and op.key is None and p.eng == op.eng:
                    if op.eng == "pe" or kind == 2:
                        continue
                lst.append(di)
                if p.key is None:
                    p.sig = True
            need.append(lst)
        cnt = {e: 0 for e in self.ENGS}
        for op in ops:
            if op.key is None and op.sig:
                cnt[op.eng] += 1
                op.ordv = cnt[op.eng]
        esem = {e: es.enter_context(nc.semaphore("tl_" + e)) for e in self.ENGS}
        ksem = {}
        for i, k in enumerate(self.key_count):
            ksem[k] = es.enter_context(nc.semaphore("dk%d" % i))
        per = {e: [] for e in self.ENGS}
        for op, lst in zip(ops, need):
            per[op.eng].append((op, lst))
        blk = es.enter_context(nc.Block())

        def run(ename, eobj):
            waited = {}
            for op, lst in per[ename]:
                req = {}
                for di in lst:
                    p = ops[di]
                    s = ksem[p.key] if p.key is not None else esem[p.eng]
                    sid = id(s)
                    if sid not in req or req[sid][1] < p.ordv:
                        req[sid] = (s, p.ordv)
                for sid, (s, v) in req.items():
                    if waited.get(sid, 0) < v:
                        eobj.wait_ge(s, v)
                        waited[sid] = v
                ins = op.fn(eobj)
                if op.key is not None:
                    if not isinstance(ins, (list, tuple)):
                        ins = [ins]
                    assert len(ins) == op.ndma, (len(ins), op.ndma)
                    for i_ in ins:
                        i_.then_inc(ksem[op.key], 16)
                elif op.sig:
                    ins.then_inc(esem[ename], 1)
            if ename == "sp":
                for k, c in self.key_count.items():
                    eobj.wait_ge(ksem[k], 16 * c)

        @blk.tensor
        def _(e):
            run("pe", e)

        @blk.scalar
        def _(e):
            run("act", e)

        @blk.vector
        def _(e):
            run("dve", e)

        @blk.gpsimd
        def _(e):
            run("pool", e)

        @blk.sync
        def _(e):
            run("sp", e)


class SeqCtx:
    pass


def build(cfg):
    nc = bass.Bass("TRN2", target_bir_lowering=False)
    NE, NO, DEPTH = cfg.NE, cfg.NO, cfg.DEPTH
    SEQ, PAST, DEC = cfg.SEQ, cfg.PAST, cfg.DEC
    NT = SEQ // TT
    NPT = PAST // TT
    TS = PAST + TT

    def din(name, shape, dt=F32):
        return nc.dram_tensor(name, list(shape), dt, kind="ExternalInput").ap()

    def dout(name, shape):
        return nc.dram_tensor(name, list(shape), F32, kind="ExternalOutput").ap()

    def dint(name, shape, dt):
        return nc.dram_tensor(name, list(shape), dt, kind="Internal").ap()

    xp = din("xp", [SEQ, D])
    xs = din("xs", [DEC, D])
    sca = din("sca", [NE, 30, DA])
    scc = din("scc", [max(NO, 1), 2, DA])
    ckv_in = din("ckv", [max(NO, 1), PAST, KVR])
    ckr_in = din("ckr", [max(NO, 1), PAST, ROPE])
    nrm = din("nrm", [2 * DEPTH + 1, D])
    w_in_even = din("w_in_even", [NE, D, 4096])
    w_out_even = din("w_out_even", [NE, D, D])
    w_in_odd = din("w_in_odd", [max(NO, 1), D, 4160])
    w_uq = din("w_uq", [max(NO, 1), QR, 3072])
    w_ukv = din("w_ukv", [max(NO, 1), KVR, 4096])
    w_out_odd = din("w_out_odd", [max(NO, 1), 3072, D])
    w_up = din("w_ffn_up", [DEPTH, D, DFF])
    w_down = din("w_ffn_down", [DEPTH, DFF, D])
    pcol_in = din("pcol", [128, 320 * NE + 24 * max(NO, 1)])
    lnv_in = din("lnv", [NE, 2, DA])
    qkg_in = din("qkg", [max(NO, 1), 2, 512])
    bsp_in = din("bsp", [NE, 1024])
    wsp_in = din("wsp", [NE, 8, 128, 128])
    invf_in = din("c_invf", [64, 1])
    mask_in = din("c_mask", [128, NSUB * TT])

    yp = dout("yp", [SEQ, D])
    ys = dout("ys", [DEC, D])
    cap = dout("cap", [NE, 30, DA])
    cas = dout("cas", [NE, 30, DA])
    gvs = dout("gvs", [NE, DEC, DA])
    ccp = dout("ccp", [max(NO, 1), 2, DA])
    ccs = dout("ccs", [max(NO, 1), 2, DA])
    latp = dout("latp", [max(NO, 1), SEQ, KVR])
    krp = dout("krp", [max(NO, 1), SEQ, ROPE])
    lats = dout("lats", [max(NO, 1), DEC, KVR])
    krs = dout("krs", [max(NO, 1), DEC, ROPE])

    KTd = {"p": dint("KT_p", [max(NO, 1), NH, 128, SEQ], BF16), "s": dint("KT_s", [max(NO, 1), NH, 128, TS], BF16)}
    Vd = {"p": dint("V_p", [max(NO, 1), NH, SEQ, 128], BF16), "s": dint("V_s", [max(NO, 1), NH, TS, 128], BF16)}
    KRd = {"p": dint("KR_p", [max(NO, 1), 64, SEQ], BF16), "s": dint("KR_s", [max(NO, 1), 64, TS], BF16)}

    P = Prog()
    es = ExitStack()

    def sb(name, shape, dt):
        return es.enter_context(nc.sbuf_tensor("s_" + name, list(shape), dt))

    X = sb("X", [128, NSUB, D], F32)
    gbc = sb("gbc", [128, D], F32)
    xn_tm = [sb("xn_tm%d" % i, [128, D], BF16) for i in range(2)]
    xnT = sb("xnT", [128, 16, TT], BF16)
    NWR = 3
    wr = [sb("wr%d" % i, [128, 16, 512], BF16) for i in range(NWR)]
    big = sb("big", [128, 24, TT], BF16)
    rscr = [sb("rscr%d" % i, [128, TT], BF16) for i in range(2)]
    aT = sb("aT", [128, 8, 30 + TT], F32)
    sg = sb("sg", [128, 4, TT], F32)
    acc = [sb("acc%d" % i, [128, TT], F32) for i in range(8)]
    convo = sb("convo", [128, 8, TT], BF16)
    sqb = [sb("sqb%d" % i, [128, TT], BF16) for i in range(2)]
    uT = sb("uT", [128, 8, TT], BF16)
    v_tm = sb("v_tm", [128, NSUB, DA], BF16)
    gv = [sb("gv%d" % i, [128, DA], F32) for i in range(2)]
    f1 = sb("f1", [128, TT], F32)
    f2 = sb("f2", [128, TT], F32)
    f3 = sb("f3", [128, TT], F32)
    stage = sb("stage", [128, DA], F32)
    ss = sb("ss", [128, 4], F32)
    rstd = sb("rstd", [128, 4], F32)
    bst = sb("bst", [128, 12], F32)
    mv = sb("mv", [128, 2], F32)
    lrs = sb("lrs", [128, 1], F32)
    pcol = sb("pcol", [128, 320 * NE + 24 * max(NO, 1)], F32)
    ones_f = sb("ones_f", [1, 128], F32)
    WsT = sb("WsT", [128, NE, 8, 128], BF16)
    haloA = sb("haloA", [128, NE, 8, 30], F32)
    haloC = sb("haloC", [128, max(NO, 1), 8, 2], F32)
    idf = sb("idf", [128, 128], F32)
    idb = sb("idb", [128, 128], BF16)
    ones_b = sb("ones_b", [128, 128], BF16)
    mask = sb("mask", [128, NSUB, TT], BF16)
    invf = sb("invf", [64, 1], F32)
    sgn = sb("sgn", [64, 1], F32)
    iota_f = sb("iota_f", [64, TT], F32)
    cosT = sb("cosT", [64, TT], F32)
    sinS = sb("sinS", [64, TT], F32)
    cosQ = sb("cosQ", [64, TT], F32)
    sinQ = sb("sinQ", [64, TT], F32)
    ri = sb("ri", [64, TT], I32)
    zqnT = sb("zqnT", [128, 4, TT], BF16)
    ckvT = sb("ckvT", [128, 4, TT], BF16)
    ztm = [sb("ztm%d" % i, [128, 512], F32) for i in range(2)]
    ztb = [sb("ztb%d" % i, [128, 512], BF16) for i in range(2)]
    krT_all = sb("krT_all", [64, TS], BF16)
    krf = sb("krf", [64, TT], F32)
    krb = sb("krb", [64, TT], BF16)
    qn = [sb("qn%d" % i, [128, TT], BF16) for i in range(2)]
    qrb = [sb("qrb%d" % i, [64, TT], BF16) for i in range(2)]
    NKV = 6
    kvK = [sb("kvK%d" % i, [128, TT], BF16) for i in range(NKV)]
    kvV = [sb("kvV%d" % i, [128, NSUB, 128], BF16) for i in range(NKV)]
    NPT_ = 3
    pt = [sb("pt%d" % i, [128, NSUB * TT], BF16) for i in range(NPT_)]
    NST = 8
    kst = [sb("kst%d" % i, [128, TT], BF16) for i in range(NST)]
    vst = [sb("vst%d" % i, [128, 256], BF16) for i in range(NST)]

    ps = [es.enter_context(nc.psum_tensor("ps%d" % i, [128, 512], F32)) for i in range(8)]
    psb = [p_[:].bitcast(BF16) for p_ in ps]

    cnt = {"fm": 0, "tr": 0, "wr": 0, "rs": 0, "acc": 0, "sq": 0, "gv": 0, "xn": 0, "zt": 0, "q": 0,
           "kv": 0, "pt": 0, "st": 0, "ob": 0}

    def rr(name, n):
        v = cnt[name] % n
        cnt[name] += 1
        return v

    def fm_bank():
        return 4 + rr("fm", 2)

    def tr_bank():
        return 6 + rr("tr", 2)

    def PS(b):
        return ("ps", b)

    def dma(q, out, in_, r, w, key):
        P.add(q, lambda e, o=out, i=in_: e.dma_start(out=o, in_=i), r=r, w=w, key=key)

    LOOKAHEAD = 6
    pst = {"order": [], "index": {}, "ptr": 0, "dry": True, "wsc": None}

    def _emit_conv(i):
        (srcs, nk) = pst["order"][i]
        view = pst["wsc"][i // 64][i % 64].rearrange("p (k c) -> p k c", c=512)

        def fn(e, srcs=srcs, nk=nk, view=view):
            out = []
            for (s_, c0, ncl) in srcs:
                out.append(e.dma_start(out=view[:, 0:nk, c0:c0 + ncl], in_=s_.rearrange("(k p) c -> p k c", p=128)))
            return out
        P.add("pool", fn, r=(), w=[("wsc", i)], key=("cv", i % (LOOKAHEAD + 2)), ndma=len(srcs))

    def wpiece(src, nk, ncol, dst_col=0, slot=None, extra=None, pid=None):
        if slot is None:
            slot = rr("wr", NWR)
        srcs = [(src, dst_col, ncol)] + (extra or [])
        assert pid is not None
        if pst["dry"]:
            if pid not in pst["index"]:
                pst["index"][pid] = len(pst["order"])
                pst["order"].append((srcs, nk))
            return slot
        i = pst["index"][pid]
        tgt = min(i + LOOKAHEAD, len(pst["order"]) - 1)
        while pst["ptr"] <= tgt:
            _emit_conv(pst["ptr"])
            pst["ptr"] += 1
        view = pst["wsc"][i // 64][i % 64].rearrange("p (k c) -> p k c", c=512)
        P.add("pool", lambda e, slot=slot, nk=nk, view=view: e.dma_start(out=wr[slot][:, 0:nk, :], in_=view[:, 0:nk, :]),
              r=[("wsc", i)], w=[("wr", slot)], key=("wr", slot))
        return slot

    def mm(out, lhsT, rhs, start, stop, r, w):
        P.add("pe", lambda e, o=out, l=lhsT, rh=rhs, s=start, t=stop: e.matmul(o, lhsT=l, rhs=rh, start=s, stop=t),
              r=r, w=w)

    def tp(out, in_, ident, r, w):
        P.add("pe", lambda e, o=out, i=in_, d=ident: e.transpose(out=o, in_=i, identity=d), r=r, w=w)

    def act(out, in_, func, r, w, **kw):
        P.add("act", lambda e, o=out, i=in_, f=func, kw=kw: e.activation(out=o, in_=i, func=f, **kw), r=r, w=w)

    def tt(out, in0, in1, op, r, w, eng="dve"):
        P.add(eng, lambda e, o=out, a=in0, b=in1, p_=op: e.tensor_tensor(out=o, in0=a, in1=b, op=p_), r=r, w=w)

    def ts(out, in0, s1, s2, op0, op1, r, w, eng="dve"):
        if s2 is None:
            P.add(eng, lambda e, o=out, a=in0, s1=s1, p0=op0: e.tensor_scalar(out=o, in0=a, scalar1=s1, scalar2=None, op0=p0),
                  r=r, w=w)
        else:
            P.add(eng, lambda e, o=out, a=in0, s1=s1, s2=s2, p0=op0, p1=op1:
                  e.tensor_scalar(out=o, in0=a, scalar1=s1, scalar2=s2, op0=p0, op1=p1), r=r, w=w)

    def stt(out, in0, sc, in1, op0, op1, r, w, eng="dve"):
        P.add(eng, lambda e, o=out, a=in0, s=sc, b=in1, p0=op0, p1=op1:
              e.scalar_tensor_tensor(out=o, in0=a, scalar=s, in1=b, op0=p0, op1=p1), r=r, w=w)

    def cp(out, in_, r, w, eng="dve"):
        P.add(eng, lambda e, o=out, i=in_: e.tensor_copy(out=o, in_=i), r=r, w=w)

    def ms(ap, val, w, eng="dve"):
        P.add(eng, lambda e, a=ap, v=val: e.memset(a, v), r=(), w=w)

    def recip(out, in_, r, w):
        P.add("dve", lambda e, o=out, i=in_: e.reciprocal(out=o, in_=i), r=r, w=w)

    def rstd_chain(dst, src, n_inv, pp, ncols, r, w):
        ts(dst[0:pp, 0:ncols], src, n_inv, EPS, ALU.mult, ALU.add, r=r, w=w)
        act(dst[0:pp, 0:ncols], dst[0:pp, 0:ncols], AF.Sqrt, r=w, w=w)
        recip(dst[0:pp, 0:ncols], dst[0:pp, 0:ncols], r=w, w=w)

    def setup():
        dma("sp", pcol[:], pcol_in[:, :], r=(), w=["pcol"], key="ld_pcol")
        dma("sp", invf[:], invf_in[:, :], r=(), w=["invf"], key="ld_misc")
        dma("sp", f1[:], mask_in[:, 0:TT], r=(), w=["f1"], key="ld_misc")
        ms(idf[:], 0.0, w=["idf"], eng="pool")
        P.add("pool", lambda e: e.affine_select(out=idf[:], in_=idf[:], pattern=[[-1, 128]], compare_op=ALU.not_equal,
                                                fill=1.0, base=0, channel_multiplier=1), r=["idf"], w=["idf"])
        P.add("pool", lambda e: e.iota(ri[:], pattern=[[1, TT]], base=0, channel_multiplier=0), r=(), w=["ri"])
        cp(idb[:], idf[:], r=["idf"], w=["idb"])
        cp(iota_f[:], ri[:], r=["ri"], w=["iota_f"])
        ms(ones_b[:], 1.0, w=["ones_b"])
        ms(ones_f[:], 1.0, w=["ones_f"])
        ms(sgn[0:32, :], -1.0, w=["sgn"])
        ms(sgn[32:64, :], 1.0, w=["sgn"])
        ms(haloA[:], 0.0, w=["haloA"])
        ms(haloC[:], 0.0, w=["haloC"])
        for kb in range(NSUB):
            if kb > 0:
                dma("sp", f1[:], mask_in[:, kb * TT:(kb + 1) * TT], r=(), w=["f1"], key="ld_misc")
            cp(mask[:, kb, :], f1[:], r=["f1"], w=["mask"])
        for p_ in range(NE):
            for g in range(8):
                dma("sp", f2[:, 0:128], wsp_in[p_, g], r=(), w=["f2"], key="ld_misc")
                P.add("pool", lambda e: e.affine_select(out=f2[:, 0:128], in_=f2[:, 0:128], pattern=[[-1, 128]],
                                                        compare_op=ALU.is_ge, fill=0.0, base=0, channel_multiplier=1),
                      r=["f2"], w=["f2"])
                b = tr_bank()
                tp(ps[b][:, 0:128], f2[:, 0:128], idf[:], r=["f2", "idf"], w=[PS(b)])
                act(WsT[:, p_, g, :], ps[b][:, 0:128], AF.Copy, r=[PS(b)], w=["WsT"])

    def pc_even(p_):
        base = 320 * p_
        return dict(cw=base, cb=base + 248, lg=base + 256, lb=base + 264)

    def pc_odd(p_):
        return 320 * NE + 24 * p_

    def rmsnorm_to_xnT(c, gidx):
        pp, nsub, N = c.pp, c.nsub, c.N
        dma("sp", gbc[:], nrm[gidx].partition_broadcast(128), r=(), w=["gbc"], key="ld_gbc")
        ms(ss[:], 0.0, w=["ss"])
        for s in range(nsub):
            i = rr("xn", 2)
            act(xn_tm[i][0:pp, :], X[0:pp, s, :], AF.Square, r=[("X", s), "ss"], w=[("xn_tm", i), "ss"],
                accum_out=ss[0:pp, s:s + 1])
        rstd_chain(rstd, ss[0:pp, 0:nsub], 1.0 / D, pp, nsub, r=["ss"], w=["rstd"])
        for s in range(nsub):
            i = rr("xn", 2)
            stt(xn_tm[i][0:pp, :], X[0:pp, s, :], rstd[0:pp, s:s + 1], gbc[0:pp, :], ALU.mult, ALU.mult,
                r=[("X", s), "rstd", "gbc"], w=[("xn_tm", i)])
            for half in range(2):
                b = tr_bank()
                for k in range(8):
                    kk = half * 8 + k
                    tp(psb[b][:, k * pp:(k + 1) * pp], xn_tm[i][0:pp, kk * 128:(kk + 1) * 128], idb[0:pp, 0:pp],
                       r=[("xn_tm", i), "idb"], w=[PS(b)])
                act(xnT[:, half * 8:half * 8 + 8, s * 128:s * 128 + pp],
                    psb[b][:, 0:8 * pp].rearrange("p (k t) -> p k t", k=8), AF.Copy,
                    r=[PS(b)], w=[("xnT", s)])

    def xnT_res(c):
        return [("xnT", s) for s in range(c.nsub)]

    def fm_group(c, slot, nk, col, M, rhs_t, rhs_res, b, prow=0):
        N = c.N
        for k in range(nk):
            mm(ps[b][prow:prow + M, 0:N], wr[slot][:, k, col:col + M], rhs_t[:, k, 0:N], k == 0, k == nk - 1,
               r=[("wr", slot)] + rhs_res, w=[PS(b)])

    def residual_add_from(c, s, nb, b):
        pp = c.pp
        tt(X[0:pp, s, nb * 512:(nb + 1) * 512], X[0:pp, s, nb * 512:(nb + 1) * 512], ps[b][0:pp, :], ALU.add,
           r=[("X", s), PS(b)], w=[("X", s)])

    def out_proj(c, W, nkc, wname):
        pp, nsub = c.pp, c.nsub
        bigres = [("big", k) for k in range(nkc)]
        halves = [(0, nkc)] if nkc <= 16 else [(0, nkc // 2), (nkc // 2, nkc)]
        for nb in range(4):
            slots = []
            for (k0, k1) in halves:
                slots.append(wpiece(W[k0 * 128:k1 * 128, nb * 512:(nb + 1) * 512], k1 - k0, 512, pid=(wname, nb, k0)))
            for s in range(nsub):
                for hi, (k0, k1) in enumerate(halves):
                    for k in range(k0, k1):
                        mm(ps[s][0:pp, :], big[:, k, s * 128:s * 128 + pp], wr[slots[hi]][:, k - k0, :],
                           k == 0, k == nkc - 1, r=[("wr", slots[hi]), ("big", k)], w=[PS(s)])
                residual_add_from(c, s, nb, s)

    def ffn(c, l):
        pp, nsub, N = c.pp, c.nsub, c.N
        for q in range(4):
            for j in range(4):
                slot = wpiece(w_up[l, :, q * 2048 + j * 512: q * 2048 + (j + 1) * 512], 16, 512, pid=("up", l, q, j))
                for m in range(4):
                    hc = j * 4 + m
                    b = fm_bank()
                    fm_group(c, slot, 16, m * 128, 128, xnT, xnT_res(c), b)
                    i = rr("rs", 2)
                    act(rscr[i][:, 0:N], ps[b][:, 0:N], AF.Relu, r=[PS(b)], w=[("rscr", i)])
                    act(big[:, hc, 0:N], rscr[i][:, 0:N], AF.Square, r=[("rscr", i)], w=[("big", hc)])
            for nb in range(4):
                slot = wpiece(w_down[l, q * 2048:(q + 1) * 2048, nb * 512:(nb + 1) * 512], 16, 512, pid=("dn", l, q, nb))
                for s in range(nsub):
                    for k in range(16):
                        mm(ps[s][0:pp, :], big[:, k, s * 128:s * 128 + pp], wr[slot][:, k, :], k == 0, k == 15,
                           r=[("wr", slot), ("big", k)], w=[PS(s)])
                    residual_add_from(c, s, nb, s)

    def save_state_rows(c, src3, ncols_state, col0, out_ap):
        n = ncols_state
        for half in range(2):
            b = tr_bank()
            for k in range(4):
                ch = half * 4 + k
                tp(ps[b][0:n, k * 128:(k + 1) * 128], src3[:, ch, col0:col0 + n], idf[:], r=["aT", "idf"], w=[PS(b)])
            act(stage[0:n, half * 512:(half + 1) * 512], ps[b][0:n, :], AF.Copy, r=[PS(b)], w=["stage"])
        dma("sp", out_ap, stage[0:n, :], r=["stage"], w=(), key="st_stage")

    def load_state_rows(c, in_ap, n, dst3):
        dma("sp", stage[0:n, :], in_ap, r=(), w=["stage"], key="ld_stage")
        b = tr_bank()
        for ch in range(8):
            tp(ps[b][:, ch * n:(ch + 1) * n], stage[0:n, ch * 128:(ch + 1) * 128], idf[0:n, 0:n],
               r=["stage", "idf"], w=[PS(b)])
        act(dst3[:, :, 0:n], ps[b][:, 0:8 * n].rearrange("p (k t) -> p k t", k=8), AF.Copy, r=[PS(b)], w=["aT"])

    def even_mixer(c, p_, last):
        pp, nsub, N = c.pp, c.nsub, c.N
        W = w_in_even[p_]
        pc = pc_even(p_)
        H = 30
        if c.name == "s":
            load_state_rows(c, sca[p_], H, aT)
        else:
            cp(aT[:, :, 0:H], haloA[:, p_, :, :], r=["haloA"], w=["aT"])
        for grp in range(2):
            slot = wpiece(W[:, 1024 + grp * 512:1024 + (grp + 1) * 512], 16, 512, pid=("wie", p_, 2 + grp))
            for m in range(4):
                b = fm_bank()
                fm_group(c, slot, 16, m * 128, 128, xnT, xnT_res(c), b)
                act(sg[:, m, 0:N], ps[b][:, 0:N], AF.Sigmoid, r=[PS(b)], w=[("sg", m)])
            slot = wpiece(W[:, grp * 512:(grp + 1) * 512], 16, 512, pid=("wie", p_, grp))
            for m in range(4):
                ch = grp * 4 + m
                b = fm_bank()
                fm_group(c, slot, 16, m * 128, 128, xnT, xnT_res(c), b)
                tt(aT[:, ch, H:H + N], ps[b][:, 0:N], sg[:, m, 0:N], ALU.mult, r=[PS(b), ("sg", m)], w=["aT"])
        SUMB, SQB = 2, 3
        conv_acc = {}
        for g0 in (0, 4):
            for ch in range(g0, g0 + 4):
                ai = ch
                conv_acc[ch] = ai
                ts(acc[ai][:, 0:N], aT[:, ch, 0:N], pcol[:, pc["cw"] + ch * 31:pc["cw"] + ch * 31 + 1],
                   pcol[:, pc["cb"] + ch:pc["cb"] + ch + 1], ALU.mult, ALU.add, r=["aT", "pcol"], w=[("acc", ai)])
            for j in range(1, CONVA):
                for ch in range(g0, g0 + 4):
                    ai = ch
                    stt(acc[ai][:, 0:N], aT[:, ch, j:j + N], pcol[:, pc["cw"] + ch * 31 + j:pc["cw"] + ch * 31 + j + 1],
                        acc[ai][:, 0:N], ALU.mult, ALU.add, r=["aT", "pcol", ("acc", ai)], w=[("acc", ai)])
        for grp in range(2):
            slot = wpiece(W[:, 2048 + grp * 512:2048 + (grp + 1) * 512], 16, 512, pid=("wie", p_, 4 + grp))
            for m in range(4):
                ch = grp * 4 + m
                b = fm_bank()
                fm_group(c, slot, 16, m * 128, 128, xnT, xnT_res(c), b)
                act(uT[:, ch, 0:N], ps[b][:, 0:N], AF.Gelu, r=[PS(b)], w=[("uT", ch)])
        vslots = [wpiece(W[:, 3072 + j * 512:3072 + (j + 1) * 512], 16, 512, pid=("wie", p_, 6 + j)) for j in range(2)]
        dma("sp", gbc[:], lnv_in[p_].rearrange("a b -> (a b)").partition_broadcast(128), r=(), w=["gbc"], key="ld_gbc")
        for s in range(nsub):
            gi = rr("gv", 2)
            for j in range(2):
                b = j
                for k in range(16):
                    mm(ps[b][0:pp, :], xnT[:, k, s * 128:s * 128 + pp], wr[vslots[j]][:, k, :], k == 0, k == 15,
                       r=[("wr", vslots[j]), ("xnT", s)], w=[PS(b)])
                act(gv[gi][0:pp, j * 512:(j + 1) * 512], ps[b][0:pp, :], AF.Gelu, r=[PS(b)], w=[("gv", gi)])
            for j in range(2):
                P.add("dve", lambda e, gi=gi, j=j, pp=pp: e.bn_stats(out=bst[0:pp, j * 6:(j + 1) * 6], in_=gv[gi][0:pp, j * 512:(j + 1) * 512]),
                      r=[("gv", gi)], w=["bst"])
            P.add("dve", lambda e, pp=pp: e.bn_aggr(out=mv[0:pp, :], in_=bst[0:pp, :]), r=["bst"], w=["mv"])
            rstd_chain(lrs, mv[0:pp, 1:2], 1.0, pp, 1, r=["mv"], w=["lrs"])
            stt(gv[gi][0:pp, :], gv[gi][0:pp, :], mv[0:pp, 0:1], gbc[0:pp, 0:DA], ALU.subtract, ALU.mult,
                r=[("gv", gi), "mv", "gbc"], w=[("gv", gi)])
            stt(gv[gi][0:pp, :], gv[gi][0:pp, :], lrs[0:pp, 0:1], gbc[0:pp, DA:2 * DA], ALU.mult, ALU.add,
                r=[("gv", gi), "lrs", "gbc"], w=[("gv", gi)])
            act(v_tm[0:pp, s, :], gv[gi][0:pp, :], AF.Copy, r=[("gv", gi)], w=[("v_tm", s)])
            if c.name == "s":
                dma("sp", gvs[p_], gv[gi][0:pp, :], r=[("gv", gi)], w=(), key=("st_gv", gi))
        for ch in range(8):
            ai = ch
            act(convo[:, ch, 0:N], acc[ai][:, 0:N], AF.Copy, r=[("acc", ai)], w=[("convo", ch)])
            si = rr("sq", 2)
            act(sqb[si][:, 0:N], acc[ai][:, 0:N], AF.Square, r=[("acc", ai)], w=[("sqb", si)])
            mm(ps[SUMB][:, 0:N], ones_b[:], convo[:, ch, 0:N], ch == 0, ch == 7, r=["ones_b", ("convo", ch)], w=[PS(SUMB)])
            mm(ps[SQB][:, 0:N], ones_b[:], sqb[si][:, 0:N], ch == 0, ch == 7, r=["ones_b", ("sqb", si)], w=[PS(SQB)])
        if c.name == "s":
            save_state_rows(c, aT, H, N, cas[p_])
        else:
            if last:
                save_state_rows(c, aT, H, N, cap[p_])
            cp(haloA[:, p_, :, :], aT[:, :, N:N + H], r=["aT"], w=["haloA"])
        ts(f1[:, 0:N], ps[SUMB][:, 0:N], 1.0 / DA, None, ALU.mult, None, r=[PS(SUMB)], w=["f1"])
        tt(f3[:, 0:N], f1[:, 0:N], f1[:, 0:N], ALU.mult, r=["f1"], w=["f3"])
        stt(f2[:, 0:N], ps[SQB][:, 0:N], 1.0 / DA, f3[:, 0:N], ALU.mult, ALU.subtract, r=[PS(SQB), "f3"], w=["f2"])
        ts(f2[:, 0:N], f2[:, 0:N], 0.0, EPS, ALU.max, ALU.add, r=["f2"], w=["f2"])
        act(f2[:, 0:N], f2[:, 0:N], AF.Sqrt, r=["f2"], w=["f2"])
        recip(f2[:, 0:N], f2[:, 0:N], r=["f2"], w=["f2"])
        for ch in range(8):
            ai = rr("acc", 4)
            a_ = acc[ai]
            tt(a_[:, 0:N], convo[:, ch, 0:N], f1[:, 0:N], ALU.subtract, r=[("convo", ch), "f1"], w=[("acc", ai)])
            tt(a_[:, 0:N], a_[:, 0:N], f2[:, 0:N], ALU.mult, r=[("acc", ai), "f2"], w=[("acc", ai)])
            act(big[:, ch, 0:N], a_[:, 0:N], AF.Silu, r=[("acc", ai), "pcol"], w=[("big", ch)],
                scale=pcol[:, pc["lg"] + ch:pc["lg"] + ch + 1], bias=pcol[:, pc["lb"] + ch:pc["lb"] + ch + 1])
        L = pp
        dma("sp", stage[0:1, :], bsp_in[p_:p_ + 1, :], r=(), w=["stage"], key="ld_stage")
        for g in range(8):
            b = fm_bank()
            for s in range(nsub):
                mm(ps[b][:, s * 128:s * 128 + L], v_tm[0:L, s, g * 128:(g + 1) * 128], WsT[0:L, p_, g, 0:L], True, False,
                   r=[("v_tm", s), "WsT"], w=[PS(b)])
                mm(ps[b][:, s * 128:s * 128 + L], ones_f[0:1, :], stage[0:1, g * 128:g * 128 + L], False, True,
                   r=["ones_f", "stage"], w=[PS(b)])
            tt(big[:, 8 + g, 0:N], ps[b][:, 0:N], uT[:, g, 0:N], ALU.mult, r=[PS(b), ("uT", g)], w=[("big", 8 + g)])
        out_proj(c, w_out_even[p_], 16, ("woe", p_))

    def rope_tables(c):
        N = c.N
        pos0 = float(c.pos0)
        ts(krf[:, 0:N], iota_f[:, 0:N], pos0, None, ALU.add, None, r=["iota_f"], w=["krf"])
        ts(krf[:, 0:N], krf[:, 0:N], invf[:, 0:1], None, ALU.mult, None, r=["krf", "invf"], w=["krf"])
        for (dst, phase) in ((cosT, 0.25), (sinS, 0.0)):
            nm = "cosT" if dst is cosT else "sinS"
            ts(f1[0:64, 0:N], krf[:, 0:N], 1.0 / TWO_PI, phase, ALU.mult, ALU.add, r=["krf"], w=["f1"])
            cp(ri[:, 0:N], f1[0:64, 0:N], r=["f1"], w=["ri"])
            cp(f2[0:64, 0:N], ri[:, 0:N], r=["ri"], w=["f2"])
            tt(f1[0:64, 0:N], f1[0:64, 0:N], f2[0:64, 0:N], ALU.subtract, r=["f1", "f2"], w=["f1"])
            stt(f2[0:64, 0:N], f1[0:64, 0:N], 0.5, f1[0:64, 0:N], ALU.is_gt, ALU.subtract, r=["f1"], w=["f2"])
            stt(f1[0:64, 0:N], f1[0:64, 0:N], -0.5, f2[0:64, 0:N], ALU.is_lt, ALU.subtract, r=["f1", "f2"], w=["f1"])
            act(dst[:, 0:N], f1[0:64, 0:N], AF.Sin, r=["f1"], w=[nm], scale=TWO_PI)
        ts(sinS[:, 0:N], sinS[:, 0:N], sgn[:, 0:1], None, ALU.mult, None, r=["sinS", "sgn"], w=["sinS"])
        ts(cosQ[:, 0:N], cosT[:, 0:N], QSCALE, None, ALU.mult, None, r=["cosT"], w=["cosQ"])
        ts(sinQ[:, 0:N], sinS[:, 0:N], QSCALE, None, ALU.mult, None, r=["sinS"], w=["sinQ"])

    def kv_up(c_name, p_, srcT, src_res, pos0, N, pp, nsub):
        Wk = w_ukv[p_]
        for pi in range(8):
            slot = wpiece(Wk[:, pi * 512:(pi + 1) * 512], 4, 512, pid=("ukv", p_, pi))
            for hh in range(2):
                h = 2 * pi + hh
                b = fm_bank()
                for k in range(4):
                    mm(ps[b][:, 0:N], wr[slot][:, k, hh * 256:hh * 256 + 128], srcT[:, k, 0:N], k == 0, k == 3,
                       r=[("wr", slot)] + src_res, w=[PS(b)])
                si = rr("st", NST)
                act(kst[si][:, 0:N], ps[b][:, 0:N], AF.Copy, r=[PS(b)], w=[("kst", si)])
                dma("sp", KTd[c_name][p_, h, :, pos0:pos0 + N], kst[si][:, 0:N], r=[("kst", si)],
                    w=[("KT", c_name, p_, h, pos0 // TT)], key=("st_k", si))
            for s in range(nsub):
                b = s
                for k in range(4):
                    mm(ps[b][0:pp, 0:256].rearrange("p (h d) -> p h d", h=2), srcT[:, k, s * 128:s * 128 + pp],
                       wr[slot][:, k, :].rearrange("p (h t d) -> p h t d", h=2, t=2)[:, :, 1, :],
                       k == 0, k == 3, r=[("wr", slot)] + src_res, w=[PS(b)])
                si = rr("st", NST)
                act(vst[si][0:pp, :], ps[b][0:pp, 0:256], AF.Copy, r=[PS(b)], w=[("vst", si)])
                for hh in range(2):
                    h = 2 * pi + hh
                    dma("sp", Vd[c_name][p_, h, pos0 + s * 128:pos0 + s * 128 + pp, :], vst[si][0:pp, hh * 128:(hh + 1) * 128],
                        r=[("vst", si)], w=[("V", c_name, p_, h, pos0 // TT, s)], key=("st_v", si, hh))

    def odd_mixer(c, p_, last):
        pp, nsub, N = c.pp, c.nsub, c.N
        W = w_in_odd[p_]
        pco = pc_odd(p_)
        H = 2
        rope_tables(c)
        if c.name == "s":
            load_state_rows(c, scc[p_], H, aT)
        else:
            cp(aT[:, :, 0:H], haloC[:, p_, :, :], r=["haloC"], w=["aT"])
        for grp in range(2):
            slot = wpiece(W[:, 1024 + grp * 512:1024 + (grp + 1) * 512], 16, 512, pid=("wio", p_, 2 + grp))
            for m in range(4):
                b = fm_bank()
                fm_group(c, slot, 16, m * 128, 128, xnT, xnT_res(c), b)
                act(sg[:, m, 0:N], ps[b][:, 0:N], AF.Copy, r=[PS(b)], w=[("sg", m)])
            slot = wpiece(W[:, 2048 + grp * 512:2048 + (grp + 1) * 512], 16, 512, pid=("wio", p_, 4 + grp))
            for m in range(4):
                ch = grp * 4 + m
                b = fm_bank()
                fm_group(c, slot, 16, m * 128, 128, xnT, xnT_res(c), b)
                tt(aT[:, ch, H:H + N], ps[b][:, 0:N], sg[:, m, 0:N], ALU.mult, r=[PS(b), ("sg", m)], w=["aT"])
        for ch in range(8):
            ai = rr("acc", 4)
            a_ = acc[ai]
            ts(a_[:, 0:N], aT[:, ch, 0:N], pcol[:, pco + ch * 3:pco + ch * 3 + 1], None, ALU.mult, None,
               r=["aT", "pcol"], w=[("acc", ai)])
            stt(a_[:, 0:N], aT[:, ch, 1:1 + N], pcol[:, pco + ch * 3 + 1:pco + ch * 3 + 2], a_[:, 0:N], ALU.mult, ALU.add,
                r=["aT", "pcol", ("acc", ai)], w=[("acc", ai)])
            stt(convo[:, ch, 0:N], aT[:, ch, 2:2 + N], pcol[:, pco + ch * 3 + 2:pco + ch * 3 + 3], a_[:, 0:N], ALU.mult, ALU.add,
                r=["aT", "pcol", ("acc", ai)], w=[("convo", ch)])
        if c.name == "s":
            save_state_rows(c, aT, H, N, ccs[p_])
        else:
            if last:
                save_state_rows(c, aT, H, N, ccp[p_])
            cp(haloC[:, p_, :, :], aT[:, :, N:N + H], r=["aT"], w=["haloC"])
        for grp in range(2):
            slot = wpiece(W[:, grp * 512:(grp + 1) * 512], 16, 512, pid=("wio", p_, grp))
            for m in range(4):
                ch = grp * 4 + m
                b = fm_bank()
                fm_group(c, slot, 16, m * 128, 128, xnT, xnT_res(c), b)
                tt(big[:, ch, 0:N], ps[b][:, 0:N], convo[:, ch, 0:N], ALU.mult, r=[PS(b), ("convo", ch)], w=[("big", ch)])
        dma("sp", gbc[:, 0:1024], qkg_in[p_].rearrange("a b -> (a b)").partition_broadcast(128), r=(), w=["gbc"], key="ld_gbc")
        for which in range(2):
            slot = wpiece(W[:, 3072 + which * 512:3072 + (which + 1) * 512], 16, 512, pid=("wio", p_, 6 + which))
            dstT = zqnT if which == 0 else ckvT
            dname = "zqnT" if which == 0 else "ckvT"
            ms(ss[:], 0.0, w=["ss"])
            for s in range(nsub):
                b = s
                for k in range(16):
                    mm(ps[b][0:pp, :], xnT[:, k, s * 128:s * 128 + pp], wr[slot][:, k, :], k == 0, k == 15,
                       r=[("wr", slot), ("xnT", s)], w=[PS(b)])
                zi = rr("zt", 2)
                act(ztm[zi][0:pp, :], ps[b][0:pp, :], AF.Square, r=[PS(b), "ss"], w=[("ztm", zi), "ss"],
                    accum_out=ss[0:pp, s:s + 1])
            rstd_chain(rstd, ss[0:pp, 0:nsub], 1.0 / 512, pp, nsub, r=["ss"], w=["rstd"])
            for s in range(nsub):
                b = s
                zi = rr("zt", 2)
                stt(ztm[zi][0:pp, :], ps[b][0:pp, :], rstd[0:pp, s:s + 1], gbc[0:pp, which * 512:(which + 1) * 512], ALU.mult, ALU.mult,
                    r=[PS(b), "rstd", "gbc"], w=[("ztm", zi)])
                if which == 1:
                    out_ap = (lats[p_] if c.name == "s" else latp[p_, c.pos0 + s * 128:c.pos0 + s * 128 + pp, :])
                    dma("sp", out_ap, ztm[zi][0:pp, :], r=[("ztm", zi)], w=(), key=("st_lat", zi))
                act(ztb[zi][0:pp, :], ztm[zi][0:pp, :], AF.Copy, r=[("ztm", zi)], w=[("ztb", zi)])
                tb = tr_bank()
                for k in range(4):
                    tp(psb[tb][:, k * pp:(k + 1) * pp], ztb[zi][0:pp, k * 128:(k + 1) * 128], idb[0:pp, 0:pp],
                       r=[("ztb", zi), "idb"], w=[PS(tb)])
                act(dstT[:, :, s * 128:s * 128 + pp], psb[tb][:, 0:4 * pp].rearrange("p (k t) -> p k t", k=4), AF.Copy,
                    r=[PS(tb)], w=[dname])
        slot = wpiece(W[:, 4096:4160], 16, 64, extra=[(W[:, 4128:4160], 64, 32), (W[:, 4096:4128], 96, 32)], pid=("wio", p_, 8))
        bA = fm_bank()
        fm_group(c, slot, 16, 0, 64, xnT, xnT_res(c), bA)
        bB = fm_bank()
        fm_group(c, slot, 16, 64, 64, xnT, xnT_res(c), bB)
        tt(f1[0:64, 0:N], ps[bA][0:64, 0:N], cosT[:, 0:N], ALU.mult, r=[PS(bA), "cosT"], w=["f1"])
        tt(f2[0:64, 0:N], ps[bB][0:64, 0:N], sinS[:, 0:N], ALU.mult, r=[PS(bB), "sinS"], w=["f2"])
        tt(krf[:, 0:N], f1[0:64, 0:N], f2[0:64, 0:N], ALU.add, r=["f1", "f2"], w=["krf"])
        act(krb[:, 0:N], krf[:, 0:N], AF.Copy, r=["krf"], w=["krb"])
        dma("sp", KRd[c.name][p_, :, c.pos0:c.pos0 + N], krb[:, 0:N], r=["krb"], w=[("KR", c.name, p_, c.pos0 // TT)], key="st_kr")
        tb = tr_bank()
        for s in range(nsub):
            tp(ps[tb][0:pp, s * 64:(s + 1) * 64], krf[:, s * 128:s * 128 + pp], idf[0:64, 0:64], r=["krf", "idf"], w=[PS(tb)])
        act(stage[0:pp, 0:nsub * 64], ps[tb][0:pp, 0:nsub * 64], AF.Copy, r=[PS(tb)], w=["stage"])
        if c.name == "s":
            dma("sp", krs[p_], stage[0:pp, 0:64], r=["stage"], w=(), key="st_stage")
        else:
            dma("sp", krp[p_, c.pos0:c.pos0 + N, :].rearrange("(s p) r -> p s r", p=128),
                stage[0:pp, 0:nsub * 64].rearrange("p (s r) -> p s r", s=nsub), r=["stage"], w=(), key="st_stage")
        kv_up(c.name, p_, ckvT, ["ckvT"], c.pos0, N, pp, nsub)
        nkeys = c.pos0 + N
        dma("sp", krT_all[:, 0:nkeys], KRd[c.name][p_, :, 0:nkeys], r=[("KR", c.name, p_, kt_) for kt_ in range(c.pos0 // TT + 1)], w=["krT_all"], key="ld_krT")
        Wq = w_uq[p_]
        nkt_full = c.pos0 // TT
        qstate = {}

        def emit_q(h):
            if h % 2 == 0:
                ex = []
                for hh in range(2):
                    o_ = hh * 256
                    src0 = (h + hh) * 192
                    if hh == 1:
                        ex.append((Wq[:, src0:src0 + 192], o_, 192))
                    ex.append((Wq[:, src0 + 160:src0 + 192], o_ + 192, 32))
                    ex.append((Wq[:, src0 + 128:src0 + 160], o_ + 224, 32))
                qstate["slot"] = wpiece(Wq[:, h * 192:h * 192 + 192], 4, 192, extra=ex, pid=("uq", p_, h))
            qslot = qstate["slot"]
            o = (h % 2) * 256
            qi = rr("q", 2)
            b = tr_bank()
            fm_group(c, qslot, 4, o, 128, zqnT, ["zqnT"], b)
            act(qn[qi][:, 0:N], ps[b][:, 0:N], AF.Identity, r=[PS(b)], w=[("qn", qi)], scale=QSCALE)
            bA = tr_bank()
            fm_group(c, qslot, 4, o + 128, 64, zqnT, ["zqnT"], bA)
            tt(f1[0:64, 0:N], ps[bA][0:64, 0:N], cosQ[:, 0:N], ALU.mult, r=[PS(bA), "cosQ"], w=["f1"])
            bB = tr_bank()
            fm_group(c, qslot, 4, o + 192, 64, zqnT, ["zqnT"], bB)
            tt(f2[0:64, 0:N], ps[bB][0:64, 0:N], sinQ[:, 0:N], ALU.mult, r=[PS(bB), "sinQ"], w=["f2"])
            tt(qrb[qi][:, 0:N], f1[0:64, 0:N], f2[0:64, 0:N], ALU.add, r=["f1", "f2"], w=[("qrb", qi)])
            return qi

        units = []
        for kt in range(nkt_full):
            units.append((kt, list(range(NSUB)), 128, False))
        if c.name == "p":
            units.append((nkt_full, list(range(NSUB)), 128, True))
        else:
            units.append((nkt_full, [0], N, False))
        nun = len(units)
        q_next = emit_q(0)
        for h in range(NH):
            qi = q_next
            ob = rr("ob", 2)
            OB, RB = ob, 2 + ob
            slots = {}

            def load_kt(kt, h=h):
                ki = rr("kv", NKV)
                kn = TT if (c.name == "p" or kt < nkt_full) else N
                dma("sp", kvK[ki][:, 0:kn], KTd[c.name][p_, h, :, kt * TT:kt * TT + kn], r=[("KT", c.name, p_, h, kt)],
                    w=[("kvK", ki)], key=("ld_k", ki))
                if kn == TT:
                    dma("sp", kvV[ki][:, :, :], Vd[c.name][p_, h, kt * TT:(kt + 1) * TT, :].rearrange("(b p) d -> p b d", p=128),
                        r=[("V", c.name, p_, h, kt, s_) for s_ in range(NSUB)], w=[("kvV", ki)], key=("ld_v", ki))
                else:
                    dma("sp", kvV[ki][0:kn, 0, :], Vd[c.name][p_, h, kt * TT:kt * TT + kn, :],
                        r=[("V", c.name, p_, h, kt, 0)], w=[("kvV", ki)], key=("ld_v", ki))
                slots[kt] = ki

            sbanks = {}

            def score(ux, qi=qi):
                kt, kbs, kp, diag = units[ux]
                if kt not in slots:
                    load_kt(kt)
                ki = slots[kt]
                b = fm_bank()
                sbanks[ux] = b
                for i_, kb in enumerate(kbs):
                    mm(ps[b][0:kp, i_ * N:(i_ + 1) * N], kvK[ki][:, kb * 128:kb * 128 + kp], qn[qi][:, 0:N], True, False,
                       r=[("kvK", ki), ("qn", qi)], w=[PS(b)])
                    mm(ps[b][0:kp, i_ * N:(i_ + 1) * N], krT_all[:, kt * TT + kb * 128:kt * TT + kb * 128 + kp], qrb[qi][:, 0:N],
                       False, True, r=["krT_all", ("qrb", qi)], w=[PS(b)])

            score(0)
            if h + 1 < NH:
                q_next = emit_q(h + 1)
            for ux in range(nun):
                kt, kbs, kp, diag = units[ux]
                if ux + 1 < nun:
                    score(ux + 1)
                b = sbanks[ux]
                pi = rr("pt", NPT_)
                wd = len(kbs) * N
                act(pt[pi][0:kp, 0:wd], ps[b][0:kp, 0:wd], AF.Exp, r=[PS(b)], w=[("pt", pi)])
                if diag:
                    tt(pt[pi][0:kp, 0:wd], pt[pi][0:kp, 0:wd], mask[0:kp, :, :].rearrange("p a b -> p (a b)"), ALU.mult,
                       r=[("pt", pi), "mask"], w=[("pt", pi)])
                ki = slots[kt]
                for i_, kb in enumerate(kbs):
                    first = (ux == 0 and i_ == 0)
                    lastb = (ux == nun - 1 and i_ == len(kbs) - 1)
                    mm(ps[OB][:, 0:N], kvV[ki][0:kp, kb, :], pt[pi][0:kp, i_ * N:(i_ + 1) * N], first, lastb,
                       r=[("kvV", ki), ("pt", pi)], w=[PS(OB)])
                    mm(ps[RB][:, 0:N], ones_b[0:kp, :], pt[pi][0:kp, i_ * N:(i_ + 1) * N], first, lastb,
                       r=["ones_b", ("pt", pi)], w=[PS(RB)])
            recip(f3[:, 0:N], ps[RB][:, 0:N], r=[PS(RB)], w=["f3"])
            tt(big[:, 8 + h, 0:N], ps[OB][:, 0:N], f3[:, 0:N], ALU.mult, r=[PS(OB), "f3"], w=[("big", 8 + h)])
        out_proj(c, w_out_odd[p_], 24, ("woo", p_))

    def sample_cache_prep(p_):
        cst = X[:, 0, 0:NSUB * 512].rearrange("p (s r) -> p s r", s=NSUB)
        ckrst = X[:, 1, 0:NSUB * 64].rearrange("p (s r) -> p s r", s=NSUB)
        for kt in range(NPT):
            dma("sp", cst, ckv_in[p_, kt * TT:(kt + 1) * TT, :].rearrange("(s p) r -> p s r", p=128),
                r=(), w=[("X", 0)], key=("ld_x", 0))
            for s in range(NSUB):
                act(ztb[0][:, :], cst[:, s, :], AF.Copy, r=[("X", 0)], w=[("ztb", 0)])
                tb = tr_bank()
                for k in range(4):
                    tp(psb[tb][:, k * 128:(k + 1) * 128], ztb[0][:, k * 128:(k + 1) * 128], idb[:], r=[("ztb", 0), "idb"], w=[PS(tb)])
                act(ckvT[:, :, s * 128:(s + 1) * 128], psb[tb][:, 0:512].rearrange("p (k t) -> p k t", k=4), AF.Copy,
                    r=[PS(tb)], w=["ckvT"])
            kv_up("s", p_, ckvT, ["ckvT"], kt * TT, TT, 128, NSUB)
            dma("sp", ckrst, ckr_in[p_, kt * TT:(kt + 1) * TT, :].rearrange("(s p) r -> p s r", p=128),
                r=(), w=[("X", 1)], key=("ld_x", 1))
            tb = tr_bank()
            for s in range(NSUB):
                tp(ps[tb][0:64, s * 128:(s + 1) * 128], ckrst[:, s, :], idf[:], r=[("X", 1), "idf"], w=[PS(tb)])
            act(krb[:, :], ps[tb][0:64, 0:TT], AF.Copy, r=[PS(tb)], w=["krb"])
            dma("sp", KRd["s"][p_, :, kt * TT:(kt + 1) * TT], krb[:, :], r=["krb"], w=[("KR", "s", p_, kt)], key="st_kr")

    def process_tile(c, last):
        pp, nsub, N = c.pp, c.nsub, c.N
        for s in range(nsub):
            dma("sp", X[0:pp, s, :], c.x_in[c.row0 + s * 128:c.row0 + s * 128 + pp, :], r=(), w=[("X", s)], key=("ld_x", s))
        for l in range(DEPTH):
            rmsnorm_to_xnT(c, l)
            if l % 2 == 0:
                even_mixer(c, l // 2, last)
            else:
                odd_mixer(c, l // 2, last)
            rmsnorm_to_xnT_ffn(c, DEPTH + l)
            ffn(c, l)
        dma("sp", gbc[:], nrm[2 * DEPTH].partition_broadcast(128), r=(), w=["gbc"], key="ld_gbc")
        ms(ss[:], 0.0, w=["ss"])
        for s in range(nsub):
            i = rr("xn", 2)
            act(xn_tm[i][0:pp, :], X[0:pp, s, :], AF.Square, r=[("X", s), "ss"], w=[("xn_tm", i), "ss"],
                accum_out=ss[0:pp, s:s + 1])
        rstd_chain(rstd, ss[0:pp, 0:nsub], 1.0 / D, pp, nsub, r=["ss"], w=["rstd"])
        for s in range(nsub):
            stt(X[0:pp, s, :], X[0:pp, s, :], rstd[0:pp, s:s + 1], gbc[0:pp, :], ALU.mult, ALU.mult,
                r=[("X", s), "rstd", "gbc"], w=[("X", s)])
            dma("sp", c.y_out[c.row0 + s * 128:c.row0 + s * 128 + pp, :], X[0:pp, s, :], r=[("X", s)], w=(), key=("st_y", s))

    rmsnorm_to_xnT_ffn = rmsnorm_to_xnT

    def schedule():
        setup()
        for p_ in range(NO):
            sample_cache_prep(p_)
        for t in range(NT):
            c = SeqCtx()
            c.name, c.N, c.pp, c.nsub = "p", TT, 128, NSUB
            c.pos0, c.row0, c.x_in, c.y_out = t * TT, t * TT, xp, yp
            process_tile(c, last=(t == NT - 1))
        c = SeqCtx()
        c.name, c.N, c.pp, c.nsub = "s", DEC, DEC, 1
        c.pos0, c.row0, c.x_in, c.y_out = PAST, 0, xs, ys
        process_tile(c, last=True)


    cnt0 = dict(cnt)
    real_P = P
    P = Prog()
    pst["dry"] = True
    schedule()
    npc = len(pst["order"])
    pst["wsc"] = [dint("wsc%d" % g_, [min(64, npc - 64 * g_), 128, 16 * 512], BF16) for g_ in range((npc + 63) // 64)]
    pst["dry"] = False
    cnt.update(cnt0)
    P = real_P
    schedule()

    P.emit(nc, es)
    es.close()
    return nc


def _consts():
    half = 32
    inv = (10000.0 ** (-np.arange(half, dtype=np.float32) / half)).astype(np.float32)
    invf = np.concatenate([inv, inv]).reshape(64, 1).astype(np.float32)
    k = np.arange(128)[:, None, None]
    r = np.arange(NSUB)[None, :, None]
    q = np.arange(TT)[None, None, :]
    m = (((r * 128 + k) // 64) <= (q // 64)).astype(np.float32).reshape(128, NSUB * TT)
    return invf, np.ascontiguousarray(m)


def run(cfg, inputs, trace=False):
    f = lambda a: np.ascontiguousarray(np.asarray(a, dtype=np.float32))
    NE, NO = cfg.NE, cfg.NO
    NOm = max(NO, 1)
    invf, maskc = _consts()
    cols = []
    for p_ in range(NE):
        cw = f(inputs["conv_a_w"])[p_]
        cols.append(cw.T.reshape(8, 128, 31).transpose(1, 0, 2).reshape(128, 248))
        for nm in ("conv_a_b", "ln_a_g", "ln_a_b"):
            cols.append(f(inputs[nm])[p_].reshape(8, 128).T)
        cols.append(np.zeros((128, 320 - 248 - 24), np.float32))
    for p_ in range(NOm):
        if NO:
            cw = f(inputs["conv_c_w"])[p_]
            cols.append(cw.T.reshape(8, 128, 3).transpose(1, 0, 2).reshape(128, 24))
        else:
            cols.append(np.zeros((128, 24), np.float32))
    pcol = np.ascontiguousarray(np.concatenate(cols, axis=1))
    nrm = np.ascontiguousarray(np.concatenate([f(inputs["norm_mix"]), f(inputs["norm_ffn"]),
                                               f(inputs["norm_final"])[None]], axis=0))
    lnv = np.ascontiguousarray(np.stack([f(inputs["ln_v_g"]), f(inputs["ln_v_b"])], axis=1))
    qkg = np.ascontiguousarray(np.stack([f(inputs["q_norm_g"]), f(inputs["kv_norm_g"])], axis=1))
    bsp = np.ascontiguousarray(f(inputs["b_spatial"]).reshape(NE, 1024))
    shared = dict(nrm=nrm, w_in_even=f(inputs["w_in_even"]), w_out_even=f(inputs["w_out_even"]),
                  w_in_odd=f(inputs["w_in_odd"]), w_uq=f(inputs["w_uq"]), w_ukv=f(inputs["w_ukv"]),
                  w_out_odd=f(inputs["w_out_odd"]), w_ffn_up=f(inputs["w_ffn_up"]), w_ffn_down=f(inputs["w_ffn_down"]),
                  pcol=pcol, lnv=lnv, qkg=qkg, bsp=bsp, wsp=f(inputs["w_spatial"]), c_invf=invf, c_mask=maskc)
    xpr, xsa = f(inputs["x_prompt"]), f(inputs["x_sample"])
    sca, scc = f(inputs["state_conv_a"]), f(inputs["state_conv_c"])
    ckv, ckr = f(inputs["cache_kv_latent"]), f(inputs["cache_k_rope"])
    in_maps = []
    for c in range(cfg.NCORES):
        m = dict(shared)
        m["xp"] = xpr[c]
        m["xs"] = xsa[c]
        m["sca"] = np.ascontiguousarray(sca[:, c])
        m["scc"] = np.ascontiguousarray(scc[:, c])
        m["ckv"] = np.ascontiguousarray(ckv[:, c])
        m["ckr"] = np.ascontiguousarray(ckr[:, c])
        in_maps.append(m)
    nc = build(cfg)
    res = run_bass_kernel_spmd(nc, in_maps, core_ids=list(range(cfg.NCORES)), **({"trace": True} if trace else {}))
    R = res.results
    st0 = lambda k: np.stack([R[c][k] for c in range(cfg.NCORES)], axis=0)
    st1 = lambda k: np.stack([R[c][k] for c in range(cfg.NCORES)], axis=1)
    outs = (st0("yp"), st0("ys"), st1("cap"), st1("cas"), st1("gvs"), st1("ccp"), st1("ccs"),
            st1("latp"), st1("krp"), st1("lats"), st1("krs"))
    return tuple(np.ascontiguousarray(o, dtype=np.float32) for o in outs), res


def kernel(**inputs):
    cfg = Cfg()
    outs, _ = run(cfg, inputs)
    return outs
```

```python
import math
from contextlib import ExitStack

import numpy as np
import concourse.bass as bass
import concourse.mybir as mybir
from concourse.bass_utils import run_bass_kernel_spmd

F32 = mybir.dt.float32
BF16 = mybir.dt.bfloat16
I32 = mybir.dt.int32
AF = mybir.ActivationFunctionType
ALU = mybir.AluOpType

D = 2048
DA = 1024
DFF = 8192
NH = 16
QR = 512
KVR = 512
ROPE = 64
EPS = 1e-6
CONVA = 31
TT = 256
NSUB = TT // 128
QSCALE = 1.0 / math.sqrt(192.0)
TWO_PI = 2.0 * math.pi


class Cfg:
    def __init__(self, SEQ=4096, PAST=4096, DEPTH=4, DEC=32, NCORES=8):
        self.SEQ, self.PAST, self.DEPTH, self.DEC, self.NCORES = SEQ, PAST, DEPTH, DEC, NCORES
        self.NE = (DEPTH + 1) // 2
        self.NO = DEPTH // 2


class _Op:
    __slots__ = ("eng", "fn", "deps", "key", "ndma", "sig", "ordv", "idx")


class Prog:
    ENGS = ("pe", "act", "dve", "pool", "sp")

    def __init__(self):
        self.ops = []
        self.res = {}
        self.key_count = {}
        self.key_last = {}

    def add(self, eng, fn, r=(), w=(), key=None, ndma=1):
        op = _Op()
        op.eng, op.fn, op.key, op.ndma = eng, fn, key, ndma
        op.sig = key is not None
        op.idx = len(self.ops)
        deps = set()
        isdma = key is not None
        for name in r:
            st = self.res.get(name)
            if st is None:
                st = self.res[name] = [None, {}, []]
            if st[0] is not None:
                deps.add((st[0], 0))
        for name in w:
            st = self.res.get(name)
            if st is None:
                st = self.res[name] = [None, {}, []]
            if st[0] is not None:
                deps.add((st[0], 1))
            for ri in st[1].values():
                deps.add((ri, 2))
            for ri in st[2]:
                deps.add((ri, 2))
        for name in r:
            st = self.res[name]
            if isdma:
                st[2].append(op.idx)
            else:
                st[1][eng] = op.idx
        for name in w:
            st = self.res[name]
            st[0] = op.idx
            st[1] = {}
            st[2] = []
        if isdma:
            prev = self.key_last.get(key)
            if prev is not None:
                deps.add((prev, 1))
            self.key_last[key] = op.idx
            c = self.key_count.get(key, 0) + ndma
            self.key_count[key] = c
            op.ordv = 16 * c
        op.deps = [d for d in deps if d[0] != op.idx]
        self.ops.append(op)
        return op

    def emit(self, nc, es):
        ops = self.ops
        need = []
        for op in ops:
            lst = []
            for (di, kind) in op.deps:
                p = ops[di]
                if p.key is None and op.key is None and p.eng == op.eng:
                    if op.eng == "pe" or kind == 2:
                        continue
                lst.append(di)
                if p.key is None:
                    p.sig = True
            need.append(lst)
        cnt = {e: 0 for e in self.ENGS}
        for op in ops:
            if op.key is None and op.sig:
                cnt[op.eng] += 1
                op.ordv = cnt[op.eng]
        esem = {e: es.enter_context(nc.semaphore("tl_" + e)) for e in self.ENGS}
        ksem = {}
        for i, k in enumerate(self.key_count):
            ksem[k] = es.enter_context(nc.semaphore("dk%d" % i))
        per = {e: [] for e in self.ENGS}
        for op, lst in zip(ops, need):
            per[op.eng].append((op, lst))
        blk = es.enter_context(nc.Block())

        def run(ename, eobj):
            waited = {}
            for op, lst in per[ename]:
                req = {}
                for di in lst:
                    p = ops[di]
                    s = ksem[p.key] if p.key is not None else esem[p.eng]
                    sid = id(s)
                    if sid not in req or req[sid][1] < p.ordv:
                        req[sid] = (s, p.ordv)
                for sid, (s, v) in req.items():
                    if waited.get(sid, 0) < v:
                        eobj.wait_ge(s, v)
                        waited[sid] = v
                ins = op.fn(eobj)
                if op.key is not None:
                    if not isinstance(ins, (list, tuple)):
                        ins = [ins]
                    assert len(ins) == op.ndma, (len(ins), op.ndma)
                    for i_ in ins:
                        i_.then_inc(ksem[op.key], 16)
                elif op.sig:
                    ins.then_inc(esem[ename], 1)
            if ename == "sp":
                for k, c in self.key_count.items():
                    eobj.wait_ge(ksem[k], 16 * c)

        @blk.tensor
        def _(e):
            run("pe", e)

        @blk.scalar
        def _(e):
            run("act", e)

        @blk.vector
        def _(e):
            run("dve", e)

        @blk.gpsimd
        def _(e):
            run("pool", e)

        @blk.sync
        def _(e):
            run("sp", e)


class SeqCtx:
    pass


def build(cfg):
    nc = bass.Bass("TRN2", target_bir_lowering=False)
    NE, NO, DEPTH = cfg.NE, cfg.NO, cfg.DEPTH
    SEQ, PAST, DEC = cfg.SEQ, cfg.PAST, cfg.DEC
    NT = SEQ // TT
    NPT = PAST // TT
    TS = PAST + TT

    def din(name, shape, dt=F32):
        return nc.dram_tensor(name, list(shape), dt, kind="ExternalInput").ap()

    def dout(name, shape):
        return nc.dram_tensor(name, list(shape), F32, kind="ExternalOutput").ap()

    def dint(name, shape, dt):
        return nc.dram_tensor(name, list(shape), dt, kind="Internal").ap()

    xp = din("xp", [SEQ, D])
    xs = din("xs", [DEC, D])
    sca = din("sca", [NE, 30, DA])
    scc = din("scc", [max(NO, 1), 2, DA])
    ckv_in = din("ckv", [max(NO, 1), PAST, KVR])
    ckr_in = din("ckr", [max(NO, 1), PAST, ROPE])
    nrm = din("nrm", [2 * DEPTH + 1, D])
    w_in_even = din("w_in_even", [NE, D, 4096])
    w_out_even = din("w_out_even", [NE, D, D])
    w_in_odd = din("w_in_odd", [max(NO, 1), D, 4160])
    w_uq = din("w_uq", [max(NO, 1), QR, 3072])
    w_ukv = din("w_ukv", [max(NO, 1), KVR, 4096])
    w_out_odd = din("w_out_odd", [max(NO, 1), 3072, D])
    w_up = din("w_ffn_up", [DEPTH, D, DFF])
    w_down = din("w_ffn_down", [DEPTH, DFF, D])
    pcol_in = din("pcol", [128, 320 * NE + 24 * max(NO, 1)])
    lnv_in = din("lnv", [NE, 2, DA])
    qkg_in = din("qkg", [max(NO, 1), 2, 512])
    bsp_in = din("bsp", [NE, 1024])
    wsp_in = din("wsp", [NE, 8, 128, 128])
    invf_in = din("c_invf", [64, 1])
    mask_in = din("c_mask", [128, NSUB * TT])

    yp = dout("yp", [SEQ, D])
    ys = dout("ys", [DEC, D])
    cap = dout("cap", [NE, 30, DA])
    cas = dout("cas", [NE, 30, DA])
    gvs = dout("gvs", [NE, DEC, DA])
    ccp = dout("ccp", [max(NO, 1), 2, DA])
    ccs = dout("ccs", [max(NO, 1), 2, DA])
    latp = dout("latp", [max(NO, 1), SEQ, KVR])
    krp = dout("krp", [max(NO, 1), SEQ, ROPE])
    lats = dout("lats", [max(NO, 1), DEC, KVR])
    krs = dout("krs", [max(NO, 1), DEC, ROPE])

    KTd = {"p": dint("KT_p", [max(NO, 1), NH, 128, SEQ], BF16), "s": dint("KT_s", [max(NO, 1), NH, 128, TS], BF16)}
    Vd = {"p": dint("V_p", [max(NO, 1), NH, SEQ, 128], BF16), "s": dint("V_s", [max(NO, 1), NH, TS, 128], BF16)}
    dgs = dint("dgs", [NE, 4, 128, 16 * 512], BF16)
    KRd = {"p": dint("KR_p", [max(NO, 1), 64, SEQ], BF16), "s": dint("KR_s", [max(NO, 1), 64, TS], BF16)}

    P = Prog()
    es = ExitStack()

    def sb(name, shape, dt):
        return es.enter_context(nc.sbuf_tensor("s_" + name, list(shape), dt))

    X = sb("X", [128, NSUB, D], F32)
    gbc = sb("gbc", [128, D], F32)
    xn_tm = [sb("xn_tm%d" % i, [128, D], BF16) for i in range(2)]
    xnT = sb("xnT", [128, 16, TT], BF16)
    NWR = 3
    wr = [sb("wr%d" % i, [128, 16, 512], BF16) for i in range(NWR)]
    big = sb("big", [128, 24, TT], BF16)
    rscr = [sb("rscr%d" % i, [128, TT], BF16) for i in range(2)]
    aT = sb("aT", [128, 8, 30 + TT], F32)
    sg = sb("sg", [128, 4, TT], F32)
    acc = [sb("acc%d" % i, [128, TT], F32) for i in range(4)]
    aTb = sb("aTb", [128, 8, 30 + TT], BF16)
    convo = sb("convo", [128, 8, TT], BF16)
    sqb = [sb("sqb%d" % i, [128, TT], BF16) for i in range(2)]
    uT = sb("uT", [128, 8, TT], BF16)
    v_tm = sb("v_tm", [128, NSUB, DA], BF16)
    gv = [sb("gv%d" % i, [128, DA], F32) for i in range(2)]
    f1 = sb("f1", [128, TT], F32)
    f2 = sb("f2", [128, TT], F32)
    f3 = sb("f3", [128, TT], F32)
    stage = sb("stage", [128, DA], F32)
    ss = sb("ss", [128, 4], F32)
    rstd = sb("rstd", [128, 4], F32)
    bst = sb("bst", [128, 12], F32)
    mv = sb("mv", [128, 2], F32)
    lrs = sb("lrs", [128, 1], F32)
    pcol = sb("pcol", [128, 320 * NE + 24 * max(NO, 1)], F32)
    ones_f = sb("ones_f", [1, 128], F32)
    WsT = sb("WsT", [128, NE, 8, 128], BF16)
    haloA = sb("haloA", [128, NE, 8, 30], F32)
    haloC = sb("haloC", [128, max(NO, 1), 8, 2], F32)
    idf = sb("idf", [128, 128], F32)
    idb = sb("idb", [128, 128], BF16)
    ones_b = sb("ones_b", [128, 128], BF16)
    mask = sb("mask", [128, NSUB, TT], BF16)
    invf = sb("invf", [64, 1], F32)
    sgn = sb("sgn", [64, 1], F32)
    iota_f = sb("iota_f", [64, TT], F32)
    cosT = sb("cosT", [64, TT], F32)
    sinS = sb("sinS", [64, TT], F32)
    cosQ = sb("cosQ", [64, TT], F32)
    sinQ = sb("sinQ", [64, TT], F32)
    ri = sb("ri", [64, TT], I32)
    zqnT = sb("zqnT", [128, 4, TT], BF16)
    ckvT = sb("ckvT", [128, 4, TT], BF16)
    ztm = [sb("ztm%d" % i, [128, 512], F32) for i in range(2)]
    ztb = [sb("ztb%d" % i, [128, 512], BF16) for i in range(2)]
    krT_all = sb("krT_all", [128, TS], BF16)
    krf = sb("krf", [64, TT], F32)
    krb = sb("krb", [64, TT], BF16)
    qn = [sb("qn%d" % i, [128, TT], BF16) for i in range(2)]
    qrb = [sb("qrb%d" % i, [128, TT], BF16) for i in range(2)]
    NKV = 6
    kvK = [sb("kvK%d" % i, [128, TT], BF16) for i in range(NKV)]
    kvV = [sb("kvV%d" % i, [128, NSUB, 128], BF16) for i in range(NKV)]
    NPT_ = 3
    pt = [sb("pt%d" % i, [128, NSUB * TT], BF16) for i in range(NPT_)]
    NST = 8
    kst = [sb("kst%d" % i, [128, TT], BF16) for i in range(NST)]
    vst = [sb("vst%d" % i, [128, 256], BF16) for i in range(NST)]

    ps = [es.enter_context(nc.psum_tensor("ps%d" % i, [128, 512], F32)) for i in range(8)]
    psb = [p_[:].bitcast(BF16) for p_ in ps]

    cnt = {"fm": 0, "tr": 0, "wr": 0, "rs": 0, "acc": 0, "sq": 0, "gv": 0, "xn": 0, "zt": 0, "q": 0,
           "kv": 0, "pt": 0, "st": 0, "ob": 0, "sbk": 0}

    def rr(name, n):
        v = cnt[name] % n
        cnt[name] += 1
        return v

    def fm_bank():
        return 4 + rr("fm", 2)

    def tr_bank():
        return 6 + rr("tr", 2)

    def PS(b):
        return ("ps", b)

    def dma(q, out, in_, r, w, key):
        P.add(q, lambda e, o=out, i=in_: e.dma_start(out=o, in_=i), r=r, w=w, key=key)

    LOOKAHEAD = 6
    pst = {"order": [], "index": {}, "ptr": 0, "dry": True, "wsc": None}

    def _emit_conv(i):
        (srcs, nk) = pst["order"][i]
        view = pst["wsc"][i // 64][i % 64].rearrange("p (k c) -> p k c", c=512)

        def fn(e, srcs=srcs, nk=nk, view=view):
            out = []
            for (s_, c0, ncl) in srcs:
                out.append(e.dma_start(out=view[:, 0:nk, c0:c0 + ncl], in_=s_.rearrange("(k p) c -> p k c", p=128)))
            return out
        P.add("pool", fn, r=(), w=[("wsc", i)], key=("cv", i % (LOOKAHEAD + 2)), ndma=len(srcs))

    def wpiece(src, nk, ncol, dst_col=0, slot=None, extra=None, pid=None):
        if slot is None:
            slot = rr("wr", NWR)
        srcs = [(src, dst_col, ncol)] + (extra or [])
        assert pid is not None
        if pst["dry"]:
            if pid not in pst["index"]:
                pst["index"][pid] = len(pst["order"])
                pst["order"].append((srcs, nk))
            return slot
        i = pst["index"][pid]
        tgt = min(i + LOOKAHEAD, len(pst["order"]) - 1)
        while pst["ptr"] <= tgt:
            _emit_conv(pst["ptr"])
            pst["ptr"] += 1
        view = pst["wsc"][i // 64][i % 64].rearrange("p (k c) -> p k c", c=512)
        P.add("pool", lambda e, slot=slot, nk=nk, view=view: e.dma_start(out=wr[slot][:, 0:nk, :], in_=view[:, 0:nk, :]),
              r=[("wsc", i)], w=[("wr", slot)], key=("wr", slot))
        return slot

    def mm(out, lhsT, rhs, start, stop, r, w):
        P.add("pe", lambda e, o=out, l=lhsT, rh=rhs, s=start, t=stop: e.matmul(o, lhsT=l, rhs=rh, start=s, stop=t),
              r=r, w=w)

    def tp(out, in_, ident, r, w):
        P.add("pe", lambda e, o=out, i=in_, d=ident: e.transpose(out=o, in_=i, identity=d), r=r, w=w)

    def act(out, in_, func, r, w, **kw):
        P.add("act", lambda e, o=out, i=in_, f=func, kw=kw: e.activation(out=o, in_=i, func=f, **kw), r=r, w=w)

    def tt(out, in0, in1, op, r, w, eng="dve"):
        P.add(eng, lambda e, o=out, a=in0, b=in1, p_=op: e.tensor_tensor(out=o, in0=a, in1=b, op=p_), r=r, w=w)

    def ts(out, in0, s1, s2, op0, op1, r, w, eng="dve"):
        if s2 is None:
            P.add(eng, lambda e, o=out, a=in0, s1=s1, p0=op0: e.tensor_scalar(out=o, in0=a, scalar1=s1, scalar2=None, op0=p0),
                  r=r, w=w)
        else:
            P.add(eng, lambda e, o=out, a=in0, s1=s1, s2=s2, p0=op0, p1=op1:
                  e.tensor_scalar(out=o, in0=a, scalar1=s1, scalar2=s2, op0=p0, op1=p1), r=r, w=w)

    def stt(out, in0, sc, in1, op0, op1, r, w, eng="dve"):
        P.add(eng, lambda e, o=out, a=in0, s=sc, b=in1, p0=op0, p1=op1:
              e.scalar_tensor_tensor(out=o, in0=a, scalar=s, in1=b, op0=p0, op1=p1), r=r, w=w)

    def cp(out, in_, r, w, eng="dve"):
        P.add(eng, lambda e, o=out, i=in_: e.tensor_copy(out=o, in_=i), r=r, w=w)

    def ms(ap, val, w, eng="dve"):
        P.add(eng, lambda e, a=ap, v=val: e.memset(a, v), r=(), w=w)

    def recip(out, in_, r, w):
        P.add("dve", lambda e, o=out, i=in_: e.reciprocal(out=o, in_=i), r=r, w=w)

    def rstd_chain(dst, src, n_inv, pp, ncols, r, w):
        ts(dst[0:pp, 0:ncols], src, n_inv, EPS, ALU.mult, ALU.add, r=r, w=w)
        act(dst[0:pp, 0:ncols], dst[0:pp, 0:ncols], AF.Sqrt, r=w, w=w)
        recip(dst[0:pp, 0:ncols], dst[0:pp, 0:ncols], r=w, w=w)

    def setup():
        dma("sp", pcol[:], pcol_in[:, :], r=(), w=["pcol"], key="ld_pcol")
        dma("sp", invf[:], invf_in[:, :], r=(), w=["invf"], key="ld_misc")
        dma("sp", f1[:], mask_in[:, 0:TT], r=(), w=["f1"], key="ld_misc")
        ms(idf[:], 0.0, w=["idf"], eng="pool")
        P.add("pool", lambda e: e.affine_select(out=idf[:], in_=idf[:], pattern=[[-1, 128]], compare_op=ALU.not_equal,
                                                fill=1.0, base=0, channel_multiplier=1), r=["idf"], w=["idf"])
        P.add("pool", lambda e: e.iota(ri[:], pattern=[[1, TT]], base=0, channel_multiplier=0), r=(), w=["ri"])
        cp(idb[:], idf[:], r=["idf"], w=["idb"])
        cp(iota_f[:], ri[:], r=["ri"], w=["iota_f"])
        ms(ones_b[:], 1.0, w=["ones_b"])
        ms(ones_f[:], 1.0, w=["ones_f"])
        ms(sgn[0:32, :], -1.0, w=["sgn"])
        ms(sgn[32:64, :], 1.0, w=["sgn"])
        ms(haloA[:], 0.0, w=["haloA"])
        ms(haloC[:], 0.0, w=["haloC"])
        ms(krT_all[64:128, :], 0.0, w=["krT_all"])
        for i_ in range(2):
            ms(qrb[i_][64:128, :], 0.0, w=[("qrb", i_)])
        for kb in range(NSUB):
            if kb > 0:
                dma("sp", f1[:], mask_in[:, kb * TT:(kb + 1) * TT], r=(), w=["f1"], key="ld_misc")
            cp(mask[:, kb, :], f1[:], r=["f1"], w=["mask"])
        for p_ in range(NE):
            for g in range(8):
                dma("sp", f2[:, 0:128], wsp_in[p_, g], r=(), w=["f2"], key="ld_misc")
                P.add("pool", lambda e: e.affine_select(out=f2[:, 0:128], in_=f2[:, 0:128], pattern=[[-1, 128]],
                                                        compare_op=ALU.is_ge, fill=0.0, base=0, channel_multiplier=1),
                      r=["f2"], w=["f2"])
                b = tr_bank()
                tp(ps[b][:, 0:128], f2[:, 0:128], idf[:], r=["f2", "idf"], w=[PS(b)])
                act(WsT[:, p_, g, :], ps[b][:, 0:128], AF.Copy, r=[PS(b)], w=["WsT"])

    def build_diag():
        stg = big[:, :, :].rearrange("p a b -> p (a b)")
        for p_ in range(NE):
            cw = 320 * p_
            for ch in range(8):
                for j in range(CONVA):
                    ts(stg[:, j * 128:(j + 1) * 128], idb[:], pcol[:, cw + ch * 31 + j:cw + ch * 31 + j + 1], None, ALU.mult, None,
                       r=["idb", "pcol"], w=[("big", k_) for k_ in range(16)])
                dma("sp", dgs[p_, ch // 2][:, (ch % 2) * 4096:(ch % 2) * 4096 + CONVA * 128], stg[:, 0:CONVA * 128],
                    r=[("big", k_) for k_ in range(16)], w=[("dgs", p_, ch // 2, ch % 2)], key="st_dg")

    def pc_even(p_):
        base = 320 * p_
        return dict(cw=base, cb=base + 248, lg=base + 256, lb=base + 264)

    def pc_odd(p_):
        return 320 * NE + 24 * p_

    def rmsnorm_to_xnT(c, gidx):
        pp, nsub, N = c.pp, c.nsub, c.N
        dma("sp", gbc[:], nrm[gidx].partition_broadcast(128), r=(), w=["gbc"], key="ld_gbc")
        ms(ss[:], 0.0, w=["ss"])
        for s in range(nsub):
            i = rr("xn", 2)
            act(xn_tm[i][0:pp, :], X[0:pp, s, :], AF.Square, r=[("X", s), "ss"], w=[("xn_tm", i), "ss"],
                accum_out=ss[0:pp, s:s + 1])
        rstd_chain(rstd, ss[0:pp, 0:nsub], 1.0 / D, pp, nsub, r=["ss"], w=["rstd"])
        for s in range(nsub):
            i = rr("xn", 2)
            stt(xn_tm[i][0:pp, :], X[0:pp, s, :], rstd[0:pp, s:s + 1], gbc[0:pp, :], ALU.mult, ALU.mult,
                r=[("X", s), "rstd", "gbc"], w=[("xn_tm", i)])
            for half in range(2):
                b = tr_bank()
                for k in range(8):
                    kk = half * 8 + k
                    tp(psb[b][:, k * pp:(k + 1) * pp], xn_tm[i][0:pp, kk * 128:(kk + 1) * 128], idb[0:pp, 0:pp],
                       r=[("xn_tm", i), "idb"], w=[PS(b)])
                act(xnT[:, half * 8:half * 8 + 8, s * 128:s * 128 + pp],
                    psb[b][:, 0:8 * pp].rearrange("p (k t) -> p k t", k=8), AF.Copy,
                    r=[PS(b)], w=[("xnT", s)])

    def xnT_res(c):
        return [("xnT", s) for s in range(c.nsub)]

    def fm_group(c, slot, nk, col, M, rhs_t, rhs_res, b, prow=0):
        N = c.N
        for k in range(nk):
            mm(ps[b][prow:prow + M, 0:N], wr[slot][:, k, col:col + M], rhs_t[:, k, 0:N], k == 0, k == nk - 1,
               r=[("wr", slot)] + rhs_res, w=[PS(b)])

    def residual_add_from(c, s, nb, b):
        pp = c.pp
        tt(X[0:pp, s, nb * 512:(nb + 1) * 512], X[0:pp, s, nb * 512:(nb + 1) * 512], ps[b][0:pp, :], ALU.add,
           r=[("X", s), PS(b)], w=[("X", s)])

    def out_proj(c, W, nkc, wname):
        pp, nsub = c.pp, c.nsub
        bigres = [("big", k) for k in range(nkc)]
        halves = [(0, nkc)] if nkc <= 16 else [(0, nkc // 2), (nkc // 2, nkc)]
        for nb in range(4):
            slots = []
            for (k0, k1) in halves:
                slots.append(wpiece(W[k0 * 128:k1 * 128, nb * 512:(nb + 1) * 512], k1 - k0, 512, pid=(wname, nb, k0)))
            for s in range(nsub):
                for hi, (k0, k1) in enumerate(halves):
                    for k in range(k0, k1):
                        mm(ps[s][0:pp, :], big[:, k, s * 128:s * 128 + pp], wr[slots[hi]][:, k - k0, :],
                           k == 0, k == nkc - 1, r=[("wr", slots[hi]), ("big", k)], w=[PS(s)])
                residual_add_from(c, s, nb, s)

    def ffn(c, l):
        pp, nsub, N = c.pp, c.nsub, c.N
        for q in range(4):
            for j in range(4):
                slot = wpiece(w_up[l, :, q * 2048 + j * 512: q * 2048 + (j + 1) * 512], 16, 512, pid=("up", l, q, j))
                for m in range(4):
                    hc = j * 4 + m
                    b = fm_bank()
                    fm_group(c, slot, 16, m * 128, 128, xnT, xnT_res(c), b)
                    i = rr("rs", 2)
                    act(rscr[i][:, 0:N], ps[b][:, 0:N], AF.Relu, r=[PS(b)], w=[("rscr", i)])
                    act(big[:, hc, 0:N], rscr[i][:, 0:N], AF.Square, r=[("rscr", i)], w=[("big", hc)])
            for nb in range(4):
                slot = wpiece(w_down[l, q * 2048:(q + 1) * 2048, nb * 512:(nb + 1) * 512], 16, 512, pid=("dn", l, q, nb))
                for s in range(nsub):
                    for k in range(16):
                        mm(ps[s][0:pp, :], big[:, k, s * 128:s * 128 + pp], wr[slot][:, k, :], k == 0, k == 15,
                           r=[("wr", slot), ("big", k)], w=[PS(s)])
                    residual_add_from(c, s, nb, s)

    def save_state_rows(c, src3, ncols_state, col0, out_ap):
        n = ncols_state
        for half in range(2):
            b = tr_bank()
            for k in range(4):
                ch = half * 4 + k
                tp(ps[b][0:n, k * 128:(k + 1) * 128], src3[:, ch, col0:col0 + n], idf[:], r=["aT", "idf"], w=[PS(b)])
            act(stage[0:n, half * 512:(half + 1) * 512], ps[b][0:n, :], AF.Copy, r=[PS(b)], w=["stage"])
        dma("sp", out_ap, stage[0:n, :], r=["stage"], w=(), key="st_stage")

    def load_state_rows(c, in_ap, n, dst3):
        dma("sp", stage[0:n, :], in_ap, r=(), w=["stage"], key="ld_stage")
        b = tr_bank()
        for ch in range(8):
            tp(ps[b][:, ch * n:(ch + 1) * n], stage[0:n, ch * 128:(ch + 1) * 128], idf[0:n, 0:n],
               r=["stage", "idf"], w=[PS(b)])
        act(dst3[:, :, 0:n], ps[b][:, 0:8 * n].rearrange("p (k t) -> p k t", k=8), AF.Copy, r=[PS(b)], w=["aT"])

    def even_mixer(c, p_, last):
        pp, nsub, N = c.pp, c.nsub, c.N
        W = w_in_even[p_]
        pc = pc_even(p_)
        H = 30
        if c.name == "s":
            load_state_rows(c, sca[p_], H, aT)
        else:
            cp(aT[:, :, 0:H], haloA[:, p_, :, :], r=["haloA"], w=["aT"])
        for grp in range(2):
            slot = wpiece(W[:, 1024 + grp * 512:1024 + (grp + 1) * 512], 16, 512, pid=("wie", p_, 2 + grp))
            for m in range(4):
                b = fm_bank()
                fm_group(c, slot, 16, m * 128, 128, xnT, xnT_res(c), b)
                act(sg[:, m, 0:N], ps[b][:, 0:N], AF.Sigmoid, r=[PS(b)], w=[("sg", m)])
            slot = wpiece(W[:, grp * 512:(grp + 1) * 512], 16, 512, pid=("wie", p_, grp))
            for m in range(4):
                ch = grp * 4 + m
                b = fm_bank()
                fm_group(c, slot, 16, m * 128, 128, xnT, xnT_res(c), b)
                tt(aT[:, ch, H:H + N], ps[b][:, 0:N], sg[:, m, 0:N], ALU.mult, r=[PS(b), ("sg", m)], w=["aT"])
                act(aTb[:, ch, 0:H + N], aT[:, ch, 0:H + N], AF.Copy, r=["aT"], w=[("aTb", ch)])
        SUMB, SQB = 2, 3
        dslot = None
        for ch in range(8):
            if ch % 2 == 0:
                dslot = rr("wr", NWR)
                P.add("pool", lambda e, slot=dslot, pr=ch // 2: e.dma_start(out=wr[slot][:, :, :], in_=dgs[p_, pr].rearrange("p (k c) -> p k c", c=512)),
                      r=[("dgs", p_, ch // 2, 0), ("dgs", p_, ch // 2, 1)], w=[("wr", dslot)], key=("wr", dslot))
            b = fm_bank()
            for j in range(CONVA):
                off = (ch % 2) * 4096 + j * 128
                mm(ps[b][:, 0:N], wr[dslot][:, off // 512, off % 512:off % 512 + 128], aTb[:, ch, j:j + N], j == 0, j == CONVA - 1,
                   r=[("wr", dslot), ("aTb", ch)], w=[PS(b)])
            act(convo[:, ch, 0:N], ps[b][:, 0:N], AF.Identity, r=[PS(b), "pcol"], w=[("convo", ch)],
                bias=pcol[:, pc["cb"] + ch:pc["cb"] + ch + 1])
            si = rr("sq", 2)
            act(sqb[si][:, 0:N], ps[b][:, 0:N], AF.Square, r=[PS(b), "pcol"], w=[("sqb", si)],
                bias=pcol[:, pc["cb"] + ch:pc["cb"] + ch + 1])
            mm(ps[SUMB][:, 0:N], ones_b[:], convo[:, ch, 0:N], ch == 0, ch == 7, r=["ones_b", ("convo", ch)], w=[PS(SUMB)])
            mm(ps[SQB][:, 0:N], ones_b[:], sqb[si][:, 0:N], ch == 0, ch == 7, r=["ones_b", ("sqb", si)], w=[PS(SQB)])
        for grp in range(2):
            slot = wpiece(W[:, 2048 + grp * 512:2048 + (grp + 1) * 512], 16, 512, pid=("wie", p_, 4 + grp))
            for m in range(4):
                ch = grp * 4 + m
                b = fm_bank()
                fm_group(c, slot, 16, m * 128, 128, xnT, xnT_res(c), b)
                act(uT[:, ch, 0:N], ps[b][:, 0:N], AF.Gelu, r=[PS(b)], w=[("uT", ch)])
        vslots = [wpiece(W[:, 3072 + j * 512:3072 + (j + 1) * 512], 16, 512, pid=("wie", p_, 6 + j)) for j in range(2)]
        dma("sp", gbc[:], lnv_in[p_].rearrange("a b -> (a b)").partition_broadcast(128), r=(), w=["gbc"], key="ld_gbc")
        for s in range(nsub):
            gi = rr("gv", 2)
            for j in range(2):
                b = j
                for k in range(16):
                    mm(ps[b][0:pp, :], xnT[:, k, s * 128:s * 128 + pp], wr[vslots[j]][:, k, :], k == 0, k == 15,
                       r=[("wr", vslots[j]), ("xnT", s)], w=[PS(b)])
                act(gv[gi][0:pp, j * 512:(j + 1) * 512], ps[b][0:pp, :], AF.Gelu, r=[PS(b)], w=[("gv", gi)])
            for j in range(2):
                P.add("dve", lambda e, gi=gi, j=j, pp=pp: e.bn_stats(out=bst[0:pp, j * 6:(j + 1) * 6], in_=gv[gi][0:pp, j * 512:(j + 1) * 512]),
                      r=[("gv", gi)], w=["bst"])
            P.add("dve", lambda e, pp=pp: e.bn_aggr(out=mv[0:pp, :], in_=bst[0:pp, :]), r=["bst"], w=["mv"])
            rstd_chain(lrs, mv[0:pp, 1:2], 1.0, pp, 1, r=["mv"], w=["lrs"])
            stt(gv[gi][0:pp, :], gv[gi][0:pp, :], mv[0:pp, 0:1], gbc[0:pp, 0:DA], ALU.subtract, ALU.mult,
                r=[("gv", gi), "mv", "gbc"], w=[("gv", gi)])
            stt(gv[gi][0:pp, :], gv[gi][0:pp, :], lrs[0:pp, 0:1], gbc[0:pp, DA:2 * DA], ALU.mult, ALU.add,
                r=[("gv", gi), "lrs", "gbc"], w=[("gv", gi)])
            act(v_tm[0:pp, s, :], gv[gi][0:pp, :], AF.Copy, r=[("gv", gi)], w=[("v_tm", s)])
            if c.name == "s":
                dma("sp", gvs[p_], gv[gi][0:pp, :], r=[("gv", gi)], w=(), key=("st_gv", gi))
        if c.name == "s":
            save_state_rows(c, aT, H, N, cas[p_])
        else:
            if last:
                save_state_rows(c, aT, H, N, cap[p_])
            cp(haloA[:, p_, :, :], aT[:, :, N:N + H], r=["aT"], w=["haloA"])
        ts(f1[:, 0:N], ps[SUMB][:, 0:N], 1.0 / DA, None, ALU.mult, None, r=[PS(SUMB)], w=["f1"])
        tt(f3[:, 0:N], f1[:, 0:N], f1[:, 0:N], ALU.mult, r=["f1"], w=["f3"])
        stt(f2[:, 0:N], ps[SQB][:, 0:N], 1.0 / DA, f3[:, 0:N], ALU.mult, ALU.subtract, r=[PS(SQB), "f3"], w=["f2"])
        ts(f2[:, 0:N], f2[:, 0:N], 0.0, EPS, ALU.max, ALU.add, r=["f2"], w=["f2"])
        act(f2[:, 0:N], f2[:, 0:N], AF.Sqrt, r=["f2"], w=["f2"])
        recip(f2[:, 0:N], f2[:, 0:N], r=["f2"], w=["f2"])
        for ch in range(8):
            ai = rr("acc", 4)
            a_ = acc[ai]
            tt(a_[:, 0:N], convo[:, ch, 0:N], f1[:, 0:N], ALU.subtract, r=[("convo", ch), "f1"], w=[("acc", ai)])
            tt(a_[:, 0:N], a_[:, 0:N], f2[:, 0:N], ALU.mult, r=[("acc", ai), "f2"], w=[("acc", ai)])
            act(big[:, ch, 0:N], a_[:, 0:N], AF.Silu, r=[("acc", ai), "pcol"], w=[("big", ch)],
                scale=pcol[:, pc["lg"] + ch:pc["lg"] + ch + 1], bias=pcol[:, pc["lb"] + ch:pc["lb"] + ch + 1])
        L = pp
        dma("sp", stage[0:1, :], bsp_in[p_:p_ + 1, :], r=(), w=["stage"], key="ld_stage")
        for g in range(8):
            b = fm_bank()
            for s in range(nsub):
                mm(ps[b][:, s * 128:s * 128 + L], v_tm[0:L, s, g * 128:(g + 1) * 128], WsT[0:L, p_, g, 0:L], True, False,
                   r=[("v_tm", s), "WsT"], w=[PS(b)])
                mm(ps[b][:, s * 128:s * 128 + L], ones_f[0:1, :], stage[0:1, g * 128:g * 128 + L], False, True,
                   r=["ones_f", "stage"], w=[PS(b)])
            tt(big[:, 8 + g, 0:N], ps[b][:, 0:N], uT[:, g, 0:N], ALU.mult, r=[PS(b), ("uT", g)], w=[("big", 8 + g)])
        out_proj(c, w_out_even[p_], 16, ("woe", p_))

    def rope_tables(c):
        N = c.N
        pos0 = float(c.pos0)
        ts(krf[:, 0:N], iota_f[:, 0:N], pos0, None, ALU.add, None, r=["iota_f"], w=["krf"])
        ts(krf[:, 0:N], krf[:, 0:N], invf[:, 0:1], None, ALU.mult, None, r=["krf", "invf"], w=["krf"])
        for (dst, phase) in ((cosT, 0.25), (sinS, 0.0)):
            nm = "cosT" if dst is cosT else "sinS"
            ts(f1[0:64, 0:N], krf[:, 0:N], 1.0 / TWO_PI, phase, ALU.mult, ALU.add, r=["krf"], w=["f1"])
            cp(ri[:, 0:N], f1[0:64, 0:N], r=["f1"], w=["ri"])
            cp(f2[0:64, 0:N], ri[:, 0:N], r=["ri"], w=["f2"])
            tt(f1[0:64, 0:N], f1[0:64, 0:N], f2[0:64, 0:N], ALU.subtract, r=["f1", "f2"], w=["f1"])
            stt(f2[0:64, 0:N], f1[0:64, 0:N], 0.5, f1[0:64, 0:N], ALU.is_gt, ALU.subtract, r=["f1"], w=["f2"])
            stt(f1[0:64, 0:N], f1[0:64, 0:N], -0.5, f2[0:64, 0:N], ALU.is_lt, ALU.subtract, r=["f1", "f2"], w=["f1"])
            act(dst[:, 0:N], f1[0:64, 0:N], AF.Sin, r=["f1"], w=[nm], scale=TWO_PI)
        ts(sinS[:, 0:N], sinS[:, 0:N], sgn[:, 0:1], None, ALU.mult, None, r=["sinS", "sgn"], w=["sinS"])
        ts(cosQ[:, 0:N], cosT[:, 0:N], QSCALE, None, ALU.mult, None, r=["cosT"], w=["cosQ"])
        ts(sinQ[:, 0:N], sinS[:, 0:N], QSCALE, None, ALU.mult, None, r=["sinS"], w=["sinQ"])

    def kv_up(c_name, p_, srcT, src_res, pos0, N, pp, nsub):
        Wk = w_ukv[p_]
        for pi in range(8):
            slot = wpiece(Wk[:, pi * 512:(pi + 1) * 512], 4, 512, pid=("ukv", p_, pi))
            for hh in range(2):
                h = 2 * pi + hh
                b = fm_bank()
                for k in range(4):
                    mm(ps[b][:, 0:N], wr[slot][:, k, hh * 256:hh * 256 + 128], srcT[:, k, 0:N], k == 0, k == 3,
                       r=[("wr", slot)] + src_res, w=[PS(b)])
                si = rr("st", NST)
                act(kst[si][:, 0:N], ps[b][:, 0:N], AF.Copy, r=[PS(b)], w=[("kst", si)])
                dma("sp", KTd[c_name][p_, h, :, pos0:pos0 + N], kst[si][:, 0:N], r=[("kst", si)],
                    w=[("KT", c_name, p_, h, pos0 // TT)], key=("st_k", si))
            for s in range(nsub):
                b = s
                for k in range(4):
                    mm(ps[b][0:pp, 0:256].rearrange("p (h d) -> p h d", h=2), srcT[:, k, s * 128:s * 128 + pp],
                       wr[slot][:, k, :].rearrange("p (h t d) -> p h t d", h=2, t=2)[:, :, 1, :],
                       k == 0, k == 3, r=[("wr", slot)] + src_res, w=[PS(b)])
                si = rr("st", NST)
                act(vst[si][0:pp, :], ps[b][0:pp, 0:256], AF.Copy, r=[PS(b)], w=[("vst", si)])
                for hh in range(2):
                    h = 2 * pi + hh
                    dma("sp", Vd[c_name][p_, h, pos0 + s * 128:pos0 + s * 128 + pp, :], vst[si][0:pp, hh * 128:(hh + 1) * 128],
                        r=[("vst", si)], w=[("V", c_name, p_, h, pos0 // TT, s)], key=("st_v", si, hh))

    def odd_mixer(c, p_, last):
        pp, nsub, N = c.pp, c.nsub, c.N
        W = w_in_odd[p_]
        pco = pc_odd(p_)
        H = 2
        rope_tables(c)
        if c.name == "s":
            load_state_rows(c, scc[p_], H, aT)
        else:
            cp(aT[:, :, 0:H], haloC[:, p_, :, :], r=["haloC"], w=["aT"])
        for grp in range(2):
            slot = wpiece(W[:, 1024 + grp * 512:1024 + (grp + 1) * 512], 16, 512, pid=("wio", p_, 2 + grp))
            for m in range(4):
                b = fm_bank()
                fm_group(c, slot, 16, m * 128, 128, xnT, xnT_res(c), b)
                act(sg[:, m, 0:N], ps[b][:, 0:N], AF.Copy, r=[PS(b)], w=[("sg", m)])
            slot = wpiece(W[:, 2048 + grp * 512:2048 + (grp + 1) * 512], 16, 512, pid=("wio", p_, 4 + grp))
            for m in range(4):
                ch = grp * 4 + m
                b = fm_bank()
                fm_group(c, slot, 16, m * 128, 128, xnT, xnT_res(c), b)
                tt(aT[:, ch, H:H + N], ps[b][:, 0:N], sg[:, m, 0:N], ALU.mult, r=[PS(b), ("sg", m)], w=["aT"])
        for ch in range(8):
            ai = rr("acc", 4)
            a_ = acc[ai]
            ts(a_[:, 0:N], aT[:, ch, 0:N], pcol[:, pco + ch * 3:pco + ch * 3 + 1], None, ALU.mult, None,
               r=["aT", "pcol"], w=[("acc", ai)])
            stt(a_[:, 0:N], aT[:, ch, 1:1 + N], pcol[:, pco + ch * 3 + 1:pco + ch * 3 + 2], a_[:, 0:N], ALU.mult, ALU.add,
                r=["aT", "pcol", ("acc", ai)], w=[("acc", ai)])
            stt(convo[:, ch, 0:N], aT[:, ch, 2:2 + N], pcol[:, pco + ch * 3 + 2:pco + ch * 3 + 3], a_[:, 0:N], ALU.mult, ALU.add,
                r=["aT", "pcol", ("acc", ai)], w=[("convo", ch)])
        if c.name == "s":
            save_state_rows(c, aT, H, N, ccs[p_])
        else:
            if last:
                save_state_rows(c, aT, H, N, ccp[p_])
            cp(haloC[:, p_, :, :], aT[:, :, N:N + H], r=["aT"], w=["haloC"])
        for grp in range(2):
            slot = wpiece(W[:, grp * 512:(grp + 1) * 512], 16, 512, pid=("wio", p_, grp))
            for m in range(4):
                ch = grp * 4 + m
                b = fm_bank()
                fm_group(c, slot, 16, m * 128, 128, xnT, xnT_res(c), b)
                tt(big[:, ch, 0:N], ps[b][:, 0:N], convo[:, ch, 0:N], ALU.mult, r=[PS(b), ("convo", ch)], w=[("big", ch)])
        dma("sp", gbc[:, 0:1024], qkg_in[p_].rearrange("a b -> (a b)").partition_broadcast(128), r=(), w=["gbc"], key="ld_gbc")
        for which in range(2):
            slot = wpiece(W[:, 3072 + which * 512:3072 + (which + 1) * 512], 16, 512, pid=("wio", p_, 6 + which))
            dstT = zqnT if which == 0 else ckvT
            dname = "zqnT" if which == 0 else "ckvT"
            ms(ss[:], 0.0, w=["ss"])
            for s in range(nsub):
                b = s
                for k in range(16):
                    mm(ps[b][0:pp, :], xnT[:, k, s * 128:s * 128 + pp], wr[slot][:, k, :], k == 0, k == 15,
                       r=[("wr", slot), ("xnT", s)], w=[PS(b)])
                zi = rr("zt", 2)
                act(ztm[zi][0:pp, :], ps[b][0:pp, :], AF.Square, r=[PS(b), "ss"], w=[("ztm", zi), "ss"],
                    accum_out=ss[0:pp, s:s + 1])
            rstd_chain(rstd, ss[0:pp, 0:nsub], 1.0 / 512, pp, nsub, r=["ss"], w=["rstd"])
            for s in range(nsub):
                b = s
                zi = rr("zt", 2)
                stt(ztm[zi][0:pp, :], ps[b][0:pp, :], rstd[0:pp, s:s + 1], gbc[0:pp, which * 512:(which + 1) * 512], ALU.mult, ALU.mult,
                    r=[PS(b), "rstd", "gbc"], w=[("ztm", zi)])
                if which == 1:
                    out_ap = (lats[p_] if c.name == "s" else latp[p_, c.pos0 + s * 128:c.pos0 + s * 128 + pp, :])
                    dma("sp", out_ap, ztm[zi][0:pp, :], r=[("ztm", zi)], w=(), key=("st_lat", zi))
                act(ztb[zi][0:pp, :], ztm[zi][0:pp, :], AF.Copy, r=[("ztm", zi)], w=[("ztb", zi)])
                tb = tr_bank()
                for k in range(4):
                    tp(psb[tb][:, k * pp:(k + 1) * pp], ztb[zi][0:pp, k * 128:(k + 1) * 128], idb[0:pp, 0:pp],
                       r=[("ztb", zi), "idb"], w=[PS(tb)])
                act(dstT[:, :, s * 128:s * 128 + pp], psb[tb][:, 0:4 * pp].rearrange("p (k t) -> p k t", k=4), AF.Copy,
                    r=[PS(tb)], w=[dname])
        slot = wpiece(W[:, 4096:4160], 16, 64, extra=[(W[:, 4128:4160], 64, 32), (W[:, 4096:4128], 96, 32)], pid=("wio", p_, 8))
        bA = fm_bank()
        fm_group(c, slot, 16, 0, 64, xnT, xnT_res(c), bA)
        bB = fm_bank()
        fm_group(c, slot, 16, 64, 64, xnT, xnT_res(c), bB)
        tt(f1[0:64, 0:N], ps[bA][0:64, 0:N], cosT[:, 0:N], ALU.mult, r=[PS(bA), "cosT"], w=["f1"])
        tt(f2[0:64, 0:N], ps[bB][0:64, 0:N], sinS[:, 0:N], ALU.mult, r=[PS(bB), "sinS"], w=["f2"])
        tt(krf[:, 0:N], f1[0:64, 0:N], f2[0:64, 0:N], ALU.add, r=["f1", "f2"], w=["krf"])
        act(krb[:, 0:N], krf[:, 0:N], AF.Copy, r=["krf"], w=["krb"])
        dma("sp", KRd[c.name][p_, :, c.pos0:c.pos0 + N], krb[:, 0:N], r=["krb"], w=[("KR", c.name, p_, c.pos0 // TT)], key="st_kr")
        tb = tr_bank()
        for s in range(nsub):
            tp(ps[tb][0:pp, s * 64:(s + 1) * 64], krf[:, s * 128:s * 128 + pp], idf[0:64, 0:64], r=["krf", "idf"], w=[PS(tb)])
        act(stage[0:pp, 0:nsub * 64], ps[tb][0:pp, 0:nsub * 64], AF.Copy, r=[PS(tb)], w=["stage"])
        if c.name == "s":
            dma("sp", krs[p_], stage[0:pp, 0:64], r=["stage"], w=(), key="st_stage")
        else:
            dma("sp", krp[p_, c.pos0:c.pos0 + N, :].rearrange("(s p) r -> p s r", p=128),
                stage[0:pp, 0:nsub * 64].rearrange("p (s r) -> p s r", s=nsub), r=["stage"], w=(), key="st_stage")
        kv_up(c.name, p_, ckvT, ["ckvT"], c.pos0, N, pp, nsub)
        nkeys = c.pos0 + N
        dma("sp", krT_all[0:64, 0:nkeys], KRd[c.name][p_, :, 0:nkeys], r=[("KR", c.name, p_, kt_) for kt_ in range(c.pos0 // TT + 1)], w=["krT_all"], key="ld_krT")
        Wq = w_uq[p_]
        nkt_full = c.pos0 // TT
        qstate = {}

        def emit_q(h):
            if h % 2 == 0:
                ex = []
                for hh in range(2):
                    o_ = hh * 256
                    src0 = (h + hh) * 192
                    if hh == 1:
                        ex.append((Wq[:, src0:src0 + 192], o_, 192))
                    ex.append((Wq[:, src0 + 160:src0 + 192], o_ + 192, 32))
                    ex.append((Wq[:, src0 + 128:src0 + 160], o_ + 224, 32))
                qstate["slot"] = wpiece(Wq[:, h * 192:h * 192 + 192], 4, 192, extra=ex, pid=("uq", p_, h))
            qslot = qstate["slot"]
            o = (h % 2) * 256
            qi = rr("q", 2)
            b = 7
            fm_group(c, qslot, 4, o, 128, zqnT, ["zqnT"], b)
            act(qn[qi][:, 0:N], ps[b][:, 0:N], AF.Identity, r=[PS(b)], w=[("qn", qi)], scale=QSCALE)
            bA = 7
            fm_group(c, qslot, 4, o + 128, 64, zqnT, ["zqnT"], bA)
            tt(f1[0:64, 0:N], ps[bA][0:64, 0:N], cosQ[:, 0:N], ALU.mult, r=[PS(bA), "cosQ"], w=["f1"])
            bB = 7
            fm_group(c, qslot, 4, o + 192, 64, zqnT, ["zqnT"], bB)
            tt(f2[0:64, 0:N], ps[bB][0:64, 0:N], sinQ[:, 0:N], ALU.mult, r=[PS(bB), "sinQ"], w=["f2"])
            tt(qrb[qi][0:64, 0:N], f1[0:64, 0:N], f2[0:64, 0:N], ALU.add, r=["f1", "f2"], w=[("qrb", qi)])
            return qi

        units = []
        for kt in range(nkt_full):
            units.append((kt, list(range(NSUB)), 128, False))
        if c.name == "p":
            units.append((nkt_full, list(range(NSUB)), 128, True))
        else:
            units.append((nkt_full, [0], N, False))
        nun = len(units)
        q_next = emit_q(0)
        for h in range(NH):
            qi = q_next
            ob = rr("ob", 2)
            OB, RB = ob, 2 + ob
            slots = {}

            def load_kt(kt, h=h):
                ki = rr("kv", NKV)
                kn = TT if (c.name == "p" or kt < nkt_full) else N
                dma("sp", kvK[ki][:, 0:kn], KTd[c.name][p_, h, :, kt * TT:kt * TT + kn], r=[("KT", c.name, p_, h, kt)],
                    w=[("kvK", ki)], key=("ld_k", ki))
                if kn == TT:
                    dma("sp", kvV[ki][:, :, :], Vd[c.name][p_, h, kt * TT:(kt + 1) * TT, :].rearrange("(b p) d -> p b d", p=128),
                        r=[("V", c.name, p_, h, kt, s_) for s_ in range(NSUB)], w=[("kvV", ki)], key=("ld_v", ki))
                else:
                    dma("sp", kvV[ki][0:kn, 0, :], Vd[c.name][p_, h, kt * TT:kt * TT + kn, :],
                        r=[("V", c.name, p_, h, kt, 0)], w=[("kvV", ki)], key=("ld_v", ki))
                slots[kt] = ki

            sbanks = {}

            def score(ux, qi=qi):
                kt, kbs, kp, diag = units[ux]
                if kt not in slots:
                    load_kt(kt)
                ki = slots[kt]
                b = 4 + rr("sbk", 3)
                sbanks[ux] = b
                for i_, kb in enumerate(kbs):
                    mm(ps[b][0:kp, i_ * N:(i_ + 1) * N], kvK[ki][:, kb * 128:kb * 128 + kp], qn[qi][:, 0:N], True, False,
                       r=[("kvK", ki), ("qn", qi)], w=[PS(b)])
                    mm(ps[b][0:kp, i_ * N:(i_ + 1) * N], krT_all[:, kt * TT + kb * 128:kt * TT + kb * 128 + kp], qrb[qi][:, 0:N],
                       False, True, r=["krT_all", ("qrb", qi)], w=[PS(b)])

            score(0)
            if nun > 1:
                score(1)
            if h + 1 < NH:
                q_next = emit_q(h + 1)
            for ux in range(nun):
                kt, kbs, kp, diag = units[ux]
                if ux + 2 < nun:
                    score(ux + 2)
                b = sbanks[ux]
                pi = rr("pt", NPT_)
                wd = len(kbs) * N
                act(pt[pi][0:kp, 0:wd], ps[b][0:kp, 0:wd], AF.Exp, r=[PS(b)], w=[("pt", pi)])
                if diag:
                    tt(pt[pi][0:kp, 0:wd], pt[pi][0:kp, 0:wd], mask[0:kp, :, :].rearrange("p a b -> p (a b)"), ALU.mult,
                       r=[("pt", pi), "mask"], w=[("pt", pi)])
                ki = slots[kt]
                for i_, kb in enumerate(kbs):
                    first = (ux == 0 and i_ == 0)
                    lastb = (ux == nun - 1 and i_ == len(kbs) - 1)
                    mm(ps[OB][:, 0:N], kvV[ki][0:kp, kb, :], pt[pi][0:kp, i_ * N:(i_ + 1) * N], first, lastb,
                       r=[("kvV", ki), ("pt", pi)], w=[PS(OB)])
                    mm(ps[RB][:, 0:N], ones_b[0:kp, :], pt[pi][0:kp, i_ * N:(i_ + 1) * N], first, lastb,
                       r=["ones_b", ("pt", pi)], w=[PS(RB)])
            recip(f3[:, 0:N], ps[RB][:, 0:N], r=[PS(RB)], w=["f3"])
            tt(big[:, 8 + h, 0:N], ps[OB][:, 0:N], f3[:, 0:N], ALU.mult, r=[PS(OB), "f3"], w=[("big", 8 + h)])
        out_proj(c, w_out_odd[p_], 24, ("woo", p_))

    def sample_cache_prep(p_):
        cst = X[:, 0, 0:NSUB * 512].rearrange("p (s r) -> p s r", s=NSUB)
        ckrst = X[:, 1, 0:NSUB * 64].rearrange("p (s r) -> p s r", s=NSUB)
        for kt in range(NPT):
            dma("sp", cst, ckv_in[p_, kt * TT:(kt + 1) * TT, :].rearrange("(s p) r -> p s r", p=128),
                r=(), w=[("X", 0)], key=("ld_x", 0))
            for s in range(NSUB):
                act(ztb[0][:, :], cst[:, s, :], AF.Copy, r=[("X", 0)], w=[("ztb", 0)])
                tb = tr_bank()
                for k in range(4):
                    tp(psb[tb][:, k * 128:(k + 1) * 128], ztb[0][:, k * 128:(k + 1) * 128], idb[:], r=[("ztb", 0), "idb"], w=[PS(tb)])
                act(ckvT[:, :, s * 128:(s + 1) * 128], psb[tb][:, 0:512].rearrange("p (k t) -> p k t", k=4), AF.Copy,
                    r=[PS(tb)], w=["ckvT"])
            kv_up("s", p_, ckvT, ["ckvT"], kt * TT, TT, 128, NSUB)
            dma("sp", ckrst, ckr_in[p_, kt * TT:(kt + 1) * TT, :].rearrange("(s p) r -> p s r", p=128),
                r=(), w=[("X", 1)], key=("ld_x", 1))
            tb = tr_bank()
            for s in range(NSUB):
                tp(ps[tb][0:64, s * 128:(s + 1) * 128], ckrst[:, s, :], idf[:], r=[("X", 1), "idf"], w=[PS(tb)])
            act(krb[:, :], ps[tb][0:64, 0:TT], AF.Copy, r=[PS(tb)], w=["krb"])
            dma("sp", KRd["s"][p_, :, kt * TT:(kt + 1) * TT], krb[:, :], r=["krb"], w=[("KR", "s", p_, kt)], key="st_kr")

    def process_tile(c, last):
        pp, nsub, N = c.pp, c.nsub, c.N
        for s in range(nsub):
            dma("sp", X[0:pp, s, :], c.x_in[c.row0 + s * 128:c.row0 + s * 128 + pp, :], r=(), w=[("X", s)], key=("ld_x", s))
        for l in range(DEPTH):
            rmsnorm_to_xnT(c, l)
            if l % 2 == 0:
                even_mixer(c, l // 2, last)
            else:
                odd_mixer(c, l // 2, last)
            rmsnorm_to_xnT_ffn(c, DEPTH + l)
            ffn(c, l)
        dma("sp", gbc[:], nrm[2 * DEPTH].partition_broadcast(128), r=(), w=["gbc"], key="ld_gbc")
        ms(ss[:], 0.0, w=["ss"])
        for s in range(nsub):
            i = rr("xn", 2)
            act(xn_tm[i][0:pp, :], X[0:pp, s, :], AF.Square, r=[("X", s), "ss"], w=[("xn_tm", i), "ss"],
                accum_out=ss[0:pp, s:s + 1])
        rstd_chain(rstd, ss[0:pp, 0:nsub], 1.0 / D, pp, nsub, r=["ss"], w=["rstd"])
        for s in range(nsub):
            stt(X[0:pp, s, :], X[0:pp, s, :], rstd[0:pp, s:s + 1], gbc[0:pp, :], ALU.mult, ALU.mult,
                r=[("X", s), "rstd", "gbc"], w=[("X", s)])
            dma("sp", c.y_out[c.row0 + s * 128:c.row0 + s * 128 + pp, :], X[0:pp, s, :], r=[("X", s)], w=(), key=("st_y", s))

    rmsnorm_to_xnT_ffn = rmsnorm_to_xnT

    def schedule():
        setup()
        build_diag()
        for p_ in range(NO):
            sample_cache_prep(p_)
        for t in range(NT):
            c = SeqCtx()
            c.name, c.N, c.pp, c.nsub = "p", TT, 128, NSUB
            c.pos0, c.row0, c.x_in, c.y_out = t * TT, t * TT, xp, yp
            process_tile(c, last=(t == NT - 1))
        c = SeqCtx()
        c.name, c.N, c.pp, c.nsub = "s", DEC, DEC, 1
        c.pos0, c.row0, c.x_in, c.y_out = PAST, 0, xs, ys
        process_tile(c, last=True)


    cnt0 = dict(cnt)
    real_P = P
    P = Prog()
    pst["dry"] = True
    schedule()
    npc = len(pst["order"])
    pst["wsc"] = [dint("wsc%d" % g_, [min(64, npc - 64 * g_), 128, 16 * 512], BF16) for g_ in range((npc + 63) // 64)]
    pst["dry"] = False
    cnt.update(cnt0)
    P = real_P
    schedule()

    P.emit(nc, es)
    es.close()
    return nc


def _consts():
    half = 32
    inv = (10000.0 ** (-np.arange(half, dtype=np.float32) / half)).astype(np.float32)
    invf = np.concatenate([inv, inv]).reshape(64, 1).astype(np.float32)
    k = np.arange(128)[:, None, None]
    r = np.arange(NSUB)[None, :, None]
    q = np.arange(TT)[None, None, :]
    m = (((r * 128 + k) // 64) <= (q // 64)).astype(np.float32).reshape(128, NSUB * TT)
    return invf, np.ascontiguousarray(m)


def run(cfg, inputs, trace=False):
    f = lambda a: np.ascontiguousarray(np.asarray(a, dtype=np.float32))
    NE, NO = cfg.NE, cfg.NO
    NOm = max(NO, 1)
    invf, maskc = _consts()
    cols = []
    for p_ in range(NE):
        cw = f(inputs["conv_a_w"])[p_]
        cols.append(cw.T.reshape(8, 128, 31).transpose(1, 0, 2).reshape(128, 248))
        for nm in ("conv_a_b", "ln_a_g", "ln_a_b"):
            cols.append(f(inputs[nm])[p_].reshape(8, 128).T)
        cols.append(np.zeros((128, 320 - 248 - 24), np.float32))
    for p_ in range(NOm):
        if NO:
            cw = f(inputs["conv_c_w"])[p_]
            cols.append(cw.T.reshape(8, 128, 3).transpose(1, 0, 2).reshape(128, 24))
        else:
            cols.append(np.zeros((128, 24), np.float32))
    pcol = np.ascontiguousarray(np.concatenate(cols, axis=1))
    nrm = np.ascontiguousarray(np.concatenate([f(inputs["norm_mix"]), f(inputs["norm_ffn"]),
                                               f(inputs["norm_final"])[None]], axis=0))
    lnv = np.ascontiguousarray(np.stack([f(inputs["ln_v_g"]), f(inputs["ln_v_b"])], axis=1))
    qkg = np.ascontiguousarray(np.stack([f(inputs["q_norm_g"]), f(inputs["kv_norm_g"])], axis=1))
    bsp = np.ascontiguousarray(f(inputs["b_spatial"]).reshape(NE, 1024))
    shared = dict(nrm=nrm, w_in_even=f(inputs["w_in_even"]), w_out_even=f(inputs["w_out_even"]),
                  w_in_odd=f(inputs["w_in_odd"]), w_uq=f(inputs["w_uq"]), w_ukv=f(inputs["w_ukv"]),
                  w_out_odd=f(inputs["w_out_odd"]), w_ffn_up=f(inputs["w_ffn_up"]), w_ffn_down=f(inputs["w_ffn_down"]),
                  pcol=pcol, lnv=lnv, qkg=qkg, bsp=bsp, wsp=f(inputs["w_spatial"]), c_invf=invf, c_mask=maskc)
    xpr, xsa = f(inputs["x_prompt"]), f(inputs["x_sample"])
    sca, scc = f(inputs["state_conv_a"]), f(inputs["state_conv_c"])
    ckv, ckr = f(inputs["cache_kv_latent"]), f(inputs["cache_k_rope"])
    in_maps = []
    for c in range(cfg.NCORES):
        m = dict(shared)
        m["xp"] = xpr[c]
        m["xs"] = xsa[c]
        m["sca"] = np.ascontiguousarray(sca[:, c])
        m["scc"] = np.ascontiguousarray(scc[:, c])
        m["ckv"] = np.ascontiguousarray(ckv[:, c])
        m["ckr"] = np.ascontiguousarray(ckr[:, c])
        in_maps.append(m)
    nc = build(cfg)
    res = run_bass_kernel_spmd(nc, in_maps, core_ids=list(range(cfg.NCORES)), **({"trace": True} if trace else {}))
    R = res.results
    st0 = lambda k: np.stack([R[c][k] for c in range(cfg.NCORES)], axis=0)
    st1 = lambda k: np.stack([R[c][k] for c in range(cfg.NCORES)], axis=1)
    outs = (st0("yp"), st0("ys"), st1("cap"), st1("cas"), st1("gvs"), st1("ccp"), st1("ccs"),
            st1("latp"), st1("krp"), st1("lats"), st1("krs"))
    return tuple(np.ascontiguousarray(o, dtype=np.float32) for o in outs), res


def kernel(**inputs):
    cfg = Cfg()
    outs, _ = run(cfg, inputs)
    return outs
```
